# Optimizing a Trainium2 kernel written in Bass

```python
import math
import jax, jax.numpy as jnp
from jax import lax
import numpy as np

D_MODEL = 1024
BATCH = 8
SEQ = 8192
DEPTH = 1
DEC_BATCH = 16
DEC_SEQ = 64
PAST_LEN = 4096

CHUNK = 64
Q_BLOCK = 128
NORM_EPS = 1e-6

SSM_D_INNER = 2 * D_MODEL
SSM_HEAD_DIM = 64
SSM_HEADS = SSM_D_INNER // SSM_HEAD_DIM
SSM_GROUPS = 4
SSM_HEADS_PER_GROUP = SSM_HEADS // SSM_GROUPS
SSM_D_STATE = 128
SSM_CONV = 4
SSM_CONV_DIM = SSM_D_INNER + 2 * SSM_GROUPS * SSM_D_STATE
DT_MIN = 1e-3
DT_MAX = 1e-1

ATT_HEADS = 8
ATT_HEAD_DIM = 64
ATT_QK_WIDTH = 2 * ATT_HEADS * ATT_HEAD_DIM
ATT_V_WIDTH = ATT_HEADS * 2 * ATT_HEAD_DIM
ROPE_THETA = 500000.0
ROT_DIM = ATT_HEAD_DIM // 4

D_FF = 2816
FFN_CONV = 3

IN_SIZES = (SSM_D_INNER, SSM_CONV_DIM, SSM_HEADS, ATT_QK_WIDTH, ATT_QK_WIDTH, ATT_V_WIDTH, D_MODEL, D_MODEL)
IN_WIDTH = sum(IN_SIZES)

kernel_name = "hybrid_ssd_diffattn_convffn_stream_step"


def _rmsnorm(x, w):
    xf = x.astype(jnp.float32)
    y = xf * lax.rsqrt(jnp.mean(xf * xf, axis=-1, keepdims=True) + NORM_EPS)
    return (y * w.astype(jnp.float32)).astype(x.dtype)


def _causal_dwconv(x_past, x_new, w, b):
    width = w.shape[0]
    xp = jnp.concatenate([x_past, x_new], axis=1)
    out = lax.conv_general_dilated(xp, w[:, None, :], window_strides=(1,), padding='VALID',
                                   dimension_numbers=('NWC', 'WIO', 'NWC'),
                                   feature_group_count=w.shape[1])
    return out + b, xp[:, -(width - 1):]


def _partial_rope(x, pos):
    half = ROT_DIM // 2
    inv_freq = jnp.power(ROPE_THETA, -jnp.arange(half, dtype=jnp.float32) * 2.0 / ROT_DIM)
    ang = pos.astype(jnp.float32)[:, None] * inv_freq[None, :]
    cos = jnp.cos(ang)[None, :, None, :]
    sin = jnp.sin(ang)[None, :, None, :]
    xf = x.astype(jnp.float32)
    x1 = xf[..., :half]
    x2 = xf[..., half:ROT_DIM]
    out = jnp.concatenate([x1 * cos - x2 * sin, x2 * cos + x1 * sin, xf[..., ROT_DIM:]], axis=-1)
    return out.astype(x.dtype)


def _diff_attn_block(q, k, v, q_pos, k_pos, lam):
    b, nq = q.shape[:2]
    s = jnp.einsum('bqhd,bkhd->bhqk', q, k, preferred_element_type=jnp.float32) * (ATT_HEAD_DIM ** -0.5)
    visible = (k_pos[None, :] // CHUNK) <= (q_pos[:, None] // CHUNK)
    s = jnp.where(visible[None, None], s, -jnp.inf)
    p = jax.nn.softmax(s, axis=-1).reshape(b, ATT_HEADS, 2, nq, -1)
    a = p[:, :, 0] - lam * p[:, :, 1]
    return jnp.einsum('bhqk,bkhe->bqhe', a.astype(v.dtype), v)


def _diff_attention(q, k, v, q_pos, k_pos, lam):
    b, t = q.shape[:2]
    if t <= Q_BLOCK:
        return _diff_attn_block(q, k, v, q_pos, k_pos, lam)
    nb = t // Q_BLOCK
    qb = jnp.swapaxes(q.reshape(b, nb, Q_BLOCK, 2 * ATT_HEADS, ATT_HEAD_DIM), 0, 1)
    pb = q_pos.reshape(nb, Q_BLOCK)
    ob = lax.map(lambda a: _diff_attn_block(a[0], k, v, a[1], k_pos, lam), (qb, pb))
    return jnp.swapaxes(ob, 0, 1).reshape(b, t, ATT_HEADS, 2 * ATT_HEAD_DIM)


def _ssd_scan(x, dt, a_neg, bmat, cmat, init_state, chunk_len):
    b, t = x.shape[:2]
    nc = t // chunk_len

    def to_chunks(z):
        return jnp.moveaxis(z.reshape((b, nc, chunk_len) + z.shape[2:]), 1, 0)

    causal = jnp.tril(jnp.ones((chunk_len, chunk_len), dtype=bool))

    def step(state, inp):
        xc, dtc, bc, cc = inp
        xc = xc.astype(jnp.float32)
        bc = bc.astype(jnp.float32)
        cc = cc.astype(jnp.float32)
        acum = jnp.cumsum(dtc * a_neg, axis=1)
        seg = acum[:, :, None] - acum[:, None, :]
        decay = jnp.exp(jnp.where(causal[None, :, :, None, None], seg, -jnp.inf))
        xdt = xc * dtc[..., None]
        cb = jnp.einsum('blgn,bsgn->blsg', cc, bc)
        y = jnp.einsum('blsg,blsge,bsgep->blgep', cb, decay, xdt)
        y = y + jnp.einsum('blgn,bgepn->blgep', cc, state) * jnp.exp(acum)[..., None]
        to_end = jnp.exp(acum[:, -1:] - acum)
        state = state * jnp.exp(acum[:, -1])[..., None, None] + jnp.einsum('blgn,blge,blgep->bgepn', bc, to_end, xdt)
        return state, y

    final, ys = lax.scan(step, init_state.astype(jnp.float32),
                         (to_chunks(x), to_chunks(dt), to_chunks(bmat), to_chunks(cmat)))
    y = jnp.moveaxis(ys, 0, 1).reshape(x.shape)
    return y, final


def _layer(x, k_past, v_past, ssm_past, conv_past, ffn_past, lambda_init, p):
    b, t, _ = x.shape
    past_len = k_past.shape[1]
    q_pos = past_len + jnp.arange(t, dtype=jnp.int32)
    k_pos = jnp.arange(past_len + t, dtype=jnp.int32)
    G, E, P, N = SSM_GROUPS, SSM_HEADS_PER_GROUP, SSM_HEAD_DIM, SSM_D_STATE

    xn = _rmsnorm(x, p['norm_mix_w'])
    proj = xn @ p['w_in']
    z, xbc, dt_raw, q, k, v, gate_ssm, gate_att = jnp.split(proj, np.cumsum(IN_SIZES)[:-1].tolist(), axis=-1)

    xbc_c, conv_new = _causal_dwconv(conv_past, xbc, p['ssm_conv_w'], p['ssm_conv_b'])
    xbc_c = jax.nn.silu(xbc_c)
    xs, bm, cm = jnp.split(xbc_c, [SSM_D_INNER, SSM_D_INNER + G * N], axis=-1)
    xs = xs.reshape(b, t, G, E, P)
    bm = bm.reshape(b, t, G, N)
    cm = cm.reshape(b, t, G, N)
    dt = jax.nn.softplus(dt_raw.astype(jnp.float32) + p['ssm_dt_bias'].astype(jnp.float32)).reshape(b, t, G, E)
    a_neg = -jnp.exp(p['ssm_a_log'].astype(jnp.float32)).reshape(G, E)
    chunk_len = min(CHUNK, t)
    y_ssm, ssm_new = _ssd_scan(xs, dt, a_neg, bm, cm, ssm_past.reshape(b, G, E, P, N), chunk_len)
    y_ssm = (y_ssm + p['ssm_d'].reshape(G, E)[..., None] * xs).astype(x.dtype).reshape(b, t, SSM_D_INNER)
    y_ssm = y_ssm * jax.nn.silu(z)
    y_ssm = _rmsnorm(y_ssm.reshape(b, t, G, -1), p['ssm_norm_w'].reshape(G, -1)).reshape(b, t, SSM_D_INNER)
    ssm_new = ssm_new.reshape(b, SSM_HEADS, P, N).astype(x.dtype)

    q = _partial_rope(_rmsnorm(q.reshape(b, t, 2 * ATT_HEADS, ATT_HEAD_DIM), p['q_norm_w']), q_pos)
    k = _partial_rope(_rmsnorm(k.reshape(b, t, 2 * ATT_HEADS, ATT_HEAD_DIM), p['k_norm_w']), q_pos)
    v = v.reshape(b, t, ATT_HEADS, 2 * ATT_HEAD_DIM)
    k_all = jnp.concatenate([k_past, k], axis=1)
    v_all = jnp.concatenate([v_past, v], axis=1)
    f32 = jnp.float32
    lam = (jnp.exp(jnp.sum(p['lambda_q1'].astype(f32) * p['lambda_k1'].astype(f32)))
           - jnp.exp(jnp.sum(p['lambda_q2'].astype(f32) * p['lambda_k2'].astype(f32))) + lambda_init)
    o = _diff_attention(q, k_all, v_all, q_pos, k_pos, lam)
    o = (_rmsnorm(o, p['subln_w']) * (1.0 - lambda_init)).reshape(b, t, ATT_V_WIDTH)

    mix = jax.nn.sigmoid(gate_ssm) * (y_ssm @ p['w_branch_ssm']) + jax.nn.sigmoid(gate_att) * (o @ p['w_branch_attn'])
    x = x + mix @ p['w_out']

    h = _rmsnorm(x, p['norm_ffn_w']) @ p['w_up']
    ha, hb = jnp.split(h, 2, axis=-1)
    hc, ffn_new = _causal_dwconv(ffn_past, ha, p['ffn_conv_w'], p['ffn_conv_b'])
    x = x + (jax.nn.silu(hc) * hb) @ p['w_down']
    return x, k, v, ssm_new, conv_new, ffn_new


def setup_inputs(seed: int = 0) -> dict:
    key = jax.random.key(seed)
    ks = list(jax.random.split(key, 32))

    def nrm(i, shape, scale):
        return scale * jax.random.normal(ks[i], shape, jnp.float32)

    L = DEPTH
    u_dt = jax.random.uniform(ks[9], (L, SSM_HEADS), jnp.float32)
    dt0 = jnp.exp(u_dt * (math.log(DT_MAX) - math.log(DT_MIN)) + math.log(DT_MIN))
    return {
        'x_prompt': nrm(0, (BATCH, SEQ, D_MODEL), 1.0),
        'x_sample': nrm(1, (DEC_BATCH, DEC_SEQ, D_MODEL), 1.0),
        'cache_k': nrm(2, (L, DEC_BATCH, PAST_LEN, 2 * ATT_HEADS, ATT_HEAD_DIM), 1.0),
        'cache_v': nrm(3, (L, DEC_BATCH, PAST_LEN, ATT_HEADS, 2 * ATT_HEAD_DIM), 1.0),
        'state_ssm': nrm(4, (L, DEC_BATCH, SSM_HEADS, SSM_HEAD_DIM, SSM_D_STATE), 0.5),
        'state_ssm_conv': nrm(5, (L, DEC_BATCH, SSM_CONV - 1, SSM_CONV_DIM), 1.0),
        'state_ffn_conv': nrm(6, (L, DEC_BATCH, FFN_CONV - 1, D_FF), 1.0),
        'norm_mix_w': 1.0 + nrm(7, (L, D_MODEL), 0.02),
        'w_in': nrm(8, (L, D_MODEL, IN_WIDTH), D_MODEL ** -0.5),
        'ssm_conv_w': nrm(10, (L, SSM_CONV, SSM_CONV_DIM), SSM_CONV ** -0.5),
        'ssm_conv_b': nrm(11, (L, SSM_CONV_DIM), 0.02),
        'ssm_dt_bias': dt0 + jnp.log(-jnp.expm1(-dt0)),
        'ssm_a_log': jnp.log(jax.random.uniform(ks[12], (L, SSM_HEADS), jnp.float32, 1.0, 16.0)),
        'ssm_d': 1.0 + nrm(13, (L, SSM_HEADS), 0.1),
        'ssm_norm_w': 1.0 + nrm(14, (L, SSM_D_INNER), 0.02),
        'q_norm_w': 1.0 + nrm(15, (L, ATT_HEAD_DIM), 0.02),
        'k_norm_w': 1.0 + nrm(16, (L, ATT_HEAD_DIM), 0.02),
        'lambda_q1': nrm(17, (L, ATT_HEAD_DIM), 0.1),
        'lambda_k1': nrm(18, (L, ATT_HEAD_DIM), 0.1),
        'lambda_q2': nrm(19, (L, ATT_HEAD_DIM), 0.1),
        'lambda_k2': nrm(20, (L, ATT_HEAD_DIM), 0.1),
        'subln_w': 1.0 + nrm(21, (L, 2 * ATT_HEAD_DIM), 0.02),
        'w_branch_ssm': nrm(22, (L, SSM_D_INNER, D_MODEL), SSM_D_INNER ** -0.5),
        'w_branch_attn': nrm(23, (L, ATT_V_WIDTH, D_MODEL), ATT_V_WIDTH ** -0.5),
        'w_out': nrm(24, (L, D_MODEL, D_MODEL), D_MODEL ** -0.5),
        'norm_ffn_w': 1.0 + nrm(25, (L, D_MODEL), 0.02),
        'w_up': nrm(26, (L, D_MODEL, 2 * D_FF), D_MODEL ** -0.5),
        'ffn_conv_w': nrm(27, (L, FFN_CONV, D_FF), FFN_CONV ** -0.5),
        'ffn_conv_b': nrm(28, (L, D_FF), 0.02),
        'w_down': nrm(29, (L, D_FF, D_MODEL), D_FF ** -0.5),
    }


def reference(x_prompt, x_sample, cache_k, cache_v, state_ssm, state_ssm_conv, state_ffn_conv,
              norm_mix_w, w_in, ssm_conv_w, ssm_conv_b, ssm_dt_bias, ssm_a_log, ssm_d, ssm_norm_w,
              q_norm_w, k_norm_w, lambda_q1, lambda_k1, lambda_q2, lambda_k2, subln_w,
              w_branch_ssm, w_branch_attn, w_out, norm_ffn_w, w_up, ffn_conv_w, ffn_conv_b, w_down):
    bp = x_prompt.shape[0]
    dt_ = x_prompt.dtype
    xp, xs = x_prompt, x_sample
    kp_l, vp_l, sp_l, cp_l, fp_l = [], [], [], [], []
    ks_l, vs_l, ss_l, cs_l, fs_l = [], [], [], [], []
    for layer in range(DEPTH):
        lambda_init = 0.8 - 0.6 * math.exp(-0.3 * layer)
        p = dict(norm_mix_w=norm_mix_w[layer], w_in=w_in[layer], ssm_conv_w=ssm_conv_w[layer],
                 ssm_conv_b=ssm_conv_b[layer], ssm_dt_bias=ssm_dt_bias[layer], ssm_a_log=ssm_a_log[layer],
                 ssm_d=ssm_d[layer], ssm_norm_w=ssm_norm_w[layer], q_norm_w=q_norm_w[layer],
                 k_norm_w=k_norm_w[layer], lambda_q1=lambda_q1[layer], lambda_k1=lambda_k1[layer],
                 lambda_q2=lambda_q2[layer], lambda_k2=lambda_k2[layer], subln_w=subln_w[layer],
                 w_branch_ssm=w_branch_ssm[layer], w_branch_attn=w_branch_attn[layer], w_out=w_out[layer],
                 norm_ffn_w=norm_ffn_w[layer], w_up=w_up[layer], ffn_conv_w=ffn_conv_w[layer],
                 ffn_conv_b=ffn_conv_b[layer], w_down=w_down[layer])
        xp, kp, vp, sp, cp, fp = _layer(
            xp,
            jnp.zeros((bp, 0, 2 * ATT_HEADS, ATT_HEAD_DIM), dt_),
            jnp.zeros((bp, 0, ATT_HEADS, 2 * ATT_HEAD_DIM), dt_),
            jnp.zeros((bp, SSM_HEADS, SSM_HEAD_DIM, SSM_D_STATE), dt_),
            jnp.zeros((bp, SSM_CONV - 1, SSM_CONV_DIM), dt_),
            jnp.zeros((bp, FFN_CONV - 1, D_FF), dt_),
            lambda_init, p)
        xs, ksn, vsn, ssn, csn, fsn = _layer(
            xs, cache_k[layer], cache_v[layer], state_ssm[layer], state_ssm_conv[layer],
            state_ffn_conv[layer], lambda_init, p)
        kp_l.append(kp); vp_l.append(vp); sp_l.append(sp); cp_l.append(cp); fp_l.append(fp)
        ks_l.append(ksn); vs_l.append(vsn); ss_l.append(ssn); cs_l.append(csn); fs_l.append(fsn)
    return (xp, xs,
            jnp.stack(kp_l), jnp.stack(vp_l), jnp.stack(sp_l), jnp.stack(cp_l), jnp.stack(fp_l),
            jnp.stack(ks_l), jnp.stack(vs_l), jnp.stack(ss_l), jnp.stack(cs_l), jnp.stack(fs_l))
```

```python
import math
from contextlib import ExitStack
import numpy as np
import concourse.bass as bass
import concourse.mybir as mybir
from concourse.bass_utils import run_bass_kernel_spmd

F32 = mybir.dt.float32
BF16 = mybir.dt.bfloat16
I32 = mybir.dt.int32
AF = mybir.ActivationFunctionType
ALU = mybir.AluOpType
AX = mybir.AxisListType

SAME_ENGINE_SYNC = True
EPOCH = 20000
NDMA_SEM = 24
EPS = 1e-6
D = 1024
DI = 2048
CD = 3072
DFF = 2816
INW = 10272
ROPE_THETA = 500000.0


class Sched:
    def __init__(self, nc, stack):
        self.nc = nc
        self.eng = {"pe": nc.tensor, "act": nc.scalar, "dve": nc.vector, "pool": nc.gpsimd, "sp": nc.sync}
        self.ops = []
        self.last_writer = {}
        self.readers = {}
        self.stack = stack
        self.sig_count = {s: 0 for s in self.eng}
        self.sems = {s: [] for s in self.eng}
        self.dma_sems = {s: [stack.enter_context(nc.semaphore(f"dq_{s}_{i}")) for i in range(NDMA_SEM)]
                         for s in ("sp", "act", "pool")}
        self.dma_count = {s: 0 for s in ("sp", "act", "pool")}
        self.waited = {}
        self.n_emitted = 0

    def _sem(self, stream, epoch):
        while len(self.sems[stream]) <= epoch:
            i = len(self.sems[stream])
            self.sems[stream].append(self.stack.enter_context(self.nc.semaphore(f"s_{stream}_{i}")))
        return self.sems[stream][epoch]

    def op(self, stream, method, *args, r=(), w=(), dma=False, **kwargs):
        oid = len(self.ops)
        deps = set()
        for x in list(r) + list(w):
            lw = self.last_writer.get(x)
            if lw is not None:
                deps.add(lw)
        for x in w:
            for rd in self.readers.get(x, ()):
                deps.add(rd)
        for x in r:
            if x[:2] in ("ps", "p3"):
                for rd in self.readers.get(x, ()):
                    if self.ops[rd]["stream"] != stream:
                        deps.add(rd)
        deps.discard(oid)
        for x in r:
            self.readers.setdefault(x, []).append(oid)
        for x in w:
            self.last_writer[x] = oid
            self.readers[x] = []
        self.ops.append(dict(stream=stream, method=method, args=args, kwargs=kwargs, deps=deps, dma=dma,
                             sig=False))
        return oid

    def pe(self, m, *a, **k):
        return self.op("pe", m, *a, **k)

    def act(self, m, *a, **k):
        return self.op("act", m, *a, **k)

    def dve(self, m, *a, **k):
        return self.op("dve", m, *a, **k)

    def pool(self, m, *a, **k):
        return self.op("pool", m, *a, **k)

    def dma(self, q, out, in_, r=(), w=(), **k):
        m = self.eng[q].dma_start
        return self.op(q, m, r=r, w=w, dma=True, out=out, in_=in_, **k)

    def _wait(self, stream, sem, val):
        key = (stream, id(sem))
        if self.waited.get(key, 0) >= val:
            return
        self.waited[key] = val
        if getattr(self, 'dbg', False):
            print('   WAIT', stream, getattr(sem, 'name', sem), val)
        self.eng[stream].wait_ge(sem, val)

    def flush(self):
        import os as _os
        _cut = int(_os.environ.get('KCUT', '0'))
        if _cut > 0 and self.n_emitted == 0:
            print('total ops in flush', len(self.ops))
            for _i in range(max(0, _cut - 6), min(len(self.ops), _cut + 2)):
                _o = self.ops[_i]
                print('OP', _i, _o['stream'], getattr(_o['method'], '__name__', _o['method']), [str(a)[:150] for a in _o['args']], {k: str(v)[:150] for k, v in _o['kwargs'].items()})
            self.ops = self.ops[:_cut]
        ops = self.ops
        for o in ops:
            for d in o["deps"]:
                od = ops[d]
                if od["dma"]:
                    continue
                if od["stream"] == o["stream"] and not o["dma"]:
                    if od["stream"] == "pe" or not SAME_ENGINE_SYNC:
                        continue
                od["sig"] = True
        lastc = {}
        for i, o in enumerate(ops):
            if not o["dma"]:
                lastc[o["stream"]] = i
        for s, i in lastc.items():
            ops[i]["sig"] = True
        for i, o in enumerate(ops):
            s = o["stream"]
            self.dbg = (_cut > 0 and i >= _cut - 12)
            if self.dbg:
                print('EMIT', i, s, getattr(o['method'], '__name__', ''), sorted(o['deps']), 'sig', o['sig'])
            for d in sorted(o["deps"]):
                od = ops[d]
                if od["dma"]:
                    self._wait(s, od["dsem"], od["dval"])
                else:
                    if od["stream"] == s and not o["dma"]:
                        if s == "pe" or not SAME_ENGINE_SYNC:
                            continue
                    self._wait(s, od["ssem"], od["sval"])
            if o["dma"]:
                k = self.dma_count[s]
                self.dma_count[s] = k + 1
                sem = self.dma_sems[s][k % NDMA_SEM]
                use = k // NDMA_SEM
                if use > 0:
                    self._wait(s, sem, 16 * use)
                inst = o["method"](*o["args"], **o["kwargs"])
                inst.then_inc(sem, 16)
                o["dsem"] = sem
                o["dval"] = 16 * (use + 1)
            else:
                inst = o["method"](*o["args"], **o["kwargs"])
                if o["sig"]:
                    k = self.sig_count[s]
                    self.sig_count[s] = k + 1
                    sem = self._sem(s, k // EPOCH)
                    inst.then_inc(sem, 1)
                    o["ssem"] = sem
                    o["sval"] = k % EPOCH + 1
            o["args"] = None
            o["kwargs"] = None
        self.n_emitted += len(ops)
        for s in list(self.eng.keys()):
            for s2, i in lastc.items():
                if s2 != s:
                    self._wait(s, ops[i]["ssem"], ops[i]["sval"])
            for q in self.dma_sems:
                n = self.dma_count[q]
                for j in range(min(n, NDMA_SEM)):
                    uses = (n - 1 - j) // NDMA_SEM + 1
                    self._wait(s, self.dma_sems[q][j], 16 * uses)
        self.ops = []
        self.last_writer = {}
        self.readers = {}


class Ring:
    def __init__(self, items):
        self.items = items
        self.i = 0

    def next(self):
        it = self.items[self.i % len(self.items)]
        self.i += 1
        return it


SEG = {}
_o = 0
for _n, _w in (("z", DI), ("xbc", CD), ("dt", 32), ("q", D), ("k", D), ("v", D), ("gs", D), ("ga", D)):
    SEG[_n] = (_o, _w)
    _o += _w
WIN_BLOCKS = []
for _n in ("xbc", "dt", "z", "q", "k", "v", "gs", "ga"):
    o0, w0 = SEG[_n]
    nb = (w0 + 511) // 512
    for b in range(nb):
        WIN_BLOCKS.append((_n, b, o0 + b * 512, min(512, w0 - b * 512)))
NWB = len(WIN_BLOCKS)


def build_program(Tp, NS, Ts, past):
    nc = bass.Bass("TRN2", target_bir_lowering=False)
    dram_in = lambda n, sh: nc.dram_tensor(n, sh, F32, kind="ExternalInput").ap()
    dram_out = lambda n, sh: nc.dram_tensor(n, sh, F32, kind="ExternalOutput").ap()
    xp = dram_in("xp", [Tp, D])
    xs = dram_in("xs", [NS, Ts, D])
    ck = dram_in("ck", [NS, past, D])
    cv = dram_in("cv", [NS, past, D])
    sssm = dram_in("sssm", [NS, 32, 64, 128])
    sconv = dram_in("sconv", [NS, 3, CD])
    sffn = dram_in("sffn", [NS, 2, DFF])
    norm_mix_w = dram_in("norm_mix_w", [1, D])
    w_in = dram_in("w_in", [D, INW])
    ssm_conv_w = dram_in("ssm_conv_w", [4, CD])
    ssm_conv_b = dram_in("ssm_conv_b", [1, CD])
    ssm_dt_bias = dram_in("ssm_dt_bias", [1, 32])
    ssm_a_log = dram_in("ssm_a_log", [1, 32])
    ssm_d = dram_in("ssm_d", [1, 32])
    ssm_norm_w = dram_in("ssm_norm_w", [1, DI])
    q_norm_w = dram_in("q_norm_w", [1, 64])
    k_norm_w = dram_in("k_norm_w", [1, 64])
    lam_in = dram_in("lam_in", [4, 64])
    subln_w = dram_in("subln_w", [128, 1])
    w_bs = dram_in("w_bs", [DI, D])
    w_ba = dram_in("w_ba", [D, D])
    w_out = dram_in("w_out", [D, D])
    norm_ffn_w = dram_in("norm_ffn_w", [1, D])
    w_up = dram_in("w_up", [D, 2 * DFF])
    ffn_conv_w = dram_in("ffn_conv_w", [3, DFF])
    ffn_conv_b = dram_in("ffn_conv_b", [1, DFF])
    w_down = dram_in("w_down", [DFF, D])

    y_p = dram_out("y_p", [Tp, D]); k_p = dram_out("k_p", [Tp, D]); v_p = dram_out("v_p", [Tp, D])
    ssm_p = dram_out("ssm_p", [32 * 64, 128]); conv_p = dram_out("conv_p", [3, CD]); ffn_p = dram_out("ffn_p", [2, DFF])
    y_s = dram_out("y_s", [NS, Ts, D]); k_s = dram_out("k_s", [NS, Ts, D]); v_s = dram_out("v_s", [NS, Ts, D])
    ssm_s = dram_out("ssm_s", [NS, 32 * 64, 128]); conv_s = dram_out("conv_s", [NS, 3, CD]); ffn_s = dram_out("ffn_s", [NS, 2, DFF])

    win_b = nc.dram_tensor("win_b", [NWB, 128, 8, 512], BF16).ap()
    wbs_b = nc.dram_tensor("wbs_b", [8, 128, 16, 128], BF16).ap()
    wba_b = nc.dram_tensor("wba_b", [128, 8, D], BF16).ap()
    wout_b = nc.dram_tensor("wout_b", [128, 8, D], BF16).ap()
    wup_b = nc.dram_tensor("wup_b", [11, 128, 8, 512], BF16).ap()
    wdn_b = nc.dram_tensor("wdn_b", [128, 22, D], BF16).ap()

    seqs = [dict(T=Tp, P=128, x=xp, past=0, y=y_p, k=k_p, v=v_p, ssm=ssm_p, conv=conv_p, ffn=ffn_p, idx=None)]
    for i in range(NS):
        seqs.append(dict(T=Ts, P=64, x=xs[i], past=past, y=y_s[i], k=k_s[i], v=v_s[i], ssm=ssm_s[i], conv=conv_s[i],
                         ffn=ffn_s[i], idx=i))
    for si, sq in enumerate(seqs):
        Tk = sq["past"] + sq["T"]
        sq["Tk"] = Tk
        sq["QT"] = nc.dram_tensor(f"QT{si}", [D, sq["T"]], BF16).ap()
        sq["KT"] = nc.dram_tensor(f"KT{si}", [D, Tk], BF16).ap()
        sq["V"] = nc.dram_tensor(f"V{si}", [Tk, D], BF16).ap()
        import os as _os
        _dbg = _os.environ.get("KDBG", "0") == "1"
        sq["OT"] = nc.dram_tensor(f"OT{si}", [D, sq["T"]], BF16, **({"kind": "ExternalOutput"} if _dbg else {})).ap()
        sq["MS"] = nc.dram_tensor(f"MS{si}", [D, sq["T"]], F32, **({"kind": "ExternalOutput"} if _dbg else {})).ap()
        sq["GA"] = nc.dram_tensor(f"GA{si}", [D, sq["T"]], F32, **({"kind": "ExternalOutput"} if _dbg else {})).ap()

    with ExitStack() as gst:
        S = Sched(nc, gst)
        gsb = lambda n, sh, dt=F32: gst.enter_context(nc.sbuf_tensor(n, sh, dt))

        ident_bf = gsb("ident_bf", [128, 128], BF16)
        ident_f = gsb("ident_f", [128, 128])
        ones_bf = gsb("ones_bf", [128, 128], BF16)
        ones_f = gsb("ones_f", [128, 128])
        U = gsb("U", [128, 128]); SU = gsb("SU", [128, 128]); BDm = gsb("BDm", [128, 128])
        IA = gsb("IA", [128, 128]); IB = gsb("IB", [128, 128])
        tmpc = gsb("tmpc", [128, 128])
        S.pool(nc.gpsimd.memset, tmpc[:], 0.0, w=["tmpc"])
        S.pool(nc.gpsimd.affine_select, tmpc[:], tmpc[:], [[-1, 128]], ALU.not_equal, 1.0, base=0, channel_multiplier=1,
               r=["tmpc"], w=["tmpc"])
        S.dve(nc.vector.tensor_copy, ident_bf[:], tmpc[:], r=["tmpc"], w=["ident_bf"])
        S.dve(nc.vector.tensor_copy, ident_f[:], tmpc[:], r=["tmpc"], w=["ident_f"])
        S.pool(nc.gpsimd.memset, ones_f[:], 1.0, w=["ones_f"])
        S.pool(nc.gpsimd.memset, ones_bf[:], 1.0, w=["ones_bf"])
        S.pool(nc.gpsimd.memset, U[:], 1.0, w=["U"])
        S.pool(nc.gpsimd.affine_select, U[:], U[:], [[1, 128]], ALU.is_ge, 0.0, base=0, channel_multiplier=-1, r=["U"], w=["U"])
        S.pool(nc.gpsimd.memset, U[0:64, 64:128], 0.0, r=["U"], w=["U"])
        S.pool(nc.gpsimd.memset, SU[:], 1.0, w=["SU"])
        S.pool(nc.gpsimd.affine_select, SU[:], SU[:], [[-1, 128]], ALU.is_gt, 0.0, base=0, channel_multiplier=1, r=["SU"], w=["SU"])
        S.pool(nc.gpsimd.memset, SU[64:128, 0:64], 0.0, r=["SU"], w=["SU"])
        SU_bf = gsb("SU_bf", [128, 128], BF16)
        S.dve(nc.vector.tensor_copy, SU_bf[:], SU[:], r=["SU"], w=["SU_bf"])
        S.pool(nc.gpsimd.memset, BDm[:], 0.0, w=["BDm"])
        S.pool(nc.gpsimd.memset, BDm[0:64, 0:64], 1.0, r=["BDm"], w=["BDm"])
        S.pool(nc.gpsimd.memset, BDm[64:128, 64:128], 1.0, r=["BDm"], w=["BDm"])
        S.pool(nc.gpsimd.memset, IA[:], 0.0, w=["IA"])
        S.pool(nc.gpsimd.memset, IA[0:64, :], 1.0, r=["IA"], w=["IA"])
        S.pool(nc.gpsimd.memset, IB[:], 0.0, w=["IB"])
        S.pool(nc.gpsimd.memset, IB[64:128, :], 1.0, r=["IB"], w=["IB"])

        def bc_load(name, src_row, n, stk=None):
            t = (stk or gst).enter_context(nc.sbuf_tensor(name, [128, n], F32))
            S.dma("sp", t[:], src_row.partition_broadcast(128), w=[name])
            return t

        qnw_bc = bc_load("qnw_bc", q_norm_w[0], 64)
        knw_bc = bc_load("knw_bc", k_norm_w[0], 64)
        dtb_bc = bc_load("dtb_bc", ssm_dt_bias[0], 32)
        alog_bc = bc_load("alog_bc", ssm_a_log[0], 32)
        dsk_bc = bc_load("dsk_bc", ssm_d[0], 32)
        A_bc = gsb("A_bc", [128, 32])
        S.act(nc.scalar.activation, A_bc[:], alog_bc[:], AF.Exp, r=["alog_bc"], w=["A_bc"])
        S.dve(nc.vector.tensor_scalar, A_bc[:], A_bc[:], -1.0, None, ALU.mult, r=["A_bc"], w=["A_bc"])
        scw = gsb("scw", [128, 24, 4]); scb = gsb("scb", [128, 24])
        fcw = gsb("fcw", [128, 22, 3]); fcb = gsb("fcb", [128, 22])
        v_scw = ssm_conv_w.rearrange("j (c p) -> p c j", p=128)
        for c in range(24):
            S.dma("sp", scw[:, c, :], v_scw[:, c, :], w=["scw"], allow_slow_non_contiguous=True)
        S.dma("sp", scb[:], ssm_conv_b[0].rearrange("(c p) -> p c", p=128), w=["scb"], allow_slow_non_contiguous=True)
        v_fcw = ffn_conv_w.rearrange("j (c p) -> p c j", p=128)
        for c in range(22):
            S.dma("sp", fcw[:, c, :], v_fcw[:, c, :], w=["fcw"], allow_slow_non_contiguous=True)
        S.dma("sp", fcb[:], ffn_conv_b[0].rearrange("(c p) -> p c", p=128), w=["fcb"], allow_slow_non_contiguous=True)
        lam_bc = gsb("lam_bc", [128, 4, 64])
        for i in range(4):
            S.dma("sp", lam_bc[:, i, :], lam_in[i].partition_broadcast(128), w=["lam_bc"])
        lamt = gsb("lamt", [128, 2, 64]); lams = gsb("lams", [128, 4])
        S.dve(nc.vector.tensor_tensor, lamt[:, 0, :], lam_bc[:, 0, :], lam_bc[:, 1, :], ALU.mult, r=["lam_bc"], w=["lamt"])
        S.dve(nc.vector.tensor_tensor, lamt[:, 1, :], lam_bc[:, 2, :], lam_bc[:, 3, :], ALU.mult, r=["lam_bc", "lamt"], w=["lamt"])
        S.dve(nc.vector.tensor_reduce, lams[:, 0:2], lamt[:], AX.X, ALU.add, r=["lamt"], w=["lams"])
        S.act(nc.scalar.activation, lams[:, 0:2], lams[:, 0:2], AF.Exp, r=["lams"], w=["lams"])
        lambda_init = 0.8 - 0.6 * math.exp(-0.3 * 0)
        S.dve(nc.vector.tensor_tensor, lams[:, 2:3], lams[:, 1:2], lams[:, 0:1], ALU.subtract, r=["lams"], w=["lams"])
        S.dve(nc.vector.tensor_scalar, lams[:, 2:3], lams[:, 2:3], -lambda_init, None, ALU.add, r=["lams"], w=["lams"])
        neglam = lams[:, 2:3]
        sublw = gsb("sublw", [128, 1])
        S.dma("sp", sublw[:], subln_w[:, :], w=["sublw"])
        S.dve(nc.vector.tensor_scalar, sublw[:], sublw[:], 1.0 - lambda_init, None, ALU.mult, r=["sublw"], w=["sublw"])

        v_win = w_in.rearrange("(k p) n -> p k n", p=128)
        for bi, (_n, _b, c0, wd) in enumerate(WIN_BLOCKS):
            S.dma("pool", win_b[bi, :, :, 0:wd], v_win[:, :, c0:c0 + wd], w=["win_b"])
        v_wbs = w_bs.rearrange("(k p) n -> p k n", p=128)
        for m in range(8):
            S.dma("pool", wbs_b[m], v_wbs[:, :, m * 128:(m + 1) * 128], w=["wbs_b"])
        for h2 in range(2):
            S.dma("pool", wba_b[:, :, h2 * 512:(h2 + 1) * 512], w_ba.rearrange("(k p) n -> p k n", p=128)[:, :, h2 * 512:(h2 + 1) * 512], w=["wba_b"])
            S.dma("pool", wout_b[:, :, h2 * 512:(h2 + 1) * 512], w_out.rearrange("(k p) n -> p k n", p=128)[:, :, h2 * 512:(h2 + 1) * 512], w=["wout_b"])
        v_wup = w_up.rearrange("(k p) n -> p k n", p=128)
        for j in range(11):
            S.dma("pool", wup_b[j, :, :, 0:256], v_wup[:, :, 256 * j:256 * j + 256], w=["wup_b"])
            S.dma("pool", wup_b[j, :, :, 256:512], v_wup[:, :, DFF + 256 * j:DFF + 256 * j + 256], w=["wup_b"])
        v_wdn = w_down.rearrange("(k p) n -> p k n", p=128)
        for h2 in range(2):
            for kk in range(2):
                S.dma("pool", wdn_b[:, kk * 11:(kk + 1) * 11, h2 * 512:(h2 + 1) * 512],
                      v_wdn[:, kk * 11:(kk + 1) * 11, h2 * 512:(h2 + 1) * 512], w=["wdn_b"])
        for sq in seqs:
            if sq["past"] > 0:
                i = sq["idx"]
                npc = sq["past"] // 512
                for j in range(npc):
                    S.dma("pool", sq["V"][j * 512:(j + 1) * 512, :], cv[i, j * 512:(j + 1) * 512, :], w=["Vscr%d" % i])

        invf = gsb("invf", [128, 8])
        inv_np = np.power(np.float32(ROPE_THETA), -(np.arange(8, dtype=np.float32) * np.float32(2.0) / np.float32(16))).astype(np.float32)
        for i in range(8):
            S.pool(nc.gpsimd.memset, invf[:, i:i + 1], float(inv_np[i]) / (2.0 * math.pi), r=["invf"], w=["invf"])

        def rope_tables(st, name, P, ntile, pos0, tmp_i, tmp_f, tmp_u):
            cs = st.enter_context(nc.sbuf_tensor(name + "_cs", [128, 2, ntile, 8], F32))
            pi_ = st.enter_context(nc.sbuf_tensor(name + "_pi", [128, ntile], I32))
            pf = st.enter_context(nc.sbuf_tensor(name + "_pf", [128, ntile], F32))
            u = tmp_u[:, 0:2 * ntile * 8].rearrange("p (a b c) -> p a b c", a=2, b=ntile)
            ui = tmp_i[:, 0:2 * ntile * 8].rearrange("p (a b c) -> p a b c", a=2, b=ntile)
            uf = tmp_f[:, 0:2 * ntile * 8].rearrange("p (a b c) -> p a b c", a=2, b=ntile)
            rn = name + "_rope"
            S.pool(nc.gpsimd.iota, pi_[:], [[P, ntile]], base=pos0, channel_multiplier=1, w=[rn])
            S.dve(nc.vector.tensor_copy, pf[:], pi_[:], r=[rn], w=[rn])
            S.dve(nc.vector.tensor_tensor, u[:, 1], pf[:, :, None].broadcast_to([128, ntile, 8]),
                  invf[:, None, :].broadcast_to([128, ntile, 8]), ALU.mult, r=[rn, "invf", "sgs"], w=[rn, "sgs"])
            S.dve(nc.vector.tensor_scalar, u[:, 0], u[:, 1], 0.25, None, ALU.add, r=[rn, "sgs"], w=[rn, "sgs"])
            S.dve(nc.vector.tensor_copy, ui, u, r=[rn, "sgs"], w=[rn, "sttmp"])
            S.dve(nc.vector.tensor_copy, uf, ui, r=[rn, "sttmp", "sgs"], w=[rn, "sttmp", "sgs"])
            S.dve(nc.vector.tensor_tensor, u, u, uf, ALU.subtract, r=[rn, "sttmp", "sgs"], w=[rn, "sgs"])
            S.dve(nc.vector.tensor_scalar, u, u, 0.49999, -0.49999, ALU.min, ALU.max, r=[rn, "sgs"], w=[rn, "sgs"])
            S.act(nc.scalar.activation, cs[:], u, AF.Sin, scale=2.0 * math.pi, r=[rn, "sgs"], w=[rn])
            return cs, rn

        import os as _os
        _stop = int(_os.environ.get('KSTOP', '9'))
        if _stop == 0:
            S.flush()
            return nc
        with ExitStack() as st:
            sb = lambda n, sh, dt=F32: st.enter_context(nc.sbuf_tensor(n, sh, dt))
            psb = lambda n, sh, dt=F32: st.enter_context(nc.psum_tensor(n, sh, dt))
            nmw_bc = bc_load("nmw_bc", norm_mix_w[0], D, st)
            snw_bc = bc_load("snw_bc", ssm_norm_w[0], DI, st)
            GEN = Ring([(psb(f"psg{i}", [128, 512]), f"psg{i}") for i in range(2)])
            pstr = psb("pstr", [128, 1024], BF16)
            pscb = psb("pscb", [128, 512])
            psdt = psb("psdt", [128, 512])
            psseg = psb("psseg", [128, 1024])
            psyi = psb("psyi", [128, 512])
            WR = Ring([(sb(f"wr{i}", [128, 8, 512], BF16), f"wr{i}") for i in range(2)])
            WBS = Ring([(sb(f"wbs{i}", [128, 16, 128], BF16), f"wbs{i}") for i in range(2)])
            XT = Ring([(sb(f"xt{i}", [128, D]), f"xt{i}") for i in range(2)])
            xnb = sb("xnb", [128, D], BF16)
            XNT = Ring([(sb(f"xnT{i}", [128, 8, 256], BF16), f"xnT{i}") for i in range(2)])
            PRE = Ring([(sb(f"pre{i}", [128, 3 + 256]), f"pre{i}") for i in range(4)])
            CTMP = Ring([(sb(f"cvt{i}", [128, 256]), f"cvt{i}") for i in range(2)])
            halo = sb("halo", [128, 24, 3])
            xbcT = sb("xbcT", [128, 24, 256], BF16)
            sz = sb("sz", [128, 2, DI])
            sgs = sb("sgs", [128, 8, 256])
            small = sb("small", [128, 16, 32])
            ssq = sb("ssq", [128, 16])
            junk = sb("junk", [128, D], BF16)
            XS = Ring([(sb(f"xs{i}", [128, 512], BF16), f"xs{i}") for i in range(3)])
            XDT = Ring([(sb(f"xdt{i}", [128, 512], BF16), f"xdt{i}") for i in range(3)])
            XDE = Ring([(sb(f"xde{i}", [128, 512], BF16), f"xde{i}") for i in range(3)])
            BTOK = Ring([(sb(f"btok{i}", [128, 4, 128], BF16), f"btok{i}") for i in range(2)])
            RG = Ring([((sb(f"rgh{i}", [128, 8, 128], BF16), sb(f"rgl{i}", [128, 8, 128], BF16)), f"rg{i}") for i in range(1)])
            small2 = sb("small2", [128, 4, 32]); smallb = sb("smallb", [128, 2, 32], BF16)
            DEC = Ring([(sb(f"dec{i}", [128, 8, 128]), f"dec{i}") for i in range(1)])
            MT = Ring([(sb(f"mt{i}", [128, 8, 128], BF16), f"mt{i}") for i in range(3)])
            CBM = Ring([(sb(f"cbm{i}", [128, 4, 128]), f"cbm{i}") for i in range(2)])
            CT = Ring([((sb(f"cta{i}", [128, 4, 128], BF16), sb(f"ctb{i}", [128, 4, 128], BF16)), (f"cta{i}", f"ctb{i}")) for i in range(2)])
            Sst = sb("Sst", [128, DI])
            s0bf = sb("s0bf", [128, DI], BF16); s1bf = sb("s1bf", [128, DI], BF16)
            YB = Ring([(sb(f"yb{i}", [128, 512]), f"yb{i}") for i in range(2)])
            dsk = sb("dsk", [128, 512])
            stmp = sb("stmp", [128, 512])
            YNB = Ring([(sb(f"ynb{i}", [128, 512], BF16), f"ynb{i}") for i in range(2)])
            ynT = sb("ynT", [128, 16, 256], BF16)
            STG = Ring([(sb(f"stg{i}", [128, 512]), f"stg{i}") for i in range(2)])
            STGB = Ring([(sb(f"stgb{i}", [128, 512], BF16), f"stgb{i}") for i in range(4)])
            QF = Ring([(sb(f"qf{i}", [128, 8, 64]), f"qf{i}") for i in range(2)])
            QSQ = Ring([(sb(f"qsq{i}", [128, 8, 64]), f"qsq{i}") for i in range(2)])
            SSK = Ring([0, 1, 2])
            ssqk = sb("ssqk", [128, 3, 8])
            rt = sb("rt", [128, 4, 8, 8])
            sttmp = sb("sttmp", [128, 16, 128])
            _stf = sttmp[:].rearrange("p a b -> p (a b)")
            rtmp_i = _stf[:, 0:1024].bitcast(I32); rtmp_f = _stf[:, 1024:2048]
            rtmp_u = sgs[:].rearrange("p a b -> p (a b)")
            SSTN = ["Sst0", "Sst1", "Sst2", "Sst3"]
            HALON = ["halo%d" % c for c in range(24)]
            XBCN = ["xbcT%d" % c for c in range(24)]
            for (_ca, _cb), (_can, _cbn) in CT.items:
                S.pool(nc.gpsimd.memset, _ca[:], 0.0, w=[_can])
                S.pool(nc.gpsimd.memset, _cb[:], 0.0, w=[_cbn])

            def rstd_from_ss(ap, n, res):
                S.act(nc.scalar.activation, ap, ap, AF.Ln, scale=1.0 / n, bias=EPS, r=res, w=res)
                S.act(nc.scalar.activation, ap, ap, AF.Exp, scale=-0.5, r=res, w=res)

            for si, sq in enumerate(seqs):
                T, P = sq["T"], sq["P"]
                NT = 2 if P == 128 else 1
                GW = P * NT
                NG = T // GW
                assert NG * GW == T
                nch = P // 64
                ntile = T // P
                pastn = sq["past"]
                cs, rn = rope_tables(st, f"rp{si}", P, ntile, pastn, rtmp_i, rtmp_f, rtmp_u)
                if pastn == 0:
                    S.pool(nc.gpsimd.memset, halo[:], 0.0, w=HALON)
                    S.pool(nc.gpsimd.memset, Sst[:], 0.0, w=SSTN)
                    S.pool(nc.gpsimd.memset, s0bf[:], 0.0, w=["s0bf0", "s0bf1", "s0bf2", "s0bf3"])
                else:
                    i = sq["idx"]
                    vsc = sconv[i].rearrange("j (c p) -> p c j", p=128)
                    for c in range(24):
                        S.dma("sp", halo[:, c, :], vsc[:, c, :], w=["halo%d" % c], allow_slow_non_contiguous=True)
                    S.dma("sp", sttmp[:], sssm[i].rearrange("(c h) p n -> (h p) c n", h=2), w=["sttmp"])
                    for c in range(16):
                        ps, pn = GEN.next()
                        S.pe(nc.tensor.transpose, ps[:, 0:128], sttmp[:, c, :], ident_f[:], r=["sttmp", "ident_f"], w=[pn])
                        S.act(nc.scalar.copy, Sst[:, c * 128:(c + 1) * 128], ps[:, 0:128], r=[pn], w=["Sst%d" % (c // 4)])
                    S.dve(nc.vector.tensor_copy, s0bf[:], Sst[:], r=SSTN, w=["s0bf0", "s0bf1", "s0bf2", "s0bf3"])
                    for j in range(pastn // 128):
                        xt, xn_ = XT.next()
                        S.dma("sp", xt[:], ck[i, j * 128:(j + 1) * 128, :], w=[xn_])
                        S.dve(nc.vector.tensor_copy, xnb[:], xt[:], r=[xn_], w=["xnb"])
                        for k in range(8):
                            S.pe(nc.tensor.transpose, pstr[:, k * 128:(k + 1) * 128], xnb[:, k * 128:(k + 1) * 128], ident_bf[:],
                                 r=["xnb", "ident_bf"], w=["pstr"])
                        for hh in range(2):
                            stb, stbn = STGB.next()
                            S.act(nc.scalar.copy, stb[:], pstr[:, hh * 512:(hh + 1) * 512], r=["pstr"], w=[stbn])
                            S.dma("sp", sq["KT"].rearrange("(c p) t -> p c t", p=128)[:, hh * 4:(hh + 1) * 4, j * 128:(j + 1) * 128],
                                  stb[:].rearrange("p (c t) -> p c t", c=4), r=[stbn], w=["KTscr%d" % si])

                def xnorm(g):
                    t0 = g * GW
                    xnT, xnTn = XNT.next()
                    for ti in range(NT):
                        r0 = t0 + ti * P
                        xt, xn_ = XT.next()
                        S.dma("sp", xt[:P], sq["x"][r0:r0 + P, :], w=[xn_])
                        S.dve(nc.vector.memset, ssq[:P, 0:1], 0.0, w=["ssq"])
                        S.act(nc.scalar.activation, junk[:P], xt[:P], AF.Square, accum_out=ssq[:P, 0:1], r=[xn_, "ssq"], w=["junk", "ssq"])
                        rstd_from_ss(ssq[:P, 0:1], D, ["ssq"])
                        S.dve(nc.vector.scalar_tensor_tensor, xnb[:P], xt[:P], ssq[:P, 0:1], nmw_bc[:P], ALU.mult, ALU.mult,
                              r=[xn_, "ssq", "nmw_bc"], w=["xnb"])
                        for k in range(8):
                            S.pe(nc.tensor.transpose, pstr[:, k * P:(k + 1) * P], xnb[:P, k * 128:(k + 1) * 128], ident_bf[:P, :P],
                                 r=["xnb", "ident_bf"], w=["pstr"])
                        S.act(nc.scalar.copy, xnT[:, :, ti * P:(ti + 1) * P], pstr[:, 0:8 * P].rearrange("p (k t) -> p k t", k=8),
                              r=["pstr"], w=[xnTn])
                    return xnT, xnTn

                xn_next = xnorm(0)
                for g in range(NG):
                    t0 = g * GW
                    xnT, xnTn = xn_next
                    def qk_s0(it):
                        qf_, qfn_, qs_, qsn_, ss_ = it["qf"], it["qfn"], it["qsq"], it["qsn"], it["ss"]
                        ssa = ssqk[:P, ss_, :]
                        ssn = "ssqk%d" % ss_
                        S.act(nc.scalar.copy, qf_[:P], it["ps"][:P, :].rearrange("p (h d) -> p h d", h=8), r=[it["pn"]], w=[qfn_])
                        S.pool(nc.gpsimd.tensor_tensor, qs_[:P], qf_[:P], qf_[:P], ALU.mult, r=[qfn_], w=[qsn_])
                        S.dve(nc.vector.tensor_reduce, ssa, qs_[:P], AX.X, ALU.add, r=[qsn_], w=[ssn])
                        rstd_from_ss(ssa, 64, [ssn])

                    def qk_s1(it):
                        seg_, b_, r0_ = it["seg"], it["b"], it["r0"]
                        qf_, qfn_, qs_, qsn_, ss_ = it["qf"], it["qfn"], it["qsq"], it["qsn"], it["ss"]
                        ssn = "ssqk%d" % ss_
                        wbc = qnw_bc if seg_ == "q" else knw_bc
                        S.dve(nc.vector.tensor_tensor, qs_[:P], qf_[:P], ssqk[:P, ss_, :, None].broadcast_to([P, 8, 64]), ALU.mult,
                              r=[qfn_, ssn, qsn_], w=[qsn_])
                        S.dve(nc.vector.tensor_tensor, qf_[:P], qs_[:P], wbc[:P, None, :].broadcast_to([P, 8, 64]), ALU.mult,
                              r=[qsn_, qfn_, "qnw_bc", "knw_bc"], w=[qfn_])
                        tidx = (r0_ // P)
                        cosb = cs[:P, 0, tidx, None, :].broadcast_to([P, 8, 8])
                        sinb = cs[:P, 1, tidx, None, :].broadcast_to([P, 8, 8])
                        S.dve(nc.vector.tensor_tensor, rt[:P, 0], qf_[:P, :, 0:8], cosb, ALU.mult, r=[qfn_, rn], w=["rt0"])
                        S.dve(nc.vector.tensor_tensor, rt[:P, 1], qf_[:P, :, 8:16], sinb, ALU.mult, r=[qfn_, rn], w=["rt1"])
                        S.dve(nc.vector.tensor_tensor, rt[:P, 2], qf_[:P, :, 8:16], cosb, ALU.mult, r=[qfn_, rn], w=["rt2"])
                        S.dve(nc.vector.tensor_tensor, rt[:P, 3], qf_[:P, :, 0:8], sinb, ALU.mult, r=[qfn_, rn], w=["rt3"])
                        S.dve(nc.vector.tensor_tensor, qf_[:P, :, 0:8], rt[:P, 0], rt[:P, 1], ALU.subtract, r=["rt0", "rt1", qfn_], w=[qfn_])
                        S.dve(nc.vector.tensor_tensor, qf_[:P, :, 8:16], rt[:P, 2], rt[:P, 3], ALU.add, r=["rt2", "rt3", qfn_], w=[qfn_])
                        stb, stbn = STGB.next()
                        if seg_ == "k":
                            stg, stgn = STG.next()
                            S.pool(nc.gpsimd.tensor_copy, stg[:P], qf_[:P].rearrange("p h d -> p (h d)"), r=[qfn_], w=[stgn])
                            S.dma("sp", sq["k"][r0_:r0_ + P, b_ * 512:(b_ + 1) * 512], stg[:P], r=[stgn])
                            S.dve(nc.vector.tensor_copy, stb[:P], qf_[:P].rearrange("p h d -> p (h d)"), r=[qfn_], w=[stbn])
                        else:
                            S.dve(nc.vector.tensor_scalar, stb[:P], qf_[:P].rearrange("p h d -> p (h d)"), 0.125, None, ALU.mult,
                                  r=[qfn_], w=[stbn])
                        it["stb"] = (stb, stbn)

                    def qk_s2(it):
                        seg_, b_, r0_ = it["seg"], it["b"], it["r0"]
                        stb, stbn = it["stb"]
                        for j in range(4):
                            S.pe(nc.tensor.transpose, pstr[:, j * P:(j + 1) * P], stb[:P, j * 128:(j + 1) * 128], ident_bf[:P, :P],
                                 r=[stbn, "ident_bf"], w=["pstr"])
                        stb2, stbn2 = STGB.next()
                        S.act(nc.scalar.copy, stb2[:, 0:4 * P], pstr[:, 0:4 * P], r=["pstr"], w=[stbn2])
                        dst = sq["QT"] if seg_ == "q" else sq["KT"]
                        coff = r0_ if seg_ == "q" else pastn + r0_
                        S.dma("sp", dst.rearrange("(c p) t -> p c t", p=128)[:, b_ * 4:(b_ + 1) * 4, coff:coff + P],
                              stb2[:, 0:4 * P].rearrange("p (c t) -> p c t", c=4), r=[stbn2],
                              w=[("QTscr%d" if seg_ == "q" else "KTscr%d") % si])

                    qk_items = []

                    def qk_advance(item):
                        if item is not None:
                            qk_items.append(item)
                        n_ = len(qk_items)
                        if item is not None:
                            if n_ >= 3:
                                qk_s2(qk_items[n_ - 3])
                            if n_ >= 2:
                                qk_s1(qk_items[n_ - 2])
                            qk_s0(item)
                        else:
                            if n_ >= 2:
                                qk_s2(qk_items[n_ - 2])
                            if n_ >= 1:
                                qk_s1(qk_items[n_ - 1])
                                qk_s2(qk_items[n_ - 1])

                    for bi, (seg, b, c0, wd) in enumerate(WIN_BLOCKS):
                        wt, wn = WR.next()
                        S.dma("sp", wt[:], win_b[bi], r=["win_b"], w=[wn])
                        if seg == "xbc":
                            for cp in range(2):
                                cs_ = [b * 4 + cp * 2, b * 4 + cp * 2 + 1]
                                pss_ = []
                                for c in cs_:
                                    cc = c - b * 4
                                    ps, pn = GEN.next()
                                    for k in range(8):
                                        S.pe(nc.tensor.matmul, ps[:, 0:GW], wt[:, k, cc * 128:(cc + 1) * 128], xnT[:, k, 0:GW],
                                             start=(k == 0), stop=(k == 7), r=[wn, xnTn], w=[pn])
                                    pss_.append((ps, pn))
                                pres = []
                                for c, (ps, pn) in zip(cs_, pss_):
                                    pre, pren = PRE.next()
                                    ct_, ctn_ = CTMP.next()
                                    S.pool(nc.gpsimd.tensor_copy, pre[:, 0:3], halo[:, c, :], r=["halo%d" % c], w=[pren])
                                    S.act(nc.scalar.copy, pre[:, 3:3 + GW], ps[:, 0:GW], r=[pn], w=[pren])
                                    S.pool(nc.gpsimd.tensor_copy, halo[:, c, :], pre[:, GW:GW + 3], r=[pren], w=["halo%d" % c])
                                    pres.append((pre, pren, ct_, ctn_))
                                for j in range(4):
                                    for c, (pre, pren, ct_, ctn_) in zip(cs_, pres):
                                        if j == 0:
                                            S.dve(nc.vector.tensor_scalar, ct_[:, 0:GW], pre[:, 0:GW], scw[:, c, 0:1], scb[:, c:c + 1], ALU.mult, ALU.add,
                                                  r=[pren, "scw", "scb"], w=[ctn_])
                                        else:
                                            S.dve(nc.vector.scalar_tensor_tensor, ct_[:, 0:GW], pre[:, j:j + GW], scw[:, c, j:j + 1], ct_[:, 0:GW],
                                                  ALU.mult, ALU.add, r=[pren, "scw", ctn_], w=[ctn_])
                                for c, (pre, pren, ct_, ctn_) in zip(cs_, pres):
                                    S.act(nc.scalar.activation, xbcT[:, c, 0:GW], ct_[:, 0:GW], AF.Silu, r=[ctn_], w=["xbcT%d" % c])
                        elif seg == "dt":
                            for ti in range(NT):
                                for k in range(8):
                                    S.pe(nc.tensor.matmul, psdt[:P, ti * 32:(ti + 1) * 32], xnT[:, k, ti * P:(ti + 1) * P], wt[:, k, 0:32],
                                         start=(k == 0), stop=(k == 7), r=[xnTn, wn], w=["psdt"])
                            for ti in range(NT):
                                S.dve(nc.vector.tensor_tensor, small[:P, ti, :], psdt[:P, ti * 32:(ti + 1) * 32], dtb_bc[:P], ALU.add,
                                      r=["psdt", "dtb_bc"], w=["sm%d" % ti])
                                S.act(nc.scalar.activation, small[:P, ti, :], small[:P, ti, :], AF.Exp, r=["sm%d" % ti], w=["sm%d" % ti])
                                S.act(nc.scalar.activation, small[:P, ti, :], small[:P, ti, :], AF.Ln, bias=1.0, r=["sm%d" % ti], w=["sm%d" % ti])
                        elif seg == "z":
                            for ti in range(NT):
                                ps, pn = GEN.next()
                                for k in range(8):
                                    S.pe(nc.tensor.matmul, ps[:P, :], xnT[:, k, ti * P:(ti + 1) * P], wt[:, k, :],
                                         start=(k == 0), stop=(k == 7), r=[xnTn, wn], w=[pn])
                                S.act(nc.scalar.activation, sz[:P, ti, b * 512:(b + 1) * 512], ps[:P, :], AF.Silu, r=[pn], w=["sz"])
                        elif seg in ("q", "k", "v"):
                            for ti in range(NT):
                                r0 = t0 + ti * P
                                ps, pn = GEN.next()
                                for k in range(8):
                                    S.pe(nc.tensor.matmul, ps[:P, :], xnT[:, k, ti * P:(ti + 1) * P], wt[:, k, :],
                                         start=(k == 0), stop=(k == 7), r=[xnTn, wn], w=[pn])
                                if seg == "v":
                                    if b == 0 and ti == 0:
                                        qk_advance(None)
                                    stg, stgn = STG.next()
                                    S.act(nc.scalar.copy, stg[:P], ps[:P, :], r=[pn], w=[stgn])
                                    S.dma("sp", sq["v"][r0:r0 + P, b * 512:(b + 1) * 512], stg[:P], r=[stgn])
                                    stb, stbn = STGB.next()
                                    S.dve(nc.vector.tensor_copy, stb[:P], stg[:P], r=[stgn], w=[stbn])
                                    S.dma("sp", sq["V"][pastn + r0:pastn + r0 + P, b * 512:(b + 1) * 512], stb[:P], r=[stbn], w=["Vscr%s" % si])
                                    continue
                                qfb, qfn = QF.next()
                                qsb, qsn = QSQ.next()
                                sslot = SSK.next()
                                item = dict(seg=seg, b=b, ti=ti, r0=r0, ps=ps, pn=pn, qf=qfb, qfn=qfn, qsq=qsb, qsn=qsn, ss=sslot)
                                qk_advance(item)
                        else:
                            for cc in range(4):
                                m = b * 4 + cc
                                ps, pn = GEN.next()
                                for k in range(8):
                                    S.pe(nc.tensor.matmul, ps[:, 0:GW], wt[:, k, cc * 128:(cc + 1) * 128], xnT[:, k, 0:GW],
                                         start=(k == 0), stop=(k == 7), r=[wn, xnTn], w=[pn])
                                if seg == "gs":
                                    S.act(nc.scalar.activation, sgs[:, m, 0:GW], ps[:, 0:GW], AF.Sigmoid, r=[pn], w=["sgs"])
                                else:
                                    stg, stgn = STG.next()
                                    S.act(nc.scalar.activation, stg[:, 0:GW], ps[:, 0:GW], AF.Sigmoid, r=[pn], w=[stgn])
                                    S.dma("sp", sq["GA"][m * 128:(m + 1) * 128, t0:t0 + GW], stg[:, 0:GW], r=[stgn], w=["GAscr%d" % si])

                    if g + 1 < NG:
                        xn_next = xnorm(g + 1)
                    wbq = []
                    for m in range(2):
                        wb, wbn = WBS.next()
                        S.dma("sp", wb[:], wbs_b[m], r=["wbs_b"], w=[wbn])
                        wbq.append((wb, wbn))
                    def prologue(ti):
                        tc0 = ti * P
                        dt = small[:P, ti, :]
                        dtA = small[:P, 2 + ti, :]
                        expA = small[:P, 4 + ti, :]
                        acs = small[:P, 6 + ti, :]
                        toend = small[:P, 8 + ti, :]
                        dte = small[:P, 10 + ti, :]
                        dA = small[:, 12 + ti, :]
                        dB = small[:, 14 + ti, :]
                        n = lambda k: "sm%d" % k
                        S.dve(nc.vector.tensor_tensor, dtA, dt, A_bc[:P], ALU.mult, r=[n(ti), "A_bc"], w=[n(2 + ti)])
                        S.dve(nc.vector.tensor_copy, smallb[:P, ti, :], dtA, r=[n(2 + ti)], w=["smb%d" % ti])
                        S.dve(nc.vector.tensor_copy, small2[:P, ti, :], smallb[:P, ti, :], r=["smb%d" % ti], w=["smh%d" % ti])
                        S.dve(nc.vector.tensor_tensor, small2[:P, 2 + ti, :], dtA, small2[:P, ti, :], ALU.subtract, r=[n(2 + ti), "smh%d" % ti], w=["sml%d" % ti])
                        S.pe(nc.tensor.matmul, psdt[:P, 64:96], U[:P, :P], dtA, start=True, stop=True, r=["U", n(2 + ti)], w=["psdt"])
                        S.pe(nc.tensor.matmul, psdt[:P, 96:128], BDm[:P, :P], dtA, start=True, stop=True, r=["BDm", n(2 + ti)], w=["psdt"])
                        S.pe(nc.tensor.matmul, psdt[:, 128:160], IA[:P, :], dtA, start=True, stop=True, r=["IA", n(2 + ti)], w=["psdt"])
                        if nch == 2:
                            S.pe(nc.tensor.matmul, psdt[:, 160:192], IB[:P, :], dtA, start=True, stop=True, r=["IB", n(2 + ti)], w=["psdt"])
                        S.act(nc.scalar.activation, expA, psdt[:P, 64:96], AF.Exp, r=["psdt"], w=[n(4 + ti)])
                        S.act(nc.scalar.copy, acs, psdt[:P, 64:96], r=["psdt"], w=[n(6 + ti)])
                        S.dve(nc.vector.tensor_tensor, toend, psdt[:P, 96:128], acs, ALU.subtract, r=["psdt", n(6 + ti)], w=[n(8 + ti)])
                        S.act(nc.scalar.activation, toend, toend, AF.Exp, r=[n(8 + ti)], w=[n(8 + ti)])
                        S.dve(nc.vector.tensor_tensor, dte, dt, toend, ALU.mult, r=[n(ti), n(8 + ti)], w=[n(10 + ti)])
                        S.act(nc.scalar.activation, dA, psdt[:, 128:160], AF.Exp, r=["psdt"], w=[n(12 + ti)])
                        if nch == 2:
                            S.act(nc.scalar.activation, dB, psdt[:, 160:192], AF.Exp, r=["psdt"], w=[n(14 + ti)])
                        btok, btokn = BTOK.next()
                        cbm, cbmn = CBM.next()
                        (cta, ctb), (ctan, ctbn) = CT.next()
                        for gg in range(4):
                            S.pe(nc.tensor.transpose, pstr[:P, gg * 128:(gg + 1) * 128], xbcT[:, 16 + gg, tc0:tc0 + P], ident_bf[:],
                                 r=["xbcT%d" % (16 + gg), "ident_bf"], w=["pstr"])
                        S.act(nc.scalar.copy, btok[:P], pstr[:P, 0:512].rearrange("p (g n) -> p g n", g=4), r=["pstr"], w=[btokn])
                        for gg in range(4):
                            S.pe(nc.tensor.matmul, pscb[:P, gg * P:(gg + 1) * P], xbcT[:, 16 + gg, tc0:tc0 + P], xbcT[:, 20 + gg, tc0:tc0 + P],
                                 start=True, stop=True, r=["xbcT%d" % (16 + gg), "xbcT%d" % (20 + gg)], w=["pscb"])
                        S.dve(nc.vector.tensor_tensor, cbm[:P, :, :P], pscb[:P, 0:4 * P].rearrange("p (g l) -> p g l", g=4),
                              U[:P, None, :P].broadcast_to([P, 4, P]), ALU.mult, r=["pscb", "U"], w=[cbmn])
                        if nch == 2:
                            S.pool(nc.gpsimd.tensor_copy, cta[:, :, 0:64], xbcT[:, 20:24, tc0:tc0 + 64], r=XBCN[20:24], w=[ctan])
                            S.pool(nc.gpsimd.tensor_copy, ctb[:, :, 64:128], xbcT[:, 20:24, tc0 + 64:tc0 + 128], r=XBCN[20:24], w=[ctbn])
                        return dict(ti=ti, tc0=tc0, btok=btok, btokn=btokn, cbm=cbm, cbmn=cbmn, cta=cta, ctb=ctb, ctan=ctan, ctbn=ctbn)

                    def st0(it):
                        ti, gg = it["ti"], it["gg"]
                        if gg == 0:
                            PRO[ti] = prologue(ti)
                        pro = PRO[ti]
                        it["pro"] = pro
                        tc0 = pro["tc0"]
                        hs = slice(8 * gg, 8 * gg + 8)
                        n = lambda k: "sm%d" % k
                        for j in range(4):
                            S.pe(nc.tensor.transpose, pstr[:P, 512 + j * 128:512 + (j + 1) * 128], xbcT[:, 4 * gg + j, tc0:tc0 + P], ident_bf[:],
                                 r=["xbcT%d" % (4 * gg + j), "ident_bf"], w=["pstr"])
                        xs_, xsn = XS.next()
                        S.act(nc.scalar.copy, xs_[:P], pstr[:P, 512:1024], r=["pstr"], w=[xsn])
                        xdt, xdtn = XDT.next()
                        xde, xden = XDE.next()
                        S.dve(nc.vector.tensor_tensor, xdt[:P].rearrange("p (h d) -> p h d", h=8), xs_[:P].rearrange("p (h d) -> p h d", h=8),
                              small[:P, ti, hs, None].broadcast_to([P, 8, 64]), ALU.mult, r=[xsn, n(ti)], w=[xdtn])
                        S.pool(nc.gpsimd.tensor_tensor, xde[:P].rearrange("p (h d) -> p h d", h=8), xs_[:P].rearrange("p (h d) -> p h d", h=8),
                               small[:P, 10 + ti, hs, None].broadcast_to([P, 8, 64]), ALU.mult, r=[xsn, n(10 + ti)], w=[xden])
                        (rgh, rgl), rgn = RG.next()
                        S.dve(nc.vector.tensor_tensor, rgh[:P, :, :P], U[:P, None, :P].broadcast_to([P, 8, P]),
                              small2[:P, ti, hs, None].broadcast_to([P, 8, P]), ALU.mult, r=["U", "smh%d" % ti], w=[rgn + "h"])
                        S.dve(nc.vector.tensor_tensor, rgl[:P, :, :P], U[:P, None, :P].broadcast_to([P, 8, P]),
                              small2[:P, 2 + ti, hs, None].broadcast_to([P, 8, P]), ALU.mult, r=["U", "sml%d" % ti], w=[rgn + "l"])
                        it.update(xs=xs_, xsn=xsn, xdt=xdt, xdtn=xdtn, xde=xde, xden=xden, rgh=rgh, rgl=rgl, rgn=rgn)

                    def st0b(it):
                        ti, gg, pro = it["ti"], it["gg"], it["pro"]
                        rgh, rgl, rgn = it["rgh"], it["rgl"], it["rgn"]
                        nsub = (8 * P) // 512
                        hps = 512 // P
                        for j in range(nsub):
                            S.pe(nc.tensor.matmul, psseg[:P, j * 512:(j + 1) * 512], SU_bf[:P, :P],
                                 rgh[:P, j * hps:(j + 1) * hps, :P], start=True, stop=False, r=["SU_bf", rgn + "h"], w=["psseg"])
                            S.pe(nc.tensor.matmul, psseg[:P, j * 512:(j + 1) * 512], SU_bf[:P, :P],
                                 rgl[:P, j * hps:(j + 1) * hps, :P], start=False, stop=True, r=["SU_bf", rgn + "l"], w=["psseg"])
                        dec, decn = DEC.next()
                        S.act(nc.scalar.activation, dec[:P, :, :P], psseg[:P, 0:8 * P].rearrange("p (h l) -> p h l", h=8), AF.Exp,
                              r=["psseg"], w=[decn])
                        mt, mtn = MT.next()
                        S.dve(nc.vector.tensor_tensor, mt[:P, :, :P], dec[:P, :, :P], pro["cbm"][:P, gg, None, :P].broadcast_to([P, 8, P]), ALU.mult,
                              r=[decn, pro["cbmn"]], w=[mtn])
                        it.update(mt=mt, mtn=mtn)

                    def st1(it):
                        ti, gg, pro = it["ti"], it["gg"], it["pro"]
                        tc0 = pro["tc0"]
                        hs = slice(8 * gg, 8 * gg + 8)
                        fs = slice(512 * gg, 512 * gg + 512)
                        n = lambda k: "sm%d" % k
                        mt, mtn, xdt, xdtn, xde, xden = it["mt"], it["mtn"], it["xdt"], it["xdtn"], it["xde"], it["xden"]
                        btok, btokn = pro["btok"], pro["btokn"]
                        sn, s0n, s1n = "Sst%d" % gg, "s0bf%d" % gg, "s1bf%d" % gg
                        for h in range(8):
                            S.pe(nc.tensor.matmul, psyi[:P, h * 64:(h + 1) * 64], mt[:P, h, :P], xdt[:P, h * 64:(h + 1) * 64],
                                 start=True, stop=True, r=[mtn, xdtn], w=["psyi"])
                        pyo, pyon = GEN.next()
                        if nch == 2:
                            S.pe(nc.tensor.matmul, pyo[:P, :], pro["cta"][:, gg, :], s0bf[:, fs], start=True, stop=False,
                                 r=[pro["ctan"], s0n], w=[pyon])
                        else:
                            S.pe(nc.tensor.matmul, pyo[:P, :], xbcT[:, 20 + gg, tc0:tc0 + P], s0bf[:, fs], start=True, stop=True,
                                 r=["xbcT%d" % (20 + gg), s0n], w=[pyon])
                        pds, pdsn = GEN.next()
                        S.pe(nc.tensor.matmul, pds[:, :], btok[0:64, gg, :], xde[0:64, :], start=True, stop=True, r=[btokn, xden], w=[pdsn])
                        S.pool(nc.gpsimd.tensor_tensor, stmp[:].rearrange("p (h d) -> p h d", h=8), Sst[:, fs].rearrange("p (h d) -> p h d", h=8),
                               small[:, 12 + ti, hs, None].broadcast_to([128, 8, 64]), ALU.mult, r=[sn, n(12 + ti), "stmp"], w=["stmp"])
                        S.dve(nc.vector.tensor_tensor, Sst[:, fs], stmp[:], pds[:, :], ALU.add, r=["stmp", pdsn, sn], w=[sn])
                        if nch == 2:
                            S.act(nc.scalar.copy, s1bf[:, fs], Sst[:, fs], r=[sn], w=[s1n])
                        it.update(pyo=pyo, pyon=pyon, pds=pds, pdsn=pdsn)

                    def st1b(it):
                        ti, gg, pro = it["ti"], it["gg"], it["pro"]
                        hs = slice(8 * gg, 8 * gg + 8)
                        fs = slice(512 * gg, 512 * gg + 512)
                        n = lambda k: "sm%d" % k
                        xde, xden = it["xde"], it["xden"]
                        btok, btokn = pro["btok"], pro["btokn"]
                        sn, s0n, s1n = "Sst%d" % gg, "s0bf%d" % gg, "s1bf%d" % gg
                        pyo, pyon, pds, pdsn = it["pyo"], it["pyon"], it["pds"], it["pdsn"]
                        if nch == 2:
                            S.pe(nc.tensor.matmul, pyo[:P, :], pro["ctb"][:, gg, :], s1bf[:, fs], start=False, stop=True,
                                 r=[pro["ctbn"], s1n], w=[pyon])
                            S.pe(nc.tensor.matmul, pds[:, :], btok[64:128, gg, :], xde[64:128, :], start=True, stop=True,
                                 r=[btokn, xden], w=[pdsn])
                            S.pool(nc.gpsimd.tensor_tensor, stmp[:].rearrange("p (h d) -> p h d", h=8),
                                   Sst[:, fs].rearrange("p (h d) -> p h d", h=8),
                                   small[:, 14 + ti, hs, None].broadcast_to([128, 8, 64]), ALU.mult, r=[sn, n(14 + ti), "stmp"], w=["stmp"])
                            S.dve(nc.vector.tensor_tensor, Sst[:, fs], stmp[:], pds[:, :], ALU.add, r=["stmp", pdsn, sn], w=[sn])
                        S.act(nc.scalar.copy, s0bf[:, fs], Sst[:, fs], r=[sn], w=[s0n])
                        it.update(pyo=pyo, pyon=pyon)

                    def st2(it):
                        ti, gg = it["ti"], it["gg"]
                        hs = slice(8 * gg, 8 * gg + 8)
                        fs = slice(512 * gg, 512 * gg + 512)
                        n = lambda k: "sm%d" % k
                        pyo, pyon, xs_, xsn = it["pyo"], it["pyon"], it["xs"], it["xsn"]
                        yb, ybn = YB.next()
                        S.dve(nc.vector.tensor_tensor, yb[:P].rearrange("p (h d) -> p h d", h=8), pyo[:P, :].rearrange("p (h d) -> p h d", h=8),
                              small[:P, 4 + ti, hs, None].broadcast_to([P, 8, 64]), ALU.mult, r=[pyon, n(4 + ti)], w=[ybn])
                        S.dve(nc.vector.tensor_tensor, yb[:P], yb[:P], psyi[:P, :], ALU.add, r=[ybn, "psyi"], w=[ybn])
                        S.pool(nc.gpsimd.tensor_tensor, dsk[:P].rearrange("p (h d) -> p h d", h=8), xs_[:P].rearrange("p (h d) -> p h d", h=8),
                               dsk_bc[:P, hs, None].broadcast_to([P, 8, 64]), ALU.mult, r=[xsn, "dsk_bc"], w=["dsk"])
                        S.dve(nc.vector.tensor_tensor, yb[:P], yb[:P], dsk[:P], ALU.add, r=[ybn, "dsk"], w=[ybn])
                        S.dve(nc.vector.tensor_tensor, yb[:P], yb[:P], sz[:P, ti, fs], ALU.mult, r=[ybn, "sz"], w=[ybn])
                        S.dve(nc.vector.memset, ssq[:P, 1:2], 0.0, w=["ssq"])
                        S.act(nc.scalar.activation, junk[:P, 0:512], yb[:P], AF.Square, accum_out=ssq[:P, 1:2], r=[ybn, "ssq"], w=["junk", "ssq"])
                        rstd_from_ss(ssq[:P, 1:2], 512, ["ssq"])
                        ynb, ynbn = YNB.next()
                        S.dve(nc.vector.scalar_tensor_tensor, ynb[:P], yb[:P], ssq[:P, 1:2], snw_bc[:P, fs], ALU.mult, ALU.mult,
                              r=[ybn, "ssq", "snw_bc"], w=[ynbn])
                        it.update(ynb=ynb, ynbn=ynbn)

                    def st3(it):
                        gg, tc0 = it["gg"], it["pro"]["tc0"]
                        ynb, ynbn = it["ynb"], it["ynbn"]
                        for j in range(4):
                            S.pe(nc.tensor.transpose, pstr[:, j * P:(j + 1) * P], ynb[:P, j * 128:(j + 1) * 128], ident_bf[:P, :P],
                                 r=[ynbn, "ident_bf"], w=["pstr"])
                        S.act(nc.scalar.copy, ynT[:, 4 * gg:4 * gg + 4, tc0:tc0 + P], pstr[:, 0:4 * P].rearrange("p (c t) -> p c t", c=4),
                              r=["pstr"], w=["ynT"])

                    PRO = {}
                    items = [dict(ti=ti, gg=gg) for ti in range(NT) for gg in range(4)]
                    def run(fn, kk):
                        if 0 <= kk < len(items):
                            fn(items[kk])
                    for step in range(len(items) + 3):
                        run(st0, step)
                        run(st3, step - 3)
                        run(st2, step - 2)
                        run(st1, step - 1)
                        run(st0b, step)
                        run(st1b, step - 1)
                    for m in range(8):
                        wb, wbn = wbq.pop(0)
                        ps, pn = GEN.next()
                        for k in range(16):
                            S.pe(nc.tensor.matmul, ps[:, 0:GW], wb[:, k, :], ynT[:, k, 0:GW], start=(k == 0), stop=(k == 15),
                                 r=[wbn, "ynT"], w=[pn])
                        if m + 2 < 8:
                            wb2, wbn2 = WBS.next()
                            S.dma("sp", wb2[:], wbs_b[m + 2], r=["wbs_b"], w=[wbn2])
                            wbq.append((wb2, wbn2))
                        stg, stgn = STG.next()
                        S.dve(nc.vector.tensor_tensor, stg[:, 0:GW], ps[:, 0:GW], sgs[:, m, 0:GW], ALU.mult, r=[pn, "sgs"], w=[stgn])
                        S.dma("sp", sq["MS"][m * 128:(m + 1) * 128, t0:t0 + GW], stg[:, 0:GW], r=[stgn], w=["MSscr%d" % si])
                vco = sq["conv"].rearrange("j (c p) -> p c j", p=128)
                for c in range(24):
                    S.dma("sp", vco[:, c, :], halo[:, c, :], r=["halo%d" % c], allow_slow_non_contiguous=True)
                for c in range(16):
                    ps, pn = GEN.next()
                    S.pe(nc.tensor.transpose, ps[:, 0:128], Sst[:, c * 128:(c + 1) * 128], ident_f[:],
                         r=["Sst%d" % (c // 4), "ident_f"], w=[pn])
                    S.act(nc.scalar.copy, sttmp[:, c, :], ps[:, 0:128], r=[pn], w=["sttmp"])
                S.dma("sp", sq["ssm"].rearrange("(c q) n -> q c n", q=128), sttmp[:], r=["sttmp"])
            S.flush()

        if _stop == 1:
            return nc
        with ExitStack() as st:
            sb = lambda n, sh, dt=F32: st.enter_context(nc.sbuf_tensor(n, sh, dt))
            psb = lambda n, sh, dt=F32: st.enter_context(nc.psum_tensor(n, sh, dt))
            LA = 1
            PSS = Ring([(psb(f"pss{i}", [128, 2, 512]), f"pss{i}") for i in range(2)])
            psl0 = psb("psl0", [128, 512])
            pso = [psb(f"pso{j}", [128, 512]) for j in range(2)]
            TKmax = max(s_["Tk"] for s_ in seqs)
            TQmax = max(s_["T"] for s_ in seqs)
            NKTmax = (TKmax + 127) // 128
            KTB = Ring([(sb(f"ktb{i}", [128, TKmax], BF16), f"ktb{i}") for i in range(2)])
            VB = Ring([(sb(f"vb{i}", [128, NKTmax, 128], BF16), f"vb{i}") for i in range(2)])
            QTB = Ring([(sb(f"qtb{i}", [128, TQmax], BF16), f"qtb{i}") for i in range(2)])
            PT = Ring([(sb(f"pt{i}", [128, 2, 512], BF16), f"pt{i}") for i in range(LA + 3)])
            ACC = Ring([(sb(f"acc{i}", [128, 2, 512]), f"acc{i}") for i in range(2)])
            rl = sb("rl", [128, 2, 512]); oraw = sb("oraw", [128, 2, 512])
            ob = sb("ob", [128, 512]); ob2 = sb("ob2", [128, 512]); osq = sb("osq", [128, 512]); rs2 = sb("rs2", [128, 512])
            OST = Ring([(sb(f"ost{i}", [128, 512], BF16), f"ost{i}") for i in range(2)])
            iters = []
            for si, sq in enumerate(seqs):
                T, Tk, pastn = sq["T"], sq["Tk"], sq["past"]
                causal = (pastn == 0)
                nkt = (Tk + 127) // 128
                QG = min(512, T)
                for h in range(8):
                    for qg in range(T // QG):
                        q0 = qg * QG
                        kts = list(range(0, (q0 + QG) // 128)) if causal else list(range(nkt))
                        grp = dict(si=si, sq=sq, h=h, q0=q0, QG=QG, first=(qg == 0), T=T, Tk=Tk, pastn=pastn)
                        for kt in kts:
                            nk = min(128, Tk - kt * 128)
                            qs = max(q0, kt * 128) if causal else q0
                            iters.append(dict(grp=grp, kt=kt, nk=nk, qs=qs, nq=q0 + QG - qs, lo=qs - q0,
                                              diag=causal and (kt * 128 >= q0), first=(kt == 0), last=(kt == kts[-1])))
            cur = {}

            def load_head(grp):
                sq, si, h, T, Tk, pastn = grp["sq"], grp["si"], grp["h"], grp["T"], grp["Tk"], grp["pastn"]
                ktb, ktn = KTB.next(); vb, vbn = VB.next(); qtb, qtn = QTB.next()
                S.dma("sp", ktb[:, 0:Tk], sq["KT"][h * 128:(h + 1) * 128, :], w=[ktn])
                S.dma("sp", qtb[:, 0:T], sq["QT"][h * 128:(h + 1) * 128, :], w=[qtn])
                nfull = Tk // 128
                S.dma("sp", vb[:, 0:nfull, :], sq["V"][0:nfull * 128, h * 128:(h + 1) * 128].rearrange("(j p) e -> p j e", p=128), w=[vbn])
                if Tk % 128:
                    rem = Tk % 128
                    S.dma("sp", vb[0:rem, nfull, :], sq["V"][nfull * 128:Tk, h * 128:(h + 1) * 128], w=[vbn])
                grp["bufs"] = (ktb, ktn, vb, vbn, qtb, qtn)

            def front(it):
                grp = it["grp"]
                key = (grp["si"], grp["h"])
                if key not in cur:
                    load_head(grp)
                    cur[key] = grp["bufs"]
                grp["bufs"] = cur[key]
                ktb, ktn, vb, vbn, qtb, qtn = grp["bufs"]
                if "acc" not in grp:
                    grp["acc"] = ACC.next()
                acc, accn = grp["acc"]
                kt, nk, qs, nq, lo = it["kt"], it["nk"], it["qs"], it["nq"], it["lo"]
                pss, pssn = PSS.next()
                for j in range(2):
                    S.pe(nc.tensor.matmul, pss[:nk, j, 0:nq], ktb[64 * j:64 * j + 64, kt * 128:kt * 128 + nk],
                         qtb[64 * j:64 * j + 64, qs:qs + nq], start=True, stop=True, r=[ktn, qtn], w=[pssn])
                pt, ptn = PT.next()
                it["pt"] = (pt, ptn)
                S.act(nc.scalar.activation, pt[:nk, :, 0:nq], pss[:nk, :, 0:nq], AF.Exp, r=[pssn], w=[ptn])
                if it["diag"]:
                    S.pool(nc.gpsimd.memset, pt[64:128, :, 0:64], 0.0, r=[ptn], w=[ptn])
                if it["first"]:
                    S.dve(nc.vector.tensor_copy, acc[:nk, 1, lo:lo + nq], pt[:nk, 1, 0:nq], r=[ptn], w=[accn + "b"])
                else:
                    S.dve(nc.vector.tensor_tensor, acc[:nk, 1, lo:lo + nq], acc[:nk, 1, lo:lo + nq], pt[:nk, 1, 0:nq], ALU.add,
                          r=[ptn, accn + "b"], w=[accn + "b"])

            def back(it):
                grp = it["grp"]
                ktb, ktn, vb, vbn, qtb, qtn = grp["bufs"]
                kt, nk, nq, lo = it["kt"], it["nk"], it["nq"], it["lo"]
                pt, ptn = it["pt"]
                for j in range(2):
                    S.pe(nc.tensor.matmul, pso[j][:, lo:lo + nq], vb[:nk, kt, :], pt[:nk, j, 0:nq], start=it["first"], stop=it["last"],
                         r=[vbn, ptn], w=["pso%d" % j])
                S.pe(nc.tensor.matmul, psl0[:, lo:lo + nq], ones_bf[:nk, :], pt[:nk, 0, 0:nq], start=it["first"], stop=it["last"],
                     r=["ones_bf", ptn], w=["psl0"])
                if it["last"]:
                    finalize(grp)

            def finalize(grp):
                sq, si, h, q0, QG = grp["sq"], grp["si"], grp["h"], grp["q0"], grp["QG"]
                acc, accn = grp["acc"]
                for j in range(2):
                    S.act(nc.scalar.copy, oraw[:, j, 0:QG], pso[j][:, 0:QG], r=["pso%d" % j], w=["oraw"])
                psl, psln = PSS.next()
                S.pe(nc.tensor.matmul, psl[:, 1, 0:QG], ones_f[:], acc[:, 1, 0:QG], start=True, stop=True, r=["ones_f", accn + "b"], w=[psln])
                S.dve(nc.vector.reciprocal, rl[:, 0, 0:QG], psl0[:, 0:QG], r=["psl0"], w=["rl"])
                S.dve(nc.vector.reciprocal, rl[:, 1, 0:QG], psl[:, 1, 0:QG], r=[psln, "rl"], w=["rl"])
                S.dve(nc.vector.tensor_tensor, ob[:, 0:QG], oraw[:, 0, 0:QG], rl[:, 0, 0:QG], ALU.mult, r=["oraw", "rl"], w=["ob"])
                S.dve(nc.vector.tensor_tensor, ob2[:, 0:QG], oraw[:, 1, 0:QG], rl[:, 1, 0:QG], ALU.mult, r=["oraw", "rl"], w=["ob2"])
                S.dve(nc.vector.scalar_tensor_tensor, osq[:, 0:QG], ob2[:, 0:QG], neglam, ob[:, 0:QG], ALU.mult, ALU.add,
                      r=["ob", "ob2", "lams"], w=["osq"])
                S.pool(nc.gpsimd.tensor_tensor, ob2[:, 0:QG], osq[:, 0:QG], osq[:, 0:QG], ALU.mult, r=["osq"], w=["ob2"])
                psq, psqn = PSS.next()
                S.pe(nc.tensor.matmul, psq[:, 0, 0:QG], ones_f[:], ob2[:, 0:QG], start=True, stop=True, r=["ones_f", "ob2"], w=[psqn])
                S.act(nc.scalar.activation, rs2[:, 0:QG], psq[:, 0, 0:QG], AF.Ln, scale=1.0 / 128, bias=EPS, r=[psqn], w=["rs2"])
                S.act(nc.scalar.activation, rs2[:, 0:QG], rs2[:, 0:QG], AF.Exp, scale=-0.5, r=["rs2"], w=["rs2"])
                ost, ostn = OST.next()
                S.dve(nc.vector.scalar_tensor_tensor, ost[:, 0:QG], osq[:, 0:QG], sublw[:, 0:1], rs2[:, 0:QG], ALU.mult, ALU.mult,
                      r=["osq", "sublw", "rs2"], w=[ostn])
                S.dma("sp", sq["OT"][h * 128:(h + 1) * 128, q0:q0 + QG], ost[:, 0:QG], r=[ostn], w=["OTscr%d" % si])

            pend = []
            for it in iters:
                front(it)
                pend.append(it)
                if len(pend) > LA:
                    back(pend.pop(0))
            while pend:
                back(pend.pop(0))
            S.flush()

        if _stop == 2:
            return nc
        with ExitStack() as st:
            sb = lambda n, sh, dt=F32: st.enter_context(nc.sbuf_tensor(n, sh, dt))
            psb = lambda n, sh, dt=F32: st.enter_context(nc.psum_tensor(n, sh, dt))
            nfw_bc = bc_load("nfw_bc", norm_ffn_w[0], D, st)
            GEN = Ring([(psb(f"p3g{i}", [128, 512]), f"p3g{i}") for i in range(4)])
            pstr = psb("p3tr", [128, 1024], BF16)
            PSHB = Ring([(psb(f"p3hb{i}", [128, 512]), f"p3hb{i}") for i in range(2)])
            wba = sb("wba", [128, 8, D], BF16); wout = sb("wout", [128, 8, D], BF16); wdn = sb("wdn", [128, 22, D], BF16)
            S.dma("sp", wba[:], wba_b, w=["wba"])
            S.dma("sp", wout[:], wout_b, w=["wout"])
            S.dma("sp", wdn[:, 0:11], wdn_b[:, 0:11], w=["wdn"])
            S.dma("sp", wdn[:, 11:22], wdn_b[:, 11:22], w=["wdn"])
            WR = Ring([(sb(f"w3r{i}", [128, 8, 512], BF16), f"w3r{i}") for i in range(2)])
            oT = sb("oT", [128, 8, 256], BF16)
            gat = sb("gat", [128, 8, 256]); mss = sb("mss", [128, 8, 256])
            mixT = sb("mixT", [128, 8, 256], BF16)
            mtmp = sb("mtmp", [128, 256])
            XT = Ring([(sb(f"x3t{i}", [128, D]), f"x3t{i}") for i in range(2)])
            x1 = sb("x1", [128, 2, D])
            xnb = sb("x3nb", [128, D], BF16)
            xn2T = sb("xn2T", [128, 8, 256], BF16)
            ssq = sb("ssq3", [128, 4]); junk = sb("junk3", [128, D], BF16)
            PRE = Ring([(sb(f"pre3{i}", [128, 2 + 256]), f"pre3{i}") for i in range(2)])
            halo2 = sb("halo2", [128, 22, 2])
            ctmp = sb("ctmp3", [128, 256]); sha = sb("sha", [128, 256])
            gT = sb("gT", [128, 22, 256], BF16)
            YST = Ring([(sb(f"yst{i}", [128, D]), f"yst{i}") for i in range(1)])
            for si, sq in enumerate(seqs):
                T, P = sq["T"], sq["P"]
                NT = 2 if P == 128 else 1
                GW = P * NT
                NG = T // GW
                if sq["past"] == 0:
                    S.pool(nc.gpsimd.memset, halo2[:], 0.0, w=["halo2"])
                else:
                    vsf = sffn[sq["idx"]].rearrange("j (c p) -> p c j", p=128)
                    for c in range(22):
                        S.dma("sp", halo2[:, c, :], vsf[:, c, :], w=["halo2"], allow_slow_non_contiguous=True)
                for g in range(NG):
                    t0 = g * GW
                    S.dma("sp", oT[:, :, 0:GW], sq["OT"].rearrange("(c p) t -> p c t", p=128)[:, :, t0:t0 + GW], r=["OTscr%d" % si], w=["oT"])
                    S.dma("sp", gat[:, :, 0:GW], sq["GA"].rearrange("(c p) t -> p c t", p=128)[:, :, t0:t0 + GW], r=["GAscr%d" % si], w=["gat"])
                    S.dma("sp", mss[:, :, 0:GW], sq["MS"].rearrange("(c p) t -> p c t", p=128)[:, :, t0:t0 + GW], r=["MSscr%d" % si], w=["mss"])
                    for m in range(8):
                        ps, pn = GEN.next()
                        for k in range(8):
                            S.pe(nc.tensor.matmul, ps[:, 0:GW], wba[:, k, m * 128:(m + 1) * 128], oT[:, k, 0:GW], start=(k == 0), stop=(k == 7),
                                 r=["wba", "oT"], w=[pn])
                        S.dve(nc.vector.tensor_tensor, mtmp[:, 0:GW], ps[:, 0:GW], gat[:, m, 0:GW], ALU.mult, r=[pn, "gat"], w=["mtmp"])
                        S.dve(nc.vector.tensor_tensor, mixT[:, m, 0:GW], mtmp[:, 0:GW], mss[:, m, 0:GW], ALU.add, r=["mtmp", "mss"], w=["mixT"])
                    for ti in range(NT):
                        r0 = t0 + ti * P
                        xt, xn_ = XT.next()
                        S.dma("sp", xt[:P], sq["x"][r0:r0 + P, :], w=[xn_])
                        for nb in range(2):
                            ps, pn = GEN.next()
                            for k in range(8):
                                S.pe(nc.tensor.matmul, ps[:P, :], mixT[:, k, ti * P:(ti + 1) * P], wout[:, k, nb * 512:(nb + 1) * 512],
                                     start=(k == 0), stop=(k == 7), r=["mixT", "wout"], w=[pn])
                            S.dve(nc.vector.tensor_tensor, x1[:P, ti, nb * 512:(nb + 1) * 512], ps[:P, :], xt[:P, nb * 512:(nb + 1) * 512], ALU.add,
                                  r=[pn, xn_], w=["x1_%d" % ti])
                        S.dve(nc.vector.memset, ssq[:P, 0:1], 0.0, w=["ssq3"])
                        S.act(nc.scalar.activation, junk[:P], x1[:P, ti, :], AF.Square, accum_out=ssq[:P, 0:1], r=["x1_%d" % ti, "ssq3"], w=["junk3", "ssq3"])
                        S.act(nc.scalar.activation, ssq[:P, 0:1], ssq[:P, 0:1], AF.Ln, scale=1.0 / D, bias=EPS, r=["ssq3"], w=["ssq3"])
                        S.act(nc.scalar.activation, ssq[:P, 0:1], ssq[:P, 0:1], AF.Exp, scale=-0.5, r=["ssq3"], w=["ssq3"])
                        S.dve(nc.vector.scalar_tensor_tensor, xnb[:P], x1[:P, ti, :], ssq[:P, 0:1], nfw_bc[:P], ALU.mult, ALU.mult,
                              r=["x1_%d" % ti, "ssq3", "nfw_bc"], w=["x3nb"])
                        for k in range(8):
                            S.pe(nc.tensor.transpose, pstr[:, k * P:(k + 1) * P], xnb[:P, k * 128:(k + 1) * 128], ident_bf[:P, :P],
                                 r=["x3nb", "ident_bf"], w=["p3tr"])
                        S.act(nc.scalar.copy, xn2T[:, :, ti * P:(ti + 1) * P], pstr[:, 0:8 * P].rearrange("p (k t) -> p k t", k=8),
                              r=["p3tr"], w=["xn2T"])
                    for bj in range(11):
                        wt, wn = WR.next()
                        S.dma("sp", wt[:], wup_b[bj], r=["wup_b"], w=[wn])
                        for cc in range(2):
                            c = bj * 2 + cc
                            ps, pn = GEN.next()
                            for k in range(8):
                                S.pe(nc.tensor.matmul, ps[:, 0:GW], wt[:, k, cc * 128:(cc + 1) * 128], xn2T[:, k, 0:GW],
                                     start=(k == 0), stop=(k == 7), r=[wn, "xn2T"], w=[pn])
                            phb, phbn = PSHB.next()
                            for k in range(8):
                                S.pe(nc.tensor.matmul, phb[:, 0:GW], wt[:, k, 256 + cc * 128:256 + (cc + 1) * 128], xn2T[:, k, 0:GW],
                                     start=(k == 0), stop=(k == 7), r=[wn, "xn2T"], w=[phbn])
                            pre, pren = PRE.next()
                            S.pool(nc.gpsimd.tensor_copy, pre[:, 0:2], halo2[:, c, :], r=["halo2"], w=[pren])
                            S.act(nc.scalar.copy, pre[:, 2:2 + GW], ps[:, 0:GW], r=[pn], w=[pren])
                            S.pool(nc.gpsimd.tensor_copy, halo2[:, c, :], pre[:, GW:GW + 2], r=[pren], w=["halo2"])
                            S.dve(nc.vector.tensor_scalar, ctmp[:, 0:GW], pre[:, 0:GW], fcw[:, c, 0:1], fcb[:, c:c + 1], ALU.mult, ALU.add,
                                  r=[pren, "fcw", "fcb"], w=["ctmp3"])
                            for j in range(1, 3):
                                S.dve(nc.vector.scalar_tensor_tensor, ctmp[:, 0:GW], pre[:, j:j + GW], fcw[:, c, j:j + 1], ctmp[:, 0:GW],
                                      ALU.mult, ALU.add, r=[pren, "fcw", "ctmp3"], w=["ctmp3"])
                            S.act(nc.scalar.activation, sha[:, 0:GW], ctmp[:, 0:GW], AF.Silu, r=["ctmp3"], w=["sha"])
                            S.dve(nc.vector.tensor_tensor, gT[:, c, 0:GW], sha[:, 0:GW], phb[:, 0:GW], ALU.mult, r=["sha", phbn], w=["gT"])
                    for ti in range(NT):
                        r0 = t0 + ti * P
                        yst, ystn = YST.next()
                        for nb in range(2):
                            ps, pn = GEN.next()
                            for k in range(22):
                                S.pe(nc.tensor.matmul, ps[:P, :], gT[:, k, ti * P:(ti + 1) * P], wdn[:, k, nb * 512:(nb + 1) * 512],
                                     start=(k == 0), stop=(k == 21), r=["gT", "wdn"], w=[pn])
                            S.dve(nc.vector.tensor_tensor, yst[:P, nb * 512:(nb + 1) * 512], ps[:P, :], x1[:P, ti, nb * 512:(nb + 1) * 512], ALU.add,
                                  r=[pn, "x1_%d" % ti], w=[ystn])
                        S.dma("sp", sq["y"][r0:r0 + P, :], yst[:P], r=[ystn])
                vfo = sq["ffn"].rearrange("j (c p) -> p c j", p=128)
                for c in range(22):
                    S.dma("sp", vfo[:, c, :], halo2[:, c, :], r=["halo2"], allow_slow_non_contiguous=True)
            S.flush()
    return nc


_PROG_CACHE = {}


def kernel(x_prompt, x_sample, cache_k, cache_v, state_ssm, state_ssm_conv, state_ffn_conv,
           norm_mix_w, w_in, ssm_conv_w, ssm_conv_b, ssm_dt_bias, ssm_a_log, ssm_d, ssm_norm_w,
           q_norm_w, k_norm_w, lambda_q1, lambda_k1, lambda_q2, lambda_k2, subln_w,
           w_branch_ssm, w_branch_attn, w_out, norm_ffn_w, w_up, ffn_conv_w, ffn_conv_b, w_down):
    f = lambda a: np.ascontiguousarray(np.asarray(a, dtype=np.float32))
    x_prompt = f(x_prompt); x_sample = f(x_sample)
    B, Tp, _ = x_prompt.shape
    BS, Ts, _ = x_sample.shape
    past = cache_k.shape[2]
    import os as _os
    NC = int(_os.environ.get('KCORES', '8'))
    NS = BS // 8
    key = (Tp, NS, Ts, past)
    nc = build_program(Tp, NS, Ts, past)
    ck = f(cache_k)[0].reshape(BS, past, D)
    cv = f(cache_v)[0].reshape(BS, past, D)
    sssm = f(state_ssm)[0]
    sconv = f(state_ssm_conv)[0]
    sffn = f(state_ffn_conv)[0]
    lam_in = np.stack([f(lambda_q1)[0], f(lambda_k1)[0], f(lambda_q2)[0], f(lambda_k2)[0]], axis=0)
    shared = {
        "norm_mix_w": f(norm_mix_w), "w_in": f(w_in)[0], "ssm_conv_w": f(ssm_conv_w)[0], "ssm_conv_b": f(ssm_conv_b),
        "ssm_dt_bias": f(ssm_dt_bias), "ssm_a_log": f(ssm_a_log), "ssm_d": f(ssm_d), "ssm_norm_w": f(ssm_norm_w),
        "q_norm_w": f(q_norm_w), "k_norm_w": f(k_norm_w), "lam_in": f(lam_in), "subln_w": f(subln_w)[0].reshape(128, 1),
        "w_bs": f(w_branch_ssm)[0], "w_ba": f(w_branch_attn)[0], "w_out": f(w_out)[0], "norm_ffn_w": f(norm_ffn_w),
        "w_up": f(w_up)[0], "ffn_conv_w": f(ffn_conv_w)[0], "ffn_conv_b": f(ffn_conv_b), "w_down": f(w_down)[0],
    }
    in_maps = []
    for c in range(NC):
        m = dict(shared)
        m["xp"] = x_prompt[c]
        sl = slice(c * NS, (c + 1) * NS)
        m["xs"] = x_sample[sl]; m["ck"] = ck[sl]; m["cv"] = cv[sl]
        m["sssm"] = sssm[sl]; m["sconv"] = sconv[sl]; m["sffn"] = sffn[sl]
        in_maps.append(m)
    res = run_bass_kernel_spmd(nc, in_maps, core_ids=list(range(NC)))
    R = res.results
    global _LAST
    _LAST = R
    if NC < 8:
        R = list(R) + [R[0]] * (8 - NC)
        NC = 8
    cat = lambda k: np.stack([R[c][k] for c in range(NC)], axis=0)
    cats = lambda k: np.concatenate([R[c][k] for c in range(NC)], axis=0)
    y_p = cat("y_p")
    y_s = cats("y_s")
    k_p = cat("k_p").reshape(1, B, Tp, 16, 64)
    v_p = cat("v_p").reshape(1, B, Tp, 8, 128)
    ssm_p = cat("ssm_p").reshape(1, B, 32, 64, 128)
    conv_p = cat("conv_p").reshape(1, B, 3, CD)
    ffn_p = cat("ffn_p").reshape(1, B, 2, DFF)
    k_s = cats("k_s").reshape(1, BS, Ts, 16, 64)
    v_s = cats("v_s").reshape(1, BS, Ts, 8, 128)
    ssm_s = cats("ssm_s").reshape(1, BS, 32, 64, 128)
    conv_s = cats("conv_s").reshape(1, BS, 3, CD)
    ffn_s = cats("ffn_s").reshape(1, BS, 2, DFF)
    return (y_p, y_s, k_p, v_p, ssm_p, conv_p, ffn_p, k_s, v_s, ssm_s, conv_s, ffn_s)
```

```python
import math
from contextlib import ExitStack
import numpy as np
import concourse.bass as bass
import concourse.mybir as mybir
from concourse.bass_utils import run_bass_kernel_spmd

F32 = mybir.dt.float32
BF16 = mybir.dt.bfloat16
I32 = mybir.dt.int32
AF = mybir.ActivationFunctionType
ALU = mybir.AluOpType
AX = mybir.AxisListType

SAME_ENGINE_SYNC = True
EPOCH = 20000
NDMA_SEM = 24
EPS = 1e-6
D = 1024
DI = 2048
CD = 3072
DFF = 2816
INW = 10272
ROPE_THETA = 500000.0


class Sched:
    def __init__(self, nc, stack):
        self.nc = nc
        self.eng = {"pe": nc.tensor, "act": nc.scalar, "dve": nc.vector, "pool": nc.gpsimd, "sp": nc.sync}
        self.ops = []
        self.last_writer = {}
        self.readers = {}
        self.stack = stack
        self.sig_count = {s: 0 for s in self.eng}
        self.sems = {s: [] for s in self.eng}
        self.dma_sems = {s: [stack.enter_context(nc.semaphore(f"dq_{s}_{i}")) for i in range(NDMA_SEM)]
                         for s in ("sp", "act", "pool")}
        self.dma_count = {s: 0 for s in ("sp", "act", "pool")}
        self.waited = {}
        self.n_emitted = 0

    def _sem(self, stream, epoch):
        while len(self.sems[stream]) <= epoch:
            i = len(self.sems[stream])
            self.sems[stream].append(self.stack.enter_context(self.nc.semaphore(f"s_{stream}_{i}")))
        return self.sems[stream][epoch]

    def op(self, stream, method, *args, r=(), w=(), dma=False, **kwargs):
        oid = len(self.ops)
        deps = set()
        for x in list(r) + list(w):
            lw = self.last_writer.get(x)
            if lw is not None:
                deps.add(lw)
        for x in w:
            for rd in self.readers.get(x, ()):
                deps.add(rd)
        for x in r:
            if x[:2] in ("ps", "p3"):
                for rd in self.readers.get(x, ()):
                    if self.ops[rd]["stream"] != stream:
                        deps.add(rd)
        deps.discard(oid)
        for x in r:
            self.readers.setdefault(x, []).append(oid)
        for x in w:
            self.last_writer[x] = oid
            self.readers[x] = []
        self.ops.append(dict(stream=stream, method=method, args=args, kwargs=kwargs, deps=deps, dma=dma,
                             sig=False))
        return oid

    def pe(self, m, *a, **k):
        return self.op("pe", m, *a, **k)

    def act(self, m, *a, **k):
        return self.op("act", m, *a, **k)

    def dve(self, m, *a, **k):
        return self.op("dve", m, *a, **k)

    def pool(self, m, *a, **k):
        return self.op("pool", m, *a, **k)

    def dma(self, q, out, in_, r=(), w=(), **k):
        m = self.eng[q].dma_start
        return self.op(q, m, r=r, w=w, dma=True, out=out, in_=in_, **k)

    def _wait(self, stream, sem, val):
        key = (stream, id(sem))
        if self.waited.get(key, 0) >= val:
            return
        self.waited[key] = val
        if getattr(self, 'dbg', False):
            print('   WAIT', stream, getattr(sem, 'name', sem), val)
        self.eng[stream].wait_ge(sem, val)

    def flush(self):
        import os as _os
        _cut = int(_os.environ.get('KCUT', '0'))
        if _cut > 0 and self.n_emitted == 0:
            print('total ops in flush', len(self.ops))
            for _i in range(max(0, _cut - 6), min(len(self.ops), _cut + 2)):
                _o = self.ops[_i]
                print('OP', _i, _o['stream'], getattr(_o['method'], '__name__', _o['method']), [str(a)[:150] for a in _o['args']], {k: str(v)[:150] for k, v in _o['kwargs'].items()})
            self.ops = self.ops[:_cut]
        ops = self.ops
        for o in ops:
            for d in o["deps"]:
                od = ops[d]
                if od["dma"]:
                    continue
                if od["stream"] == o["stream"] and not o["dma"]:
                    if od["stream"] == "pe" or not SAME_ENGINE_SYNC:
                        continue
                od["sig"] = True
        lastc = {}
        for i, o in enumerate(ops):
            if not o["dma"]:
                lastc[o["stream"]] = i
        for s, i in lastc.items():
            ops[i]["sig"] = True
        for i, o in enumerate(ops):
            s = o["stream"]
            self.dbg = (_cut > 0 and i >= _cut - 12)
            if self.dbg:
                print('EMIT', i, s, getattr(o['method'], '__name__', ''), sorted(o['deps']), 'sig', o['sig'])
            for d in sorted(o["deps"]):
                od = ops[d]
                if od["dma"]:
                    self._wait(s, od["dsem"], od["dval"])
                else:
                    if od["stream"] == s and not o["dma"]:
                        if s == "pe" or not SAME_ENGINE_SYNC:
                            continue
                    self._wait(s, od["ssem"], od["sval"])
            if o["dma"]:
                k = self.dma_count[s]
                self.dma_count[s] = k + 1
                sem = self.dma_sems[s][k % NDMA_SEM]
                use = k // NDMA_SEM
                if use > 0:
                    self._wait(s, sem, 16 * use)
                inst = o["method"](*o["args"], **o["kwargs"])
                inst.then_inc(sem, 16)
                o["dsem"] = sem
                o["dval"] = 16 * (use + 1)
            else:
                inst = o["method"](*o["args"], **o["kwargs"])
                if o["sig"]:
                    k = self.sig_count[s]
                    self.sig_count[s] = k + 1
                    sem = self._sem(s, k // EPOCH)
                    inst.then_inc(sem, 1)
                    o["ssem"] = sem
                    o["sval"] = k % EPOCH + 1
            o["args"] = None
            o["kwargs"] = None
        self.n_emitted += len(ops)
        for s in list(self.eng.keys()):
            for s2, i in lastc.items():
                if s2 != s:
                    self._wait(s, ops[i]["ssem"], ops[i]["sval"])
            for q in self.dma_sems:
                n = self.dma_count[q]
                for j in range(min(n, NDMA_SEM)):
                    uses = (n - 1 - j) // NDMA_SEM + 1
                    self._wait(s, self.dma_sems[q][j], 16 * uses)
        self.ops = []
        self.last_writer = {}
        self.readers = {}


class Ring:
    def __init__(self, items):
        self.items = items
        self.i = 0

    def next(self):
        it = self.items[self.i % len(self.items)]
        self.i += 1
        return it


SEG = {}
_o = 0
for _n, _w in (("z", DI), ("xbc", CD), ("dt", 32), ("q", D), ("k", D), ("v", D), ("gs", D), ("ga", D)):
    SEG[_n] = (_o, _w)
    _o += _w
WIN_BLOCKS = []
for _n in ("xbc", "dt", "z", "q", "k", "v", "gs", "ga"):
    o0, w0 = SEG[_n]
    nb = (w0 + 511) // 512
    for b in range(nb):
        WIN_BLOCKS.append((_n, b, o0 + b * 512, min(512, w0 - b * 512)))
NWB = len(WIN_BLOCKS)


def build_program(Tp, NS, Ts, past):
    nc = bass.Bass("TRN2", target_bir_lowering=False)
    dram_in = lambda n, sh: nc.dram_tensor(n, sh, F32, kind="ExternalInput").ap()
    dram_out = lambda n, sh: nc.dram_tensor(n, sh, F32, kind="ExternalOutput").ap()
    xp = dram_in("xp", [Tp, D])
    xs = dram_in("xs", [NS, Ts, D])
    ck = dram_in("ck", [NS, past, D])
    cv = dram_in("cv", [NS, past, D])
    sssm = dram_in("sssm", [NS, 32, 64, 128])
    sconv = dram_in("sconv", [NS, 3, CD])
    sffn = dram_in("sffn", [NS, 2, DFF])
    norm_mix_w = dram_in("norm_mix_w", [1, D])
    w_in = dram_in("w_in", [D, INW])
    ssm_conv_w = dram_in("ssm_conv_w", [4, CD])
    ssm_conv_b = dram_in("ssm_conv_b", [1, CD])
    ssm_dt_bias = dram_in("ssm_dt_bias", [1, 32])
    ssm_a_log = dram_in("ssm_a_log", [1, 32])
    ssm_d = dram_in("ssm_d", [1, 32])
    ssm_norm_w = dram_in("ssm_norm_w", [1, DI])
    q_norm_w = dram_in("q_norm_w", [1, 64])
    k_norm_w = dram_in("k_norm_w", [1, 64])
    lam_in = dram_in("lam_in", [4, 64])
    subln_w = dram_in("subln_w", [128, 1])
    w_bs = dram_in("w_bs", [DI, D])
    w_ba = dram_in("w_ba", [D, D])
    w_out = dram_in("w_out", [D, D])
    norm_ffn_w = dram_in("norm_ffn_w", [1, D])
    w_up = dram_in("w_up", [D, 2 * DFF])
    ffn_conv_w = dram_in("ffn_conv_w", [3, DFF])
    ffn_conv_b = dram_in("ffn_conv_b", [1, DFF])
    w_down = dram_in("w_down", [DFF, D])

    y_p = dram_out("y_p", [Tp, D]); k_p = dram_out("k_p", [Tp, D]); v_p = dram_out("v_p", [Tp, D])
    ssm_p = dram_out("ssm_p", [32 * 64, 128]); conv_p = dram_out("conv_p", [3, CD]); ffn_p = dram_out("ffn_p", [2, DFF])
    y_s = dram_out("y_s", [NS, Ts, D]); k_s = dram_out("k_s", [NS, Ts, D]); v_s = dram_out("v_s", [NS, Ts, D])
    ssm_s = dram_out("ssm_s", [NS, 32 * 64, 128]); conv_s = dram_out("conv_s", [NS, 3, CD]); ffn_s = dram_out("ffn_s", [NS, 2, DFF])

    win_b = nc.dram_tensor("win_b", [NWB, 128, 8, 512], BF16).ap()
    wbs_b = nc.dram_tensor("wbs_b", [8, 128, 16, 128], BF16).ap()
    wba_b = nc.dram_tensor("wba_b", [128, 8, D], BF16).ap()
    wout_b = nc.dram_tensor("wout_b", [128, 8, D], BF16).ap()
    wup_b = nc.dram_tensor("wup_b", [11, 128, 8, 512], BF16).ap()
    wdn_b = nc.dram_tensor("wdn_b", [128, 22, D], BF16).ap()

    seqs = [dict(T=Tp, P=128, x=xp, past=0, y=y_p, k=k_p, v=v_p, ssm=ssm_p, conv=conv_p, ffn=ffn_p, idx=None)]
    for i in range(NS):
        seqs.append(dict(T=Ts, P=64, x=xs[i], past=past, y=y_s[i], k=k_s[i], v=v_s[i], ssm=ssm_s[i], conv=conv_s[i],
                         ffn=ffn_s[i], idx=i))
    for si, sq in enumerate(seqs):
        Tk = sq["past"] + sq["T"]
        sq["Tk"] = Tk
        sq["QT"] = nc.dram_tensor(f"QT{si}", [D, sq["T"]], BF16).ap()
        sq["KT"] = nc.dram_tensor(f"KT{si}", [D, Tk], BF16).ap()
        sq["V"] = nc.dram_tensor(f"V{si}", [Tk, D], BF16).ap()
        import os as _os
        _dbg = _os.environ.get("KDBG", "0") == "1"
        sq["OT"] = nc.dram_tensor(f"OT{si}", [D, sq["T"]], BF16, **({"kind": "ExternalOutput"} if _dbg else {})).ap()
        sq["MS"] = nc.dram_tensor(f"MS{si}", [D, sq["T"]], F32, **({"kind": "ExternalOutput"} if _dbg else {})).ap()
        sq["GA"] = nc.dram_tensor(f"GA{si}", [D, sq["T"]], F32, **({"kind": "ExternalOutput"} if _dbg else {})).ap()

    with ExitStack() as gst:
        S = Sched(nc, gst)
        gsb = lambda n, sh, dt=F32: gst.enter_context(nc.sbuf_tensor(n, sh, dt))

        ident_bf = gsb("ident_bf", [128, 128], BF16)
        ident_f = gsb("ident_f", [128, 128])
        ones_bf = gsb("ones_bf", [128, 128], BF16)
        ones_f = gsb("ones_f", [128, 128])
        U = gsb("U", [128, 128]); SU = gsb("SU", [128, 128]); BDm = gsb("BDm", [128, 128])
        IA = gsb("IA", [128, 128]); IB = gsb("IB", [128, 128])
        tmpc = gsb("tmpc", [128, 128])
        S.pool(nc.gpsimd.memset, tmpc[:], 0.0, w=["tmpc"])
        S.pool(nc.gpsimd.affine_select, tmpc[:], tmpc[:], [[-1, 128]], ALU.not_equal, 1.0, base=0, channel_multiplier=1,
               r=["tmpc"], w=["tmpc"])
        S.dve(nc.vector.tensor_copy, ident_bf[:], tmpc[:], r=["tmpc"], w=["ident_bf"])
        S.dve(nc.vector.tensor_copy, ident_f[:], tmpc[:], r=["tmpc"], w=["ident_f"])
        S.pool(nc.gpsimd.memset, ones_f[:], 1.0, w=["ones_f"])
        S.pool(nc.gpsimd.memset, ones_bf[:], 1.0, w=["ones_bf"])
        S.pool(nc.gpsimd.memset, U[:], 1.0, w=["U"])
        S.pool(nc.gpsimd.affine_select, U[:], U[:], [[1, 128]], ALU.is_ge, 0.0, base=0, channel_multiplier=-1, r=["U"], w=["U"])
        S.pool(nc.gpsimd.memset, U[0:64, 64:128], 0.0, r=["U"], w=["U"])
        S.pool(nc.gpsimd.memset, SU[:], 1.0, w=["SU"])
        S.pool(nc.gpsimd.affine_select, SU[:], SU[:], [[-1, 128]], ALU.is_gt, 0.0, base=0, channel_multiplier=1, r=["SU"], w=["SU"])
        S.pool(nc.gpsimd.memset, SU[64:128, 0:64], 0.0, r=["SU"], w=["SU"])
        SU_bf = gsb("SU_bf", [128, 128], BF16)
        S.dve(nc.vector.tensor_copy, SU_bf[:], SU[:], r=["SU"], w=["SU_bf"])
        S.pool(nc.gpsimd.memset, BDm[:], 0.0, w=["BDm"])
        S.pool(nc.gpsimd.memset, BDm[0:64, 0:64], 1.0, r=["BDm"], w=["BDm"])
        S.pool(nc.gpsimd.memset, BDm[64:128, 64:128], 1.0, r=["BDm"], w=["BDm"])
        S.pool(nc.gpsimd.memset, IA[:], 0.0, w=["IA"])
        S.pool(nc.gpsimd.memset, IA[0:64, :], 1.0, r=["IA"], w=["IA"])
        S.pool(nc.gpsimd.memset, IB[:], 0.0, w=["IB"])
        S.pool(nc.gpsimd.memset, IB[64:128, :], 1.0, r=["IB"], w=["IB"])

        def bc_load(name, src_row, n, stk=None):
            t = (stk or gst).enter_context(nc.sbuf_tensor(name, [128, n], F32))
            S.dma("sp", t[:], src_row.partition_broadcast(128), w=[name])
            return t

        qnw_bc = bc_load("qnw_bc", q_norm_w[0], 64)
        knw_bc = bc_load("knw_bc", k_norm_w[0], 64)
        dtb_bc = bc_load("dtb_bc", ssm_dt_bias[0], 32)
        alog_bc = bc_load("alog_bc", ssm_a_log[0], 32)
        dsk_bc = bc_load("dsk_bc", ssm_d[0], 32)
        A_bc = gsb("A_bc", [128, 32])
        S.act(nc.scalar.activation, A_bc[:], alog_bc[:], AF.Exp, r=["alog_bc"], w=["A_bc"])
        S.dve(nc.vector.tensor_scalar, A_bc[:], A_bc[:], -1.0, None, ALU.mult, r=["A_bc"], w=["A_bc"])
        scw = gsb("scw", [128, 24, 4]); scb = gsb("scb", [128, 24])
        fcw = gsb("fcw", [128, 22, 3]); fcb = gsb("fcb", [128, 22])
        v_scw = ssm_conv_w.rearrange("j (c p) -> p c j", p=128)
        for c in range(24):
            S.dma("sp", scw[:, c, :], v_scw[:, c, :], w=["scw"], allow_slow_non_contiguous=True)
        S.dma("sp", scb[:], ssm_conv_b[0].rearrange("(c p) -> p c", p=128), w=["scb"], allow_slow_non_contiguous=True)
        v_fcw = ffn_conv_w.rearrange("j (c p) -> p c j", p=128)
        for c in range(22):
            S.dma("sp", fcw[:, c, :], v_fcw[:, c, :], w=["fcw"], allow_slow_non_contiguous=True)
        S.dma("sp", fcb[:], ffn_conv_b[0].rearrange("(c p) -> p c", p=128), w=["fcb"], allow_slow_non_contiguous=True)
        lam_bc = gsb("lam_bc", [128, 4, 64])
        for i in range(4):
            S.dma("sp", lam_bc[:, i, :], lam_in[i].partition_broadcast(128), w=["lam_bc"])
        lamt = gsb("lamt", [128, 2, 64]); lams = gsb("lams", [128, 4])
        S.dve(nc.vector.tensor_tensor, lamt[:, 0, :], lam_bc[:, 0, :], lam_bc[:, 1, :], ALU.mult, r=["lam_bc"], w=["lamt"])
        S.dve(nc.vector.tensor_tensor, lamt[:, 1, :], lam_bc[:, 2, :], lam_bc[:, 3, :], ALU.mult, r=["lam_bc", "lamt"], w=["lamt"])
        S.dve(nc.vector.tensor_reduce, lams[:, 0:2], lamt[:], AX.X, ALU.add, r=["lamt"], w=["lams"])
        S.act(nc.scalar.activation, lams[:, 0:2], lams[:, 0:2], AF.Exp, r=["lams"], w=["lams"])
        lambda_init = 0.8 - 0.6 * math.exp(-0.3 * 0)
        S.dve(nc.vector.tensor_tensor, lams[:, 2:3], lams[:, 1:2], lams[:, 0:1], ALU.subtract, r=["lams"], w=["lams"])
        S.dve(nc.vector.tensor_scalar, lams[:, 2:3], lams[:, 2:3], -lambda_init, None, ALU.add, r=["lams"], w=["lams"])
        neglam = lams[:, 2:3]
        sublw = gsb("sublw", [128, 1])
        S.dma("sp", sublw[:], subln_w[:, :], w=["sublw"])
        S.dve(nc.vector.tensor_scalar, sublw[:], sublw[:], 1.0 - lambda_init, None, ALU.mult, r=["sublw"], w=["sublw"])

        v_win = w_in.rearrange("(k p) n -> p k n", p=128)
        for bi, (_n, _b, c0, wd) in enumerate(WIN_BLOCKS):
            S.dma("pool", win_b[bi, :, :, 0:wd], v_win[:, :, c0:c0 + wd], w=["win_b"])
        v_wbs = w_bs.rearrange("(k p) n -> p k n", p=128)
        for m in range(8):
            S.dma("pool", wbs_b[m], v_wbs[:, :, m * 128:(m + 1) * 128], w=["wbs_b"])
        for h2 in range(2):
            S.dma("pool", wba_b[:, :, h2 * 512:(h2 + 1) * 512], w_ba.rearrange("(k p) n -> p k n", p=128)[:, :, h2 * 512:(h2 + 1) * 512], w=["wba_b"])
            S.dma("pool", wout_b[:, :, h2 * 512:(h2 + 1) * 512], w_out.rearrange("(k p) n -> p k n", p=128)[:, :, h2 * 512:(h2 + 1) * 512], w=["wout_b"])
        v_wup = w_up.rearrange("(k p) n -> p k n", p=128)
        for j in range(11):
            S.dma("pool", wup_b[j, :, :, 0:256], v_wup[:, :, 256 * j:256 * j + 256], w=["wup_b"])
            S.dma("pool", wup_b[j, :, :, 256:512], v_wup[:, :, DFF + 256 * j:DFF + 256 * j + 256], w=["wup_b"])
        v_wdn = w_down.rearrange("(k p) n -> p k n", p=128)
        for h2 in range(2):
            for kk in range(2):
                S.dma("pool", wdn_b[:, kk * 11:(kk + 1) * 11, h2 * 512:(h2 + 1) * 512],
                      v_wdn[:, kk * 11:(kk + 1) * 11, h2 * 512:(h2 + 1) * 512], w=["wdn_b"])
        for sq in seqs:
            if sq["past"] > 0:
                i = sq["idx"]
                npc = sq["past"] // 512
                for j in range(npc):
                    S.dma("pool", sq["V"][j * 512:(j + 1) * 512, :], cv[i, j * 512:(j + 1) * 512, :], w=["Vscr%d" % i])

        invf = gsb("invf", [128, 8])
        inv_np = np.power(np.float32(ROPE_THETA), -(np.arange(8, dtype=np.float32) * np.float32(2.0) / np.float32(16))).astype(np.float32)
        for i in range(8):
            S.pool(nc.gpsimd.memset, invf[:, i:i + 1], float(inv_np[i]) / (2.0 * math.pi), r=["invf"], w=["invf"])

        def rope_tables(st, name, P, ntile, pos0, tmp_i, tmp_f, tmp_u):
            cs = st.enter_context(nc.sbuf_tensor(name + "_cs", [128, 2, ntile, 8], F32))
            pi_ = st.enter_context(nc.sbuf_tensor(name + "_pi", [128, ntile], I32))
            pf = st.enter_context(nc.sbuf_tensor(name + "_pf", [128, ntile], F32))
            u = tmp_u[:, 0:2 * ntile * 8].rearrange("p (a b c) -> p a b c", a=2, b=ntile)
            ui = tmp_i[:, 0:2 * ntile * 8].rearrange("p (a b c) -> p a b c", a=2, b=ntile)
            uf = tmp_f[:, 0:2 * ntile * 8].rearrange("p (a b c) -> p a b c", a=2, b=ntile)
            rn = name + "_rope"
            S.pool(nc.gpsimd.iota, pi_[:], [[P, ntile]], base=pos0, channel_multiplier=1, w=[rn])
            S.dve(nc.vector.tensor_copy, pf[:], pi_[:], r=[rn], w=[rn])
            S.dve(nc.vector.tensor_tensor, u[:, 1], pf[:, :, None].broadcast_to([128, ntile, 8]),
                  invf[:, None, :].broadcast_to([128, ntile, 8]), ALU.mult, r=[rn, "invf", "sgs"], w=[rn, "sgs"])
            S.dve(nc.vector.tensor_scalar, u[:, 0], u[:, 1], 0.25, None, ALU.add, r=[rn, "sgs"], w=[rn, "sgs"])
            S.dve(nc.vector.tensor_copy, ui, u, r=[rn, "sgs"], w=[rn, "sttmp"])
            S.dve(nc.vector.tensor_copy, uf, ui, r=[rn, "sttmp", "sgs"], w=[rn, "sttmp", "sgs"])
            S.dve(nc.vector.tensor_tensor, u, u, uf, ALU.subtract, r=[rn, "sttmp", "sgs"], w=[rn, "sgs"])
            S.dve(nc.vector.tensor_scalar, u, u, 0.49999, -0.49999, ALU.min, ALU.max, r=[rn, "sgs"], w=[rn, "sgs"])
            S.act(nc.scalar.activation, cs[:], u, AF.Sin, scale=2.0 * math.pi, r=[rn, "sgs"], w=[rn])
            return cs, rn

        import os as _os
        _stop = int(_os.environ.get('KSTOP', '9'))
        if _stop == 0:
            S.flush()
            return nc
        with ExitStack() as st:
            sb = lambda n, sh, dt=F32: st.enter_context(nc.sbuf_tensor(n, sh, dt))
            psb = lambda n, sh, dt=F32: st.enter_context(nc.psum_tensor(n, sh, dt))
            nmw_bc = bc_load("nmw_bc", norm_mix_w[0], D, st)
            snw_bc = bc_load("snw_bc", ssm_norm_w[0], DI, st)
            GEN = Ring([(psb(f"psg{i}", [128, 512]), f"psg{i}") for i in range(2)])
            pstr = psb("pstr", [128, 1024], BF16)
            pscb = psb("pscb", [128, 512])
            PSB = Ring([(pscb, "pscb")])
            psdt = psb("psdt", [128, 512])
            psseg = psb("psseg", [128, 1024])
            psyi = psb("psyi", [128, 512])
            WR = Ring([(sb(f"wr{i}", [128, 8, 512], BF16), f"wr{i}") for i in range(2)])
            WBS = Ring([(sb(f"wbs{i}", [128, 16, 128], BF16), f"wbs{i}") for i in range(2)])
            XT = Ring([(sb(f"xt{i}", [128, D]), f"xt{i}") for i in range(2)])
            xnb = sb("xnb", [128, D], BF16)
            XNT = Ring([(sb(f"xnT{i}", [128, 8, 256], BF16), f"xnT{i}") for i in range(2)])
            PRE = Ring([(sb(f"pre{i}", [128, 3 + 256]), f"pre{i}") for i in range(4)])
            CTMP = Ring([(sb(f"cvt{i}", [128, 256]), f"cvt{i}") for i in range(2)])
            halo = sb("halo", [128, 24, 3])
            xbcT = sb("xbcT", [128, 24, 256], BF16)
            sz = sb("sz", [128, 2, DI])
            sgs = sb("sgs", [128, 8, 256])
            small = sb("small", [128, 16, 32])
            ssq = sb("ssq", [128, 16])
            junk = sb("junk", [128, D], BF16)
            XS = Ring([(sb(f"xs{i}", [128, 512], BF16), f"xs{i}") for i in range(3)])
            XDT = Ring([(sb(f"xdt{i}", [128, 512], BF16), f"xdt{i}") for i in range(3)])
            XDE = Ring([(sb(f"xde{i}", [128, 512], BF16), f"xde{i}") for i in range(3)])
            BTOK = Ring([(sb(f"btok{i}", [128, 4, 128], BF16), f"btok{i}") for i in range(2)])
            RG = Ring([((sb(f"rgh{i}", [128, 8, 128], BF16), sb(f"rgl{i}", [128, 8, 128], BF16)), f"rg{i}") for i in range(1)])
            small2 = sb("small2", [128, 4, 32]); smallb = sb("smallb", [128, 2, 32], BF16)
            DEC = Ring([(sb(f"dec{i}", [128, 8, 128]), f"dec{i}") for i in range(1)])
            MT = Ring([(sb(f"mt{i}", [128, 8, 128], BF16), f"mt{i}") for i in range(3)])
            CBM = Ring([(sb(f"cbm{i}", [128, 4, 128]), f"cbm{i}") for i in range(2)])
            CT = Ring([((sb(f"cta{i}", [128, 4, 128], BF16), sb(f"ctb{i}", [128, 4, 128], BF16)), (f"cta{i}", f"ctb{i}")) for i in range(2)])
            Sst = sb("Sst", [128, DI])
            s0bf = sb("s0bf", [128, DI], BF16); s1bf = sb("s1bf", [128, DI], BF16)
            YB = Ring([(sb(f"yb{i}", [128, 512]), f"yb{i}") for i in range(2)])
            dsk = sb("dsk", [128, 512])
            stmp = sb("stmp", [128, 512])
            YNB = Ring([(sb(f"ynb{i}", [128, 512], BF16), f"ynb{i}") for i in range(2)])
            ynT = sb("ynT", [128, 16, 256], BF16)
            STG = Ring([(sb(f"stg{i}", [128, 512]), f"stg{i}") for i in range(2)])
            STGB = Ring([(sb(f"stgb{i}", [128, 512], BF16), f"stgb{i}") for i in range(4)])
            QF = Ring([(sb(f"qf{i}", [128, 8, 64]), f"qf{i}") for i in range(2)])
            QSQ = Ring([(sb(f"qsq{i}", [128, 8, 64]), f"qsq{i}") for i in range(2)])
            SSK = Ring([0, 1, 2])
            ssqk = sb("ssqk", [128, 3, 8])
            rt = sb("rt", [128, 4, 8, 8])
            sttmp = sb("sttmp", [128, 16, 128])
            _stf = sttmp[:].rearrange("p a b -> p (a b)")
            rtmp_i = _stf[:, 0:1024].bitcast(I32); rtmp_f = _stf[:, 1024:2048]
            rtmp_u = sgs[:].rearrange("p a b -> p (a b)")
            SSTN = ["Sst0", "Sst1", "Sst2", "Sst3"]
            HALON = ["halo%d" % c for c in range(24)]
            XBCN = ["xbcT%d" % c for c in range(24)]
            for (_ca, _cb), (_can, _cbn) in CT.items:
                S.pool(nc.gpsimd.memset, _ca[:], 0.0, w=[_can])
                S.pool(nc.gpsimd.memset, _cb[:], 0.0, w=[_cbn])

            def rstd_from_ss(ap, n, res):
                S.act(nc.scalar.activation, ap, ap, AF.Ln, scale=1.0 / n, bias=EPS, r=res, w=res)
                S.act(nc.scalar.activation, ap, ap, AF.Exp, scale=-0.5, r=res, w=res)

            for si, sq in enumerate(seqs):
                T, P = sq["T"], sq["P"]
                NT = 2 if P == 128 else 1
                GW = P * NT
                NG = T // GW
                assert NG * GW == T
                nch = P // 64
                ntile = T // P
                pastn = sq["past"]
                cs, rn = rope_tables(st, f"rp{si}", P, ntile, pastn, rtmp_i, rtmp_f, rtmp_u)
                if pastn == 0:
                    S.pool(nc.gpsimd.memset, halo[:], 0.0, w=HALON)
                    S.pool(nc.gpsimd.memset, Sst[:], 0.0, w=SSTN)
                    S.pool(nc.gpsimd.memset, s0bf[:], 0.0, w=["s0bf0", "s0bf1", "s0bf2", "s0bf3"])
                else:
                    i = sq["idx"]
                    vsc = sconv[i].rearrange("j (c p) -> p c j", p=128)
                    for c in range(24):
                        S.dma("sp", halo[:, c, :], vsc[:, c, :], w=["halo%d" % c], allow_slow_non_contiguous=True)
                    S.dma("sp", sttmp[:], sssm[i].rearrange("(c h) p n -> (h p) c n", h=2), w=["sttmp"])
                    for c in range(16):
                        ps, pn = GEN.next()
                        S.pe(nc.tensor.transpose, ps[:, 0:128], sttmp[:, c, :], ident_f[:], r=["sttmp", "ident_f"], w=[pn])
                        S.act(nc.scalar.copy, Sst[:, c * 128:(c + 1) * 128], ps[:, 0:128], r=[pn], w=["Sst%d" % (c // 4)])
                    S.dve(nc.vector.tensor_copy, s0bf[:], Sst[:], r=SSTN, w=["s0bf0", "s0bf1", "s0bf2", "s0bf3"])
                    for j in range(pastn // 128):
                        xt, xn_ = XT.next()
                        S.dma("sp", xt[:], ck[i, j * 128:(j + 1) * 128, :], w=[xn_])
                        S.dve(nc.vector.tensor_copy, xnb[:], xt[:], r=[xn_], w=["xnb"])
                        for k in range(8):
                            S.pe(nc.tensor.transpose, pstr[:, k * 128:(k + 1) * 128], xnb[:, k * 128:(k + 1) * 128], ident_bf[:],
                                 r=["xnb", "ident_bf"], w=["pstr"])
                        for hh in range(2):
                            stb, stbn = STGB.next()
                            S.act(nc.scalar.copy, stb[:], pstr[:, hh * 512:(hh + 1) * 512], r=["pstr"], w=[stbn])
                            S.dma("sp", sq["KT"].rearrange("(c p) t -> p c t", p=128)[:, hh * 4:(hh + 1) * 4, j * 128:(j + 1) * 128],
                                  stb[:].rearrange("p (c t) -> p c t", c=4), r=[stbn], w=["KTscr%d" % si])

                for g in range(NG):
                    t0 = g * GW
                    xnT, xnTn = XNT.next()
                    for ti in range(NT):
                        r0 = t0 + ti * P
                        xt, xn_ = XT.next()
                        S.dma("sp", xt[:P], sq["x"][r0:r0 + P, :], w=[xn_])
                        S.dve(nc.vector.memset, ssq[:P, 0:1], 0.0, w=["ssq"])
                        S.act(nc.scalar.activation, junk[:P], xt[:P], AF.Square, accum_out=ssq[:P, 0:1], r=[xn_, "ssq"], w=["junk", "ssq"])
                        rstd_from_ss(ssq[:P, 0:1], D, ["ssq"])
                        S.dve(nc.vector.scalar_tensor_tensor, xnb[:P], xt[:P], ssq[:P, 0:1], nmw_bc[:P], ALU.mult, ALU.mult,
                              r=[xn_, "ssq", "nmw_bc"], w=["xnb"])
                        for k in range(8):
                            S.pe(nc.tensor.transpose, pstr[:, k * P:(k + 1) * P], xnb[:P, k * 128:(k + 1) * 128], ident_bf[:P, :P],
                                 r=["xnb", "ident_bf"], w=["pstr"])
                        S.act(nc.scalar.copy, xnT[:, :, ti * P:(ti + 1) * P], pstr[:, 0:8 * P].rearrange("p (k t) -> p k t", k=8),
                              r=["pstr"], w=[xnTn])
                    def qk_s0(it):
                        qf_, qfn_, qs_, qsn_, ss_ = it["qf"], it["qfn"], it["qsq"], it["qsn"], it["ss"]
                        ssa = ssqk[:P, ss_, :]
                        ssn = "ssqk%d" % ss_
                        S.act(nc.scalar.copy, qf_[:P], it["ps"][:P, :].rearrange("p (h d) -> p h d", h=8), r=[it["pn"]], w=[qfn_])
                        S.pool(nc.gpsimd.tensor_tensor, qs_[:P], qf_[:P], qf_[:P], ALU.mult, r=[qfn_], w=[qsn_])
                        S.dve(nc.vector.tensor_reduce, ssa, qs_[:P], AX.X, ALU.add, r=[qsn_], w=[ssn])
                        rstd_from_ss(ssa, 64, [ssn])

                    def qk_s1(it):
                        seg_, b_, r0_ = it["seg"], it["b"], it["r0"]
                        qf_, qfn_, qs_, qsn_, ss_ = it["qf"], it["qfn"], it["qsq"], it["qsn"], it["ss"]
                        ssn = "ssqk%d" % ss_
                        wbc = qnw_bc if seg_ == "q" else knw_bc
                        S.dve(nc.vector.tensor_tensor, qs_[:P], qf_[:P], ssqk[:P, ss_, :, None].broadcast_to([P, 8, 64]), ALU.mult,
                              r=[qfn_, ssn, qsn_], w=[qsn_])
                        S.dve(nc.vector.tensor_tensor, qf_[:P], qs_[:P], wbc[:P, None, :].broadcast_to([P, 8, 64]), ALU.mult,
                              r=[qsn_, qfn_, "qnw_bc", "knw_bc"], w=[qfn_])
                        tidx = (r0_ // P)
                        cosb = cs[:P, 0, tidx, None, :].broadcast_to([P, 8, 8])
                        sinb = cs[:P, 1, tidx, None, :].broadcast_to([P, 8, 8])
                        S.dve(nc.vector.tensor_tensor, rt[:P, 0], qf_[:P, :, 0:8], cosb, ALU.mult, r=[qfn_, rn], w=["rt0"])
                        S.dve(nc.vector.tensor_tensor, rt[:P, 1], qf_[:P, :, 8:16], sinb, ALU.mult, r=[qfn_, rn], w=["rt1"])
                        S.dve(nc.vector.tensor_tensor, rt[:P, 2], qf_[:P, :, 8:16], cosb, ALU.mult, r=[qfn_, rn], w=["rt2"])
                        S.dve(nc.vector.tensor_tensor, rt[:P, 3], qf_[:P, :, 0:8], sinb, ALU.mult, r=[qfn_, rn], w=["rt3"])
                        S.dve(nc.vector.tensor_tensor, qf_[:P, :, 0:8], rt[:P, 0], rt[:P, 1], ALU.subtract, r=["rt0", "rt1", qfn_], w=[qfn_])
                        S.dve(nc.vector.tensor_tensor, qf_[:P, :, 8:16], rt[:P, 2], rt[:P, 3], ALU.add, r=["rt2", "rt3", qfn_], w=[qfn_])
                        stb, stbn = STGB.next()
                        if seg_ == "k":
                            stg, stgn = STG.next()
                            S.pool(nc.gpsimd.tensor_copy, stg[:P], qf_[:P].rearrange("p h d -> p (h d)"), r=[qfn_], w=[stgn])
                            S.dma("sp", sq["k"][r0_:r0_ + P, b_ * 512:(b_ + 1) * 512], stg[:P], r=[stgn])
                            S.dve(nc.vector.tensor_copy, stb[:P], qf_[:P].rearrange("p h d -> p (h d)"), r=[qfn_], w=[stbn])
                        else:
                            S.dve(nc.vector.tensor_scalar, stb[:P], qf_[:P].rearrange("p h d -> p (h d)"), 0.125, None, ALU.mult,
                                  r=[qfn_], w=[stbn])
                        it["stb"] = (stb, stbn)

                    def qk_s2(it):
                        seg_, b_, r0_ = it["seg"], it["b"], it["r0"]
                        stb, stbn = it["stb"]
                        for j in range(4):
                            S.pe(nc.tensor.transpose, pstr[:, j * P:(j + 1) * P], stb[:P, j * 128:(j + 1) * 128], ident_bf[:P, :P],
                                 r=[stbn, "ident_bf"], w=["pstr"])
                        stb2, stbn2 = STGB.next()
                        S.act(nc.scalar.copy, stb2[:, 0:4 * P], pstr[:, 0:4 * P], r=["pstr"], w=[stbn2])
                        dst = sq["QT"] if seg_ == "q" else sq["KT"]
                        coff = r0_ if seg_ == "q" else pastn + r0_
                        S.dma("sp", dst.rearrange("(c p) t -> p c t", p=128)[:, b_ * 4:(b_ + 1) * 4, coff:coff + P],
                              stb2[:, 0:4 * P].rearrange("p (c t) -> p c t", c=4), r=[stbn2],
                              w=[("QTscr%d" if seg_ == "q" else "KTscr%d") % si])

                    qk_items = []

                    def qk_advance(item):
                        if item is not None:
                            qk_items.append(item)
                        n_ = len(qk_items)
                        if item is not None:
                            if n_ >= 3:
                                qk_s2(qk_items[n_ - 3])
                            if n_ >= 2:
                                qk_s1(qk_items[n_ - 2])
                            qk_s0(item)
                        else:
                            if n_ >= 2:
                                qk_s2(qk_items[n_ - 2])
                            if n_ >= 1:
                                qk_s1(qk_items[n_ - 1])
                                qk_s2(qk_items[n_ - 1])

                    def proc_block(bi, PSR):
                        seg, b, c0, wd = WIN_BLOCKS[bi]
                        wt, wn = WR.next()
                        S.dma("sp", wt[:], win_b[bi], r=["win_b"], w=[wn])
                        if seg == "xbc":
                            for cp in range(2):
                                cs_ = [b * 4 + cp * 2, b * 4 + cp * 2 + 1]
                                pss_ = []
                                for c in cs_:
                                    cc = c - b * 4
                                    ps, pn = PSR.next()
                                    for k in range(8):
                                        S.pe(nc.tensor.matmul, ps[:, 0:GW], wt[:, k, cc * 128:(cc + 1) * 128], xnT[:, k, 0:GW],
                                             start=(k == 0), stop=(k == 7), r=[wn, xnTn], w=[pn])
                                    pss_.append((ps, pn))
                                pres = []
                                for c, (ps, pn) in zip(cs_, pss_):
                                    pre, pren = PRE.next()
                                    ct_, ctn_ = CTMP.next()
                                    S.pool(nc.gpsimd.tensor_copy, pre[:, 0:3], halo[:, c, :], r=["halo%d" % c], w=[pren])
                                    S.act(nc.scalar.copy, pre[:, 3:3 + GW], ps[:, 0:GW], r=[pn], w=[pren])
                                    S.pool(nc.gpsimd.tensor_copy, halo[:, c, :], pre[:, GW:GW + 3], r=[pren], w=["halo%d" % c])
                                    pres.append((pre, pren, ct_, ctn_))
                                for j in range(4):
                                    for c, (pre, pren, ct_, ctn_) in zip(cs_, pres):
                                        if j == 0:
                                            S.dve(nc.vector.tensor_scalar, ct_[:, 0:GW], pre[:, 0:GW], scw[:, c, 0:1], scb[:, c:c + 1], ALU.mult, ALU.add,
                                                  r=[pren, "scw", "scb"], w=[ctn_])
                                        else:
                                            S.dve(nc.vector.scalar_tensor_tensor, ct_[:, 0:GW], pre[:, j:j + GW], scw[:, c, j:j + 1], ct_[:, 0:GW],
                                                  ALU.mult, ALU.add, r=[pren, "scw", ctn_], w=[ctn_])
                                for c, (pre, pren, ct_, ctn_) in zip(cs_, pres):
                                    S.act(nc.scalar.activation, xbcT[:, c, 0:GW], ct_[:, 0:GW], AF.Silu, r=[ctn_], w=["xbcT%d" % c])
                        elif seg == "dt":
                            for ti in range(NT):
                                for k in range(8):
                                    S.pe(nc.tensor.matmul, psdt[:P, ti * 32:(ti + 1) * 32], xnT[:, k, ti * P:(ti + 1) * P], wt[:, k, 0:32],
                                         start=(k == 0), stop=(k == 7), r=[xnTn, wn], w=["psdt"])
                            for ti in range(NT):
                                S.dve(nc.vector.tensor_tensor, small[:P, ti, :], psdt[:P, ti * 32:(ti + 1) * 32], dtb_bc[:P], ALU.add,
                                      r=["psdt", "dtb_bc"], w=["sm%d" % ti])
                                S.act(nc.scalar.activation, small[:P, ti, :], small[:P, ti, :], AF.Exp, r=["sm%d" % ti], w=["sm%d" % ti])
                                S.act(nc.scalar.activation, small[:P, ti, :], small[:P, ti, :], AF.Ln, bias=1.0, r=["sm%d" % ti], w=["sm%d" % ti])
                        elif seg == "z":
                            for ti in range(NT):
                                ps, pn = PSR.next()
                                for k in range(8):
                                    S.pe(nc.tensor.matmul, ps[:P, :], xnT[:, k, ti * P:(ti + 1) * P], wt[:, k, :],
                                         start=(k == 0), stop=(k == 7), r=[xnTn, wn], w=[pn])
                                S.act(nc.scalar.activation, sz[:P, ti, b * 512:(b + 1) * 512], ps[:P, :], AF.Silu, r=[pn], w=["sz"])
                        elif seg in ("q", "k", "v"):
                            for ti in range(NT):
                                r0 = t0 + ti * P
                                ps, pn = PSR.next()
                                for k in range(8):
                                    S.pe(nc.tensor.matmul, ps[:P, :], xnT[:, k, ti * P:(ti + 1) * P], wt[:, k, :],
                                         start=(k == 0), stop=(k == 7), r=[xnTn, wn], w=[pn])
                                if seg == "v":
                                    if b == 0 and ti == 0:
                                        qk_advance(None)
                                    stg, stgn = STG.next()
                                    S.act(nc.scalar.copy, stg[:P], ps[:P, :], r=[pn], w=[stgn])
                                    S.dma("sp", sq["v"][r0:r0 + P, b * 512:(b + 1) * 512], stg[:P], r=[stgn])
                                    stb, stbn = STGB.next()
                                    S.dve(nc.vector.tensor_copy, stb[:P], stg[:P], r=[stgn], w=[stbn])
                                    S.dma("sp", sq["V"][pastn + r0:pastn + r0 + P, b * 512:(b + 1) * 512], stb[:P], r=[stbn], w=["Vscr%s" % si])
                                    continue
                                qfb, qfn = QF.next()
                                qsb, qsn = QSQ.next()
                                sslot = SSK.next()
                                item = dict(seg=seg, b=b, ti=ti, r0=r0, ps=ps, pn=pn, qf=qfb, qfn=qfn, qsq=qsb, qsn=qsn, ss=sslot)
                                qk_advance(item)
                        else:
                            for cc in range(4):
                                m = b * 4 + cc
                                ps, pn = PSR.next()
                                for k in range(8):
                                    S.pe(nc.tensor.matmul, ps[:, 0:GW], wt[:, k, cc * 128:(cc + 1) * 128], xnT[:, k, 0:GW],
                                         start=(k == 0), stop=(k == 7), r=[wn, xnTn], w=[pn])
                                if seg == "gs":
                                    S.act(nc.scalar.activation, sgs[:, m, 0:GW], ps[:, 0:GW], AF.Sigmoid, r=[pn], w=["sgs"])
                                else:
                                    stg, stgn = STG.next()
                                    S.act(nc.scalar.activation, stg[:, 0:GW], ps[:, 0:GW], AF.Sigmoid, r=[pn], w=[stgn])
                                    S.dma("sp", sq["GA"][m * 128:(m + 1) * 128, t0:t0 + GW], stg[:, 0:GW], r=[stgn], w=["GAscr%d" % si])

                    BLK_A = [i_ for i_, w_ in enumerate(WIN_BLOCKS) if w_[0] in ("xbc", "dt", "z", "gs")]
                    BLK_B = [i_ for i_, w_ in enumerate(WIN_BLOCKS) if w_[0] in ("q", "k", "v", "ga")]
                    for bi_ in BLK_A:
                        proc_block(bi_, GEN)
                    wbq = []
                    for m in range(2):
                        wb, wbn = WBS.next()
                        S.dma("sp", wb[:], wbs_b[m], r=["wbs_b"], w=[wbn])
                        wbq.append((wb, wbn))
                    def prologue(ti):
                        tc0 = ti * P
                        dt = small[:P, ti, :]
                        dtA = small[:P, 2 + ti, :]
                        expA = small[:P, 4 + ti, :]
                        acs = small[:P, 6 + ti, :]
                        toend = small[:P, 8 + ti, :]
                        dte = small[:P, 10 + ti, :]
                        dA = small[:, 12 + ti, :]
                        dB = small[:, 14 + ti, :]
                        n = lambda k: "sm%d" % k
                        S.dve(nc.vector.tensor_tensor, dtA, dt, A_bc[:P], ALU.mult, r=[n(ti), "A_bc"], w=[n(2 + ti)])
                        S.dve(nc.vector.tensor_copy, smallb[:P, ti, :], dtA, r=[n(2 + ti)], w=["smb%d" % ti])
                        S.dve(nc.vector.tensor_copy, small2[:P, ti, :], smallb[:P, ti, :], r=["smb%d" % ti], w=["smh%d" % ti])
                        S.dve(nc.vector.tensor_tensor, small2[:P, 2 + ti, :], dtA, small2[:P, ti, :], ALU.subtract, r=[n(2 + ti), "smh%d" % ti], w=["sml%d" % ti])
                        S.pe(nc.tensor.matmul, psdt[:P, 64:96], U[:P, :P], dtA, start=True, stop=True, r=["U", n(2 + ti)], w=["psdt"])
                        S.pe(nc.tensor.matmul, psdt[:P, 96:128], BDm[:P, :P], dtA, start=True, stop=True, r=["BDm", n(2 + ti)], w=["psdt"])
                        S.pe(nc.tensor.matmul, psdt[:, 128:160], IA[:P, :], dtA, start=True, stop=True, r=["IA", n(2 + ti)], w=["psdt"])
                        if nch == 2:
                            S.pe(nc.tensor.matmul, psdt[:, 160:192], IB[:P, :], dtA, start=True, stop=True, r=["IB", n(2 + ti)], w=["psdt"])
                        S.act(nc.scalar.activation, expA, psdt[:P, 64:96], AF.Exp, r=["psdt"], w=[n(4 + ti)])
                        S.act(nc.scalar.copy, acs, psdt[:P, 64:96], r=["psdt"], w=[n(6 + ti)])
                        S.dve(nc.vector.tensor_tensor, toend, psdt[:P, 96:128], acs, ALU.subtract, r=["psdt", n(6 + ti)], w=[n(8 + ti)])
                        S.act(nc.scalar.activation, toend, toend, AF.Exp, r=[n(8 + ti)], w=[n(8 + ti)])
                        S.dve(nc.vector.tensor_tensor, dte, dt, toend, ALU.mult, r=[n(ti), n(8 + ti)], w=[n(10 + ti)])
                        S.act(nc.scalar.activation, dA, psdt[:, 128:160], AF.Exp, r=["psdt"], w=[n(12 + ti)])
                        if nch == 2:
                            S.act(nc.scalar.activation, dB, psdt[:, 160:192], AF.Exp, r=["psdt"], w=[n(14 + ti)])
                        btok, btokn = BTOK.next()
                        cbm, cbmn = CBM.next()
                        (cta, ctb), (ctan, ctbn) = CT.next()
                        for gg in range(4):
                            S.pe(nc.tensor.transpose, pstr[:P, gg * 128:(gg + 1) * 128], xbcT[:, 16 + gg, tc0:tc0 + P], ident_bf[:],
                                 r=["xbcT%d" % (16 + gg), "ident_bf"], w=["pstr"])
                        S.act(nc.scalar.copy, btok[:P], pstr[:P, 0:512].rearrange("p (g n) -> p g n", g=4), r=["pstr"], w=[btokn])
                        for gg in range(4):
                            S.pe(nc.tensor.matmul, pscb[:P, gg * P:(gg + 1) * P], xbcT[:, 16 + gg, tc0:tc0 + P], xbcT[:, 20 + gg, tc0:tc0 + P],
                                 start=True, stop=True, r=["xbcT%d" % (16 + gg), "xbcT%d" % (20 + gg)], w=["pscb"])
                        S.dve(nc.vector.tensor_tensor, cbm[:P, :, :P], pscb[:P, 0:4 * P].rearrange("p (g l) -> p g l", g=4),
                              U[:P, None, :P].broadcast_to([P, 4, P]), ALU.mult, r=["pscb", "U"], w=[cbmn])
                        if nch == 2:
                            S.pool(nc.gpsimd.tensor_copy, cta[:, :, 0:64], xbcT[:, 20:24, tc0:tc0 + 64], r=XBCN[20:24], w=[ctan])
                            S.pool(nc.gpsimd.tensor_copy, ctb[:, :, 64:128], xbcT[:, 20:24, tc0 + 64:tc0 + 128], r=XBCN[20:24], w=[ctbn])
                        return dict(ti=ti, tc0=tc0, btok=btok, btokn=btokn, cbm=cbm, cbmn=cbmn, cta=cta, ctb=ctb, ctan=ctan, ctbn=ctbn)

                    def st0(it):
                        ti, gg = it["ti"], it["gg"]
                        if gg == 0:
                            PRO[ti] = prologue(ti)
                        pro = PRO[ti]
                        it["pro"] = pro
                        tc0 = pro["tc0"]
                        hs = slice(8 * gg, 8 * gg + 8)
                        n = lambda k: "sm%d" % k
                        for j in range(4):
                            S.pe(nc.tensor.transpose, pstr[:P, 512 + j * 128:512 + (j + 1) * 128], xbcT[:, 4 * gg + j, tc0:tc0 + P], ident_bf[:],
                                 r=["xbcT%d" % (4 * gg + j), "ident_bf"], w=["pstr"])
                        xs_, xsn = XS.next()
                        S.act(nc.scalar.copy, xs_[:P], pstr[:P, 512:1024], r=["pstr"], w=[xsn])
                        xdt, xdtn = XDT.next()
                        xde, xden = XDE.next()
                        S.dve(nc.vector.tensor_tensor, xdt[:P].rearrange("p (h d) -> p h d", h=8), xs_[:P].rearrange("p (h d) -> p h d", h=8),
                              small[:P, ti, hs, None].broadcast_to([P, 8, 64]), ALU.mult, r=[xsn, n(ti)], w=[xdtn])
                        S.pool(nc.gpsimd.tensor_tensor, xde[:P].rearrange("p (h d) -> p h d", h=8), xs_[:P].rearrange("p (h d) -> p h d", h=8),
                               small[:P, 10 + ti, hs, None].broadcast_to([P, 8, 64]), ALU.mult, r=[xsn, n(10 + ti)], w=[xden])
                        (rgh, rgl), rgn = RG.next()
                        S.dve(nc.vector.tensor_tensor, rgh[:P, :, :P], U[:P, None, :P].broadcast_to([P, 8, P]),
                              small2[:P, ti, hs, None].broadcast_to([P, 8, P]), ALU.mult, r=["U", "smh%d" % ti], w=[rgn + "h"])
                        S.dve(nc.vector.tensor_tensor, rgl[:P, :, :P], U[:P, None, :P].broadcast_to([P, 8, P]),
                              small2[:P, 2 + ti, hs, None].broadcast_to([P, 8, P]), ALU.mult, r=["U", "sml%d" % ti], w=[rgn + "l"])
                        it.update(xs=xs_, xsn=xsn, xdt=xdt, xdtn=xdtn, xde=xde, xden=xden, rgh=rgh, rgl=rgl, rgn=rgn)

                    def st0b(it):
                        ti, gg, pro = it["ti"], it["gg"], it["pro"]
                        rgh, rgl, rgn = it["rgh"], it["rgl"], it["rgn"]
                        nsub = (8 * P) // 512
                        hps = 512 // P
                        for j in range(nsub):
                            S.pe(nc.tensor.matmul, psseg[:P, j * 512:(j + 1) * 512], SU_bf[:P, :P],
                                 rgh[:P, j * hps:(j + 1) * hps, :P], start=True, stop=False, r=["SU_bf", rgn + "h"], w=["psseg"])
                            S.pe(nc.tensor.matmul, psseg[:P, j * 512:(j + 1) * 512], SU_bf[:P, :P],
                                 rgl[:P, j * hps:(j + 1) * hps, :P], start=False, stop=True, r=["SU_bf", rgn + "l"], w=["psseg"])
                        dec, decn = DEC.next()
                        S.act(nc.scalar.activation, dec[:P, :, :P], psseg[:P, 0:8 * P].rearrange("p (h l) -> p h l", h=8), AF.Exp,
                              r=["psseg"], w=[decn])
                        mt, mtn = MT.next()
                        S.dve(nc.vector.tensor_tensor, mt[:P, :, :P], dec[:P, :, :P], pro["cbm"][:P, gg, None, :P].broadcast_to([P, 8, P]), ALU.mult,
                              r=[decn, pro["cbmn"]], w=[mtn])
                        it.update(mt=mt, mtn=mtn)

                    def st1(it):
                        ti, gg, pro = it["ti"], it["gg"], it["pro"]
                        tc0 = pro["tc0"]
                        hs = slice(8 * gg, 8 * gg + 8)
                        fs = slice(512 * gg, 512 * gg + 512)
                        n = lambda k: "sm%d" % k
                        mt, mtn, xdt, xdtn, xde, xden = it["mt"], it["mtn"], it["xdt"], it["xdtn"], it["xde"], it["xden"]
                        btok, btokn = pro["btok"], pro["btokn"]
                        sn, s0n, s1n = "Sst%d" % gg, "s0bf%d" % gg, "s1bf%d" % gg
                        for h in range(8):
                            S.pe(nc.tensor.matmul, psyi[:P, h * 64:(h + 1) * 64], mt[:P, h, :P], xdt[:P, h * 64:(h + 1) * 64],
                                 start=True, stop=True, r=[mtn, xdtn], w=["psyi"])
                        pyo, pyon = GEN.next()
                        if nch == 2:
                            S.pe(nc.tensor.matmul, pyo[:P, :], pro["cta"][:, gg, :], s0bf[:, fs], start=True, stop=False,
                                 r=[pro["ctan"], s0n], w=[pyon])
                        else:
                            S.pe(nc.tensor.matmul, pyo[:P, :], xbcT[:, 20 + gg, tc0:tc0 + P], s0bf[:, fs], start=True, stop=True,
                                 r=["xbcT%d" % (20 + gg), s0n], w=[pyon])
                        pds, pdsn = GEN.next()
                        S.pe(nc.tensor.matmul, pds[:, :], btok[0:64, gg, :], xde[0:64, :], start=True, stop=True, r=[btokn, xden], w=[pdsn])
                        S.pool(nc.gpsimd.tensor_tensor, stmp[:].rearrange("p (h d) -> p h d", h=8), Sst[:, fs].rearrange("p (h d) -> p h d", h=8),
                               small[:, 12 + ti, hs, None].broadcast_to([128, 8, 64]), ALU.mult, r=[sn, n(12 + ti), "stmp"], w=["stmp"])
                        S.dve(nc.vector.tensor_tensor, Sst[:, fs], stmp[:], pds[:, :], ALU.add, r=["stmp", pdsn, sn], w=[sn])
                        if nch == 2:
                            S.act(nc.scalar.copy, s1bf[:, fs], Sst[:, fs], r=[sn], w=[s1n])
                        it.update(pyo=pyo, pyon=pyon, pds=pds, pdsn=pdsn)

                    def st1b(it):
                        ti, gg, pro = it["ti"], it["gg"], it["pro"]
                        hs = slice(8 * gg, 8 * gg + 8)
                        fs = slice(512 * gg, 512 * gg + 512)
                        n = lambda k: "sm%d" % k
                        xde, xden = it["xde"], it["xden"]
                        btok, btokn = pro["btok"], pro["btokn"]
                        sn, s0n, s1n = "Sst%d" % gg, "s0bf%d" % gg, "s1bf%d" % gg
                        pyo, pyon, pds, pdsn = it["pyo"], it["pyon"], it["pds"], it["pdsn"]
                        if nch == 2:
                            S.pe(nc.tensor.matmul, pyo[:P, :], pro["ctb"][:, gg, :], s1bf[:, fs], start=False, stop=True,
                                 r=[pro["ctbn"], s1n], w=[pyon])
                            S.pe(nc.tensor.matmul, pds[:, :], btok[64:128, gg, :], xde[64:128, :], start=True, stop=True,
                                 r=[btokn, xden], w=[pdsn])
                            S.pool(nc.gpsimd.tensor_tensor, stmp[:].rearrange("p (h d) -> p h d", h=8),
                                   Sst[:, fs].rearrange("p (h d) -> p h d", h=8),
                                   small[:, 14 + ti, hs, None].broadcast_to([128, 8, 64]), ALU.mult, r=[sn, n(14 + ti), "stmp"], w=["stmp"])
                            S.dve(nc.vector.tensor_tensor, Sst[:, fs], stmp[:], pds[:, :], ALU.add, r=["stmp", pdsn, sn], w=[sn])
                        S.act(nc.scalar.copy, s0bf[:, fs], Sst[:, fs], r=[sn], w=[s0n])
                        it.update(pyo=pyo, pyon=pyon)

                    def st2(it):
                        ti, gg = it["ti"], it["gg"]
                        hs = slice(8 * gg, 8 * gg + 8)
                        fs = slice(512 * gg, 512 * gg + 512)
                        n = lambda k: "sm%d" % k
                        pyo, pyon, xs_, xsn = it["pyo"], it["pyon"], it["xs"], it["xsn"]
                        yb, ybn = YB.next()
                        S.dve(nc.vector.tensor_tensor, yb[:P].rearrange("p (h d) -> p h d", h=8), pyo[:P, :].rearrange("p (h d) -> p h d", h=8),
                              small[:P, 4 + ti, hs, None].broadcast_to([P, 8, 64]), ALU.mult, r=[pyon, n(4 + ti)], w=[ybn])
                        S.dve(nc.vector.tensor_tensor, yb[:P], yb[:P], psyi[:P, :], ALU.add, r=[ybn, "psyi"], w=[ybn])
                        S.pool(nc.gpsimd.tensor_tensor, dsk[:P].rearrange("p (h d) -> p h d", h=8), xs_[:P].rearrange("p (h d) -> p h d", h=8),
                               dsk_bc[:P, hs, None].broadcast_to([P, 8, 64]), ALU.mult, r=[xsn, "dsk_bc"], w=["dsk"])
                        S.dve(nc.vector.tensor_tensor, yb[:P], yb[:P], dsk[:P], ALU.add, r=[ybn, "dsk"], w=[ybn])
                        S.dve(nc.vector.tensor_tensor, yb[:P], yb[:P], sz[:P, ti, fs], ALU.mult, r=[ybn, "sz"], w=[ybn])
                        S.dve(nc.vector.memset, ssq[:P, 1:2], 0.0, w=["ssq"])
                        S.act(nc.scalar.activation, junk[:P, 0:512], yb[:P], AF.Square, accum_out=ssq[:P, 1:2], r=[ybn, "ssq"], w=["junk", "ssq"])
                        rstd_from_ss(ssq[:P, 1:2], 512, ["ssq"])
                        ynb, ynbn = YNB.next()
                        S.dve(nc.vector.scalar_tensor_tensor, ynb[:P], yb[:P], ssq[:P, 1:2], snw_bc[:P, fs], ALU.mult, ALU.mult,
                              r=[ybn, "ssq", "snw_bc"], w=[ynbn])
                        it.update(ynb=ynb, ynbn=ynbn)

                    def st3(it):
                        gg, tc0 = it["gg"], it["pro"]["tc0"]
                        ynb, ynbn = it["ynb"], it["ynbn"]
                        for j in range(4):
                            S.pe(nc.tensor.transpose, pstr[:, j * P:(j + 1) * P], ynb[:P, j * 128:(j + 1) * 128], ident_bf[:P, :P],
                                 r=[ynbn, "ident_bf"], w=["pstr"])
                        S.act(nc.scalar.copy, ynT[:, 4 * gg:4 * gg + 4, tc0:tc0 + P], pstr[:, 0:4 * P].rearrange("p (c t) -> p c t", c=4),
                              r=["pstr"], w=["ynT"])

                    PRO = {}
                    items = [dict(ti=ti, gg=gg) for ti in range(NT) for gg in range(4)]
                    def run(fn, kk):
                        if 0 <= kk < len(items):
                            fn(items[kk])
                    for step in range(len(items) + 3):
                        run(st0, step)
                        run(st3, step - 3)
                        run(st2, step - 2)
                        run(st1, step - 1)
                        run(st0b, step)
                        run(st1b, step - 1)
                        if step < len(BLK_B):
                            proc_block(BLK_B[step], PSB)
                    for bi_ in BLK_B[len(items) + 3:]:
                        proc_block(bi_, PSB)
                    for m in range(8):
                        wb, wbn = wbq.pop(0)
                        ps, pn = GEN.next()
                        for k in range(16):
                            S.pe(nc.tensor.matmul, ps[:, 0:GW], wb[:, k, :], ynT[:, k, 0:GW], start=(k == 0), stop=(k == 15),
                                 r=[wbn, "ynT"], w=[pn])
                        if m + 2 < 8:
                            wb2, wbn2 = WBS.next()
                            S.dma("sp", wb2[:], wbs_b[m + 2], r=["wbs_b"], w=[wbn2])
                            wbq.append((wb2, wbn2))
                        stg, stgn = STG.next()
                        S.dve(nc.vector.tensor_tensor, stg[:, 0:GW], ps[:, 0:GW], sgs[:, m, 0:GW], ALU.mult, r=[pn, "sgs"], w=[stgn])
                        S.dma("sp", sq["MS"][m * 128:(m + 1) * 128, t0:t0 + GW], stg[:, 0:GW], r=[stgn], w=["MSscr%d" % si])
                vco = sq["conv"].rearrange("j (c p) -> p c j", p=128)
                for c in range(24):
                    S.dma("sp", vco[:, c, :], halo[:, c, :], r=["halo%d" % c], allow_slow_non_contiguous=True)
                for c in range(16):
                    ps, pn = GEN.next()
                    S.pe(nc.tensor.transpose, ps[:, 0:128], Sst[:, c * 128:(c + 1) * 128], ident_f[:],
                         r=["Sst%d" % (c // 4), "ident_f"], w=[pn])
                    S.act(nc.scalar.copy, sttmp[:, c, :], ps[:, 0:128], r=[pn], w=["sttmp"])
                S.dma("sp", sq["ssm"].rearrange("(c q) n -> q c n", q=128), sttmp[:], r=["sttmp"])
            S.flush()

        if _stop == 1:
            return nc
        with ExitStack() as st:
            sb = lambda n, sh, dt=F32: st.enter_context(nc.sbuf_tensor(n, sh, dt))
            psb = lambda n, sh, dt=F32: st.enter_context(nc.psum_tensor(n, sh, dt))
            LA = 1
            PSS = Ring([(psb(f"pss{i}", [128, 2, 512]), f"pss{i}") for i in range(2)])
            psl0 = psb("psl0", [128, 512])
            pso = [psb(f"pso{j}", [128, 512]) for j in range(2)]
            TKmax = max(s_["Tk"] for s_ in seqs)
            TQmax = max(s_["T"] for s_ in seqs)
            NKTmax = (TKmax + 127) // 128
            KTB = Ring([(sb(f"ktb{i}", [128, TKmax], BF16), f"ktb{i}") for i in range(2)])
            VB = Ring([(sb(f"vb{i}", [128, NKTmax, 128], BF16), f"vb{i}") for i in range(2)])
            QTB = Ring([(sb(f"qtb{i}", [128, TQmax], BF16), f"qtb{i}") for i in range(2)])
            PT = Ring([(sb(f"pt{i}", [128, 2, 512], BF16), f"pt{i}") for i in range(LA + 3)])
            ACC = Ring([(sb(f"acc{i}", [128, 2, 512]), f"acc{i}") for i in range(2)])
            rl = sb("rl", [128, 2, 512])
            ob = sb("ob", [128, 512]); ob2 = sb("ob2", [128, 512]); osq = sb("osq", [128, 512]); rs2 = sb("rs2", [128, 512])
            OST = Ring([(sb(f"ost{i}", [128, 512], BF16), f"ost{i}") for i in range(2)])
            iters = []
            for si, sq in enumerate(seqs):
                T, Tk, pastn = sq["T"], sq["Tk"], sq["past"]
                causal = (pastn == 0)
                nkt = (Tk + 127) // 128
                QG = min(512, T)
                for h in range(8):
                    for qg in range(T // QG):
                        q0 = qg * QG
                        kts = list(range(0, (q0 + QG) // 128)) if causal else list(range(nkt))
                        grp = dict(si=si, sq=sq, h=h, q0=q0, QG=QG, first=(qg == 0), T=T, Tk=Tk, pastn=pastn)
                        for kt in kts:
                            nk = min(128, Tk - kt * 128)
                            qs = max(q0, kt * 128) if causal else q0
                            iters.append(dict(grp=grp, kt=kt, nk=nk, qs=qs, nq=q0 + QG - qs, lo=qs - q0,
                                              diag=causal and (kt * 128 >= q0), first=(kt == 0), last=(kt == kts[-1])))
            cur = {}

            def load_head(grp):
                sq, si, h, T, Tk, pastn = grp["sq"], grp["si"], grp["h"], grp["T"], grp["Tk"], grp["pastn"]
                ktb, ktn = KTB.next(); vb, vbn = VB.next(); qtb, qtn = QTB.next()
                S.dma("sp", ktb[:, 0:Tk], sq["KT"][h * 128:(h + 1) * 128, :], w=[ktn])
                S.dma("sp", qtb[:, 0:T], sq["QT"][h * 128:(h + 1) * 128, :], w=[qtn])
                nfull = Tk // 128
                S.dma("sp", vb[:, 0:nfull, :], sq["V"][0:nfull * 128, h * 128:(h + 1) * 128].rearrange("(j p) e -> p j e", p=128), w=[vbn])
                if Tk % 128:
                    rem = Tk % 128
                    S.dma("sp", vb[0:rem, nfull, :], sq["V"][nfull * 128:Tk, h * 128:(h + 1) * 128], w=[vbn])
                grp["bufs"] = (ktb, ktn, vb, vbn, qtb, qtn)

            def front(it):
                grp = it["grp"]
                key = (grp["si"], grp["h"])
                if key not in cur:
                    load_head(grp)
                    cur[key] = grp["bufs"]
                grp["bufs"] = cur[key]
                ktb, ktn, vb, vbn, qtb, qtn = grp["bufs"]
                if "acc" not in grp:
                    grp["acc"] = ACC.next()
                acc, accn = grp["acc"]
                kt, nk, qs, nq, lo = it["kt"], it["nk"], it["qs"], it["nq"], it["lo"]
                pss, pssn = PSS.next()
                for j in range(2):
                    S.pe(nc.tensor.matmul, pss[:nk, j, 0:nq], ktb[64 * j:64 * j + 64, kt * 128:kt * 128 + nk],
                         qtb[64 * j:64 * j + 64, qs:qs + nq], start=True, stop=True, r=[ktn, qtn], w=[pssn])
                pt, ptn = PT.next()
                it["pt"] = (pt, ptn)
                S.act(nc.scalar.activation, pt[:nk, :, 0:nq], pss[:nk, :, 0:nq], AF.Exp, r=[pssn], w=[ptn])
                if it["diag"]:
                    S.pool(nc.gpsimd.memset, pt[64:128, :, 0:64], 0.0, r=[ptn], w=[ptn])
                if it["first"]:
                    S.dve(nc.vector.tensor_copy, acc[:nk, 1, lo:lo + nq], pt[:nk, 1, 0:nq], r=[ptn], w=[accn + "b"])
                else:
                    S.dve(nc.vector.tensor_tensor, acc[:nk, 1, lo:lo + nq], acc[:nk, 1, lo:lo + nq], pt[:nk, 1, 0:nq], ALU.add,
                          r=[ptn, accn + "b"], w=[accn + "b"])

            def back(it):
                grp = it["grp"]
                ktb, ktn, vb, vbn, qtb, qtn = grp["bufs"]
                kt, nk, nq, lo = it["kt"], it["nk"], it["nq"], it["lo"]
                pt, ptn = it["pt"]
                for j in range(2):
                    S.pe(nc.tensor.matmul, pso[j][:, lo:lo + nq], vb[:nk, kt, :], pt[:nk, j, 0:nq], start=it["first"], stop=it["last"],
                         r=[vbn, ptn], w=["pso%d" % j])
                S.pe(nc.tensor.matmul, psl0[:, lo:lo + nq], ones_bf[:nk, :], pt[:nk, 0, 0:nq], start=it["first"], stop=it["last"],
                     r=["ones_bf", ptn], w=["psl0"])
                if it["last"]:
                    finalize(grp)

            def finalize(grp):
                sq, si, h, q0, QG = grp["sq"], grp["si"], grp["h"], grp["q0"], grp["QG"]
                acc, accn = grp["acc"]
                psl, psln = PSS.next()
                S.pe(nc.tensor.matmul, psl[:, 1, 0:QG], ones_f[:], acc[:, 1, 0:QG], start=True, stop=True, r=["ones_f", accn + "b"], w=[psln])
                S.dve(nc.vector.reciprocal, rl[:, 0, 0:QG], psl0[:, 0:QG], r=["psl0"], w=["rl"])
                S.dve(nc.vector.reciprocal, rl[:, 1, 0:QG], psl[:, 1, 0:QG], r=[psln, "rl"], w=["rl"])
                S.dve(nc.vector.tensor_tensor, ob[:, 0:QG], pso[0][:, 0:QG], rl[:, 0, 0:QG], ALU.mult, r=["pso0", "rl"], w=["ob"])
                S.dve(nc.vector.tensor_tensor, ob2[:, 0:QG], pso[1][:, 0:QG], rl[:, 1, 0:QG], ALU.mult, r=["pso1", "rl"], w=["ob2"])
                S.dve(nc.vector.scalar_tensor_tensor, osq[:, 0:QG], ob2[:, 0:QG], neglam, ob[:, 0:QG], ALU.mult, ALU.add,
                      r=["ob", "ob2", "lams"], w=["osq"])
                S.pool(nc.gpsimd.tensor_tensor, ob2[:, 0:QG], osq[:, 0:QG], osq[:, 0:QG], ALU.mult, r=["osq"], w=["ob2"])
                psq, psqn = PSS.next()
                S.pe(nc.tensor.matmul, psq[:, 0, 0:QG], ones_f[:], ob2[:, 0:QG], start=True, stop=True, r=["ones_f", "ob2"], w=[psqn])
                S.act(nc.scalar.activation, rs2[:, 0:QG], psq[:, 0, 0:QG], AF.Ln, scale=1.0 / 128, bias=EPS, r=[psqn], w=["rs2"])
                S.act(nc.scalar.activation, rs2[:, 0:QG], rs2[:, 0:QG], AF.Exp, scale=-0.5, r=["rs2"], w=["rs2"])
                ost, ostn = OST.next()
                S.dve(nc.vector.scalar_tensor_tensor, ost[:, 0:QG], osq[:, 0:QG], sublw[:, 0:1], rs2[:, 0:QG], ALU.mult, ALU.mult,
                      r=["osq", "sublw", "rs2"], w=[ostn])
                S.dma("sp", sq["OT"][h * 128:(h + 1) * 128, q0:q0 + QG], ost[:, 0:QG], r=[ostn], w=["OTscr%d" % si])

            pend = []
            for it in iters:
                front(it)
                pend.append(it)
                if len(pend) > LA:
                    back(pend.pop(0))
            while pend:
                back(pend.pop(0))
            S.flush()

        if _stop == 2:
            return nc
        with ExitStack() as st:
            sb = lambda n, sh, dt=F32: st.enter_context(nc.sbuf_tensor(n, sh, dt))
            psb = lambda n, sh, dt=F32: st.enter_context(nc.psum_tensor(n, sh, dt))
            nfw_bc = bc_load("nfw_bc", norm_ffn_w[0], D, st)
            GEN = Ring([(psb(f"p3g{i}", [128, 512]), f"p3g{i}") for i in range(4)])
            pstr = psb("p3tr", [128, 1024], BF16)
            PSHB = Ring([(psb(f"p3hb{i}", [128, 512]), f"p3hb{i}") for i in range(2)])
            wba = sb("wba", [128, 8, D], BF16); wout = sb("wout", [128, 8, D], BF16); wdn = sb("wdn", [128, 22, D], BF16)
            S.dma("sp", wba[:], wba_b, w=["wba"])
            S.dma("sp", wout[:], wout_b, w=["wout"])
            S.dma("sp", wdn[:, 0:11], wdn_b[:, 0:11], w=["wdn"])
            S.dma("sp", wdn[:, 11:22], wdn_b[:, 11:22], w=["wdn"])
            WR = Ring([(sb(f"w3r{i}", [128, 8, 512], BF16), f"w3r{i}") for i in range(2)])
            oT = sb("oT", [128, 8, 256], BF16)
            gat = sb("gat", [128, 8, 256]); mss = sb("mss", [128, 8, 256])
            mixT = sb("mixT", [128, 8, 256], BF16)
            mtmp = sb("mtmp", [128, 256])
            XT = Ring([(sb(f"x3t{i}", [128, D]), f"x3t{i}") for i in range(2)])
            x1 = sb("x1", [128, 2, D])
            xnb = sb("x3nb", [128, D], BF16)
            xn2T = sb("xn2T", [128, 8, 256], BF16)
            ssq = sb("ssq3", [128, 4]); junk = sb("junk3", [128, D], BF16)
            PRE = Ring([(sb(f"pre3{i}", [128, 2 + 256]), f"pre3{i}") for i in range(2)])
            halo2 = sb("halo2", [128, 22, 2])
            ctmp = sb("ctmp3", [128, 256]); sha = sb("sha", [128, 256])
            gT = sb("gT", [128, 22, 256], BF16)
            YST = Ring([(sb(f"yst{i}", [128, D]), f"yst{i}") for i in range(1)])
            for si, sq in enumerate(seqs):
                T, P = sq["T"], sq["P"]
                NT = 2 if P == 128 else 1
                GW = P * NT
                NG = T // GW
                if sq["past"] == 0:
                    S.pool(nc.gpsimd.memset, halo2[:], 0.0, w=["halo2"])
                else:
                    vsf = sffn[sq["idx"]].rearrange("j (c p) -> p c j", p=128)
                    for c in range(22):
                        S.dma("sp", halo2[:, c, :], vsf[:, c, :], w=["halo2"], allow_slow_non_contiguous=True)
                for g in range(NG):
                    t0 = g * GW
                    S.dma("sp", oT[:, :, 0:GW], sq["OT"].rearrange("(c p) t -> p c t", p=128)[:, :, t0:t0 + GW], r=["OTscr%d" % si], w=["oT"])
                    S.dma("sp", gat[:, :, 0:GW], sq["GA"].rearrange("(c p) t -> p c t", p=128)[:, :, t0:t0 + GW], r=["GAscr%d" % si], w=["gat"])
                    S.dma("sp", mss[:, :, 0:GW], sq["MS"].rearrange("(c p) t -> p c t", p=128)[:, :, t0:t0 + GW], r=["MSscr%d" % si], w=["mss"])
                    for m in range(8):
                        ps, pn = GEN.next()
                        for k in range(8):
                            S.pe(nc.tensor.matmul, ps[:, 0:GW], wba[:, k, m * 128:(m + 1) * 128], oT[:, k, 0:GW], start=(k == 0), stop=(k == 7),
                                 r=["wba", "oT"], w=[pn])
                        S.dve(nc.vector.tensor_tensor, mtmp[:, 0:GW], ps[:, 0:GW], gat[:, m, 0:GW], ALU.mult, r=[pn, "gat"], w=["mtmp"])
                        S.dve(nc.vector.tensor_tensor, mixT[:, m, 0:GW], mtmp[:, 0:GW], mss[:, m, 0:GW], ALU.add, r=["mtmp", "mss"], w=["mixT"])
                    for ti in range(NT):
                        r0 = t0 + ti * P
                        xt, xn_ = XT.next()
                        S.dma("sp", xt[:P], sq["x"][r0:r0 + P, :], w=[xn_])
                        for nb in range(2):
                            ps, pn = GEN.next()
                            for k in range(8):
                                S.pe(nc.tensor.matmul, ps[:P, :], mixT[:, k, ti * P:(ti + 1) * P], wout[:, k, nb * 512:(nb + 1) * 512],
                                     start=(k == 0), stop=(k == 7), r=["mixT", "wout"], w=[pn])
                            S.dve(nc.vector.tensor_tensor, x1[:P, ti, nb * 512:(nb + 1) * 512], ps[:P, :], xt[:P, nb * 512:(nb + 1) * 512], ALU.add,
                                  r=[pn, xn_], w=["x1_%d" % ti])
                        S.dve(nc.vector.memset, ssq[:P, 0:1], 0.0, w=["ssq3"])
                        S.act(nc.scalar.activation, junk[:P], x1[:P, ti, :], AF.Square, accum_out=ssq[:P, 0:1], r=["x1_%d" % ti, "ssq3"], w=["junk3", "ssq3"])
                        S.act(nc.scalar.activation, ssq[:P, 0:1], ssq[:P, 0:1], AF.Ln, scale=1.0 / D, bias=EPS, r=["ssq3"], w=["ssq3"])
                        S.act(nc.scalar.activation, ssq[:P, 0:1], ssq[:P, 0:1], AF.Exp, scale=-0.5, r=["ssq3"], w=["ssq3"])
                        S.dve(nc.vector.scalar_tensor_tensor, xnb[:P], x1[:P, ti, :], ssq[:P, 0:1], nfw_bc[:P], ALU.mult, ALU.mult,
                              r=["x1_%d" % ti, "ssq3", "nfw_bc"], w=["x3nb"])
                        for k in range(8):
                            S.pe(nc.tensor.transpose, pstr[:, k * P:(k + 1) * P], xnb[:P, k * 128:(k + 1) * 128], ident_bf[:P, :P],
                                 r=["x3nb", "ident_bf"], w=["p3tr"])
                        S.act(nc.scalar.copy, xn2T[:, :, ti * P:(ti + 1) * P], pstr[:, 0:8 * P].rearrange("p (k t) -> p k t", k=8),
                              r=["p3tr"], w=["xn2T"])
                    for bj in range(11):
                        wt, wn = WR.next()
                        S.dma("sp", wt[:], wup_b[bj], r=["wup_b"], w=[wn])
                        for cc in range(2):
                            c = bj * 2 + cc
                            ps, pn = GEN.next()
                            for k in range(8):
                                S.pe(nc.tensor.matmul, ps[:, 0:GW], wt[:, k, cc * 128:(cc + 1) * 128], xn2T[:, k, 0:GW],
                                     start=(k == 0), stop=(k == 7), r=[wn, "xn2T"], w=[pn])
                            phb, phbn = PSHB.next()
                            for k in range(8):
                                S.pe(nc.tensor.matmul, phb[:, 0:GW], wt[:, k, 256 + cc * 128:256 + (cc + 1) * 128], xn2T[:, k, 0:GW],
                                     start=(k == 0), stop=(k == 7), r=[wn, "xn2T"], w=[phbn])
                            pre, pren = PRE.next()
                            S.pool(nc.gpsimd.tensor_copy, pre[:, 0:2], halo2[:, c, :], r=["halo2"], w=[pren])
                            S.act(nc.scalar.copy, pre[:, 2:2 + GW], ps[:, 0:GW], r=[pn], w=[pren])
                            S.pool(nc.gpsimd.tensor_copy, halo2[:, c, :], pre[:, GW:GW + 2], r=[pren], w=["halo2"])
                            S.dve(nc.vector.tensor_scalar, ctmp[:, 0:GW], pre[:, 0:GW], fcw[:, c, 0:1], fcb[:, c:c + 1], ALU.mult, ALU.add,
                                  r=[pren, "fcw", "fcb"], w=["ctmp3"])
                            for j in range(1, 3):
                                S.dve(nc.vector.scalar_tensor_tensor, ctmp[:, 0:GW], pre[:, j:j + GW], fcw[:, c, j:j + 1], ctmp[:, 0:GW],
                                      ALU.mult, ALU.add, r=[pren, "fcw", "ctmp3"], w=["ctmp3"])
                            S.act(nc.scalar.activation, sha[:, 0:GW], ctmp[:, 0:GW], AF.Silu, r=["ctmp3"], w=["sha"])
                            S.dve(nc.vector.tensor_tensor, gT[:, c, 0:GW], sha[:, 0:GW], phb[:, 0:GW], ALU.mult, r=["sha", phbn], w=["gT"])
                    for ti in range(NT):
                        r0 = t0 + ti * P
                        yst, ystn = YST.next()
                        for nb in range(2):
                            ps, pn = GEN.next()
                            for k in range(22):
                                S.pe(nc.tensor.matmul, ps[:P, :], gT[:, k, ti * P:(ti + 1) * P], wdn[:, k, nb * 512:(nb + 1) * 512],
                                     start=(k == 0), stop=(k == 21), r=["gT", "wdn"], w=[pn])
                            S.dve(nc.vector.tensor_tensor, yst[:P, nb * 512:(nb + 1) * 512], ps[:P, :], x1[:P, ti, nb * 512:(nb + 1) * 512], ALU.add,
                                  r=[pn, "x1_%d" % ti], w=[ystn])
                        S.dma("sp", sq["y"][r0:r0 + P, :], yst[:P], r=[ystn])
                vfo = sq["ffn"].rearrange("j (c p) -> p c j", p=128)
                for c in range(22):
                    S.dma("sp", vfo[:, c, :], halo2[:, c, :], r=["halo2"], allow_slow_non_contiguous=True)
            S.flush()
    return nc


_PROG_CACHE = {}


def kernel(x_prompt, x_sample, cache_k, cache_v, state_ssm, state_ssm_conv, state_ffn_conv,
           norm_mix_w, w_in, ssm_conv_w, ssm_conv_b, ssm_dt_bias, ssm_a_log, ssm_d, ssm_norm_w,
           q_norm_w, k_norm_w, lambda_q1, lambda_k1, lambda_q2, lambda_k2, subln_w,
           w_branch_ssm, w_branch_attn, w_out, norm_ffn_w, w_up, ffn_conv_w, ffn_conv_b, w_down):
    f = lambda a: np.ascontiguousarray(np.asarray(a, dtype=np.float32))
    x_prompt = f(x_prompt); x_sample = f(x_sample)
    B, Tp, _ = x_prompt.shape
    BS, Ts, _ = x_sample.shape
    past = cache_k.shape[2]
    import os as _os
    NC = int(_os.environ.get('KCORES', '8'))
    NS = BS // 8
    key = (Tp, NS, Ts, past)
    nc = build_program(Tp, NS, Ts, past)
    ck = f(cache_k)[0].reshape(BS, past, D)
    cv = f(cache_v)[0].reshape(BS, past, D)
    sssm = f(state_ssm)[0]
    sconv = f(state_ssm_conv)[0]
    sffn = f(state_ffn_conv)[0]
    lam_in = np.stack([f(lambda_q1)[0], f(lambda_k1)[0], f(lambda_q2)[0], f(lambda_k2)[0]], axis=0)
    shared = {
        "norm_mix_w": f(norm_mix_w), "w_in": f(w_in)[0], "ssm_conv_w": f(ssm_conv_w)[0], "ssm_conv_b": f(ssm_conv_b),
        "ssm_dt_bias": f(ssm_dt_bias), "ssm_a_log": f(ssm_a_log), "ssm_d": f(ssm_d), "ssm_norm_w": f(ssm_norm_w),
        "q_norm_w": f(q_norm_w), "k_norm_w": f(k_norm_w), "lam_in": f(lam_in), "subln_w": f(subln_w)[0].reshape(128, 1),
        "w_bs": f(w_branch_ssm)[0], "w_ba": f(w_branch_attn)[0], "w_out": f(w_out)[0], "norm_ffn_w": f(norm_ffn_w),
        "w_up": f(w_up)[0], "ffn_conv_w": f(ffn_conv_w)[0], "ffn_conv_b": f(ffn_conv_b), "w_down": f(w_down)[0],
    }
    in_maps = []
    for c in range(NC):
        m = dict(shared)
        m["xp"] = x_prompt[c]
        sl = slice(c * NS, (c + 1) * NS)
        m["xs"] = x_sample[sl]; m["ck"] = ck[sl]; m["cv"] = cv[sl]
        m["sssm"] = sssm[sl]; m["sconv"] = sconv[sl]; m["sffn"] = sffn[sl]
        in_maps.append(m)
    res = run_bass_kernel_spmd(nc, in_maps, core_ids=list(range(NC)))
    R = res.results
    global _LAST
    _LAST = R
    if NC < 8:
        R = list(R) + [R[0]] * (8 - NC)
        NC = 8
    cat = lambda k: np.stack([R[c][k] for c in range(NC)], axis=0)
    cats = lambda k: np.concatenate([R[c][k] for c in range(NC)], axis=0)
    y_p = cat("y_p")
    y_s = cats("y_s")
    k_p = cat("k_p").reshape(1, B, Tp, 16, 64)
    v_p = cat("v_p").reshape(1, B, Tp, 8, 128)
    ssm_p = cat("ssm_p").reshape(1, B, 32, 64, 128)
    conv_p = cat("conv_p").reshape(1, B, 3, CD)
    ffn_p = cat("ffn_p").reshape(1, B, 2, DFF)
    k_s = cats("k_s").reshape(1, BS, Ts, 16, 64)
    v_s = cats("v_s").reshape(1, BS, Ts, 8, 128)
    ssm_s = cats("ssm_s").reshape(1, BS, 32, 64, 128)
    conv_s = cats("conv_s").reshape(1, BS, 3, CD)
    ffn_s = cats("ffn_s").reshape(1, BS, 2, DFF)
    return (y_p, y_s, k_p, v_p, ssm_p, conv_p, ffn_p, k_s, v_s, ssm_s, conv_s, ffn_s)
```

```python
import math
from contextlib import ExitStack
import numpy as np
import concourse.bass as bass
import concourse.mybir as mybir
from concourse.bass_utils import run_bass_kernel_spmd

F32 = mybir.dt.float32
BF16 = mybir.dt.bfloat16
I32 = mybir.dt.int32
AF = mybir.ActivationFunctionType
ALU = mybir.AluOpType
AX = mybir.AxisListType

SAME_ENGINE_SYNC = True
EPOCH = 20000
NDMA_SEM = 24
EPS = 1e-6
D = 1024
DI = 2048
CD = 3072
DFF = 2816
INW = 10272
ROPE_THETA = 500000.0


class Sched:
    def __init__(self, nc, stack):
        self.nc = nc
        self.eng = {"pe": nc.tensor, "act": nc.scalar, "dve": nc.vector, "pool": nc.gpsimd, "sp": nc.sync}
        self.ops = []
        self.last_writer = {}
        self.readers = {}
        self.stack = stack
        self.sig_count = {s: 0 for s in self.eng}
        self.sems = {s: [] for s in self.eng}
        self.dma_sems = {s: [stack.enter_context(nc.semaphore(f"dq_{s}_{i}")) for i in range(NDMA_SEM)]
                         for s in ("sp", "act", "pool")}
        self.dma_count = {s: 0 for s in ("sp", "act", "pool")}
        self.waited = {}
        self.n_emitted = 0

    def _sem(self, stream, epoch):
        while len(self.sems[stream]) <= epoch:
            i = len(self.sems[stream])
            self.sems[stream].append(self.stack.enter_context(self.nc.semaphore(f"s_{stream}_{i}")))
        return self.sems[stream][epoch]

    def op(self, stream, method, *args, r=(), w=(), dma=False, **kwargs):
        oid = len(self.ops)
        deps = set()
        for x in list(r) + list(w):
            lw = self.last_writer.get(x)
            if lw is not None:
                deps.add(lw)
        for x in w:
            for rd in self.readers.get(x, ()):
                deps.add(rd)
        for x in r:
            if x[:2] in ("ps", "p3"):
                for rd in self.readers.get(x, ()):
                    if self.ops[rd]["stream"] != stream:
                        deps.add(rd)
        deps.discard(oid)
        for x in r:
            self.readers.setdefault(x, []).append(oid)
        for x in w:
            self.last_writer[x] = oid
            self.readers[x] = []
        self.ops.append(dict(stream=stream, method=method, args=args, kwargs=kwargs, deps=deps, dma=dma,
                             sig=False))
        return oid

    def pe(self, m, *a, **k):
        return self.op("pe", m, *a, **k)

    def act(self, m, *a, **k):
        return self.op("act", m, *a, **k)

    def dve(self, m, *a, **k):
        return self.op("dve", m, *a, **k)

    def pool(self, m, *a, **k):
        return self.op("pool", m, *a, **k)

    def dma(self, q, out, in_, r=(), w=(), **k):
        m = self.eng[q].dma_start
        return self.op(q, m, r=r, w=w, dma=True, out=out, in_=in_, **k)

    def _wait(self, stream, sem, val):
        key = (stream, id(sem))
        if self.waited.get(key, 0) >= val:
            return
        self.waited[key] = val
        if getattr(self, 'dbg', False):
            print('   WAIT', stream, getattr(sem, 'name', sem), val)
        self.eng[stream].wait_ge(sem, val)

    def flush(self):
        import os as _os
        _cut = int(_os.environ.get('KCUT', '0'))
        if _cut > 0 and self.n_emitted == 0:
            print('total ops in flush', len(self.ops))
            for _i in range(max(0, _cut - 6), min(len(self.ops), _cut + 2)):
                _o = self.ops[_i]
                print('OP', _i, _o['stream'], getattr(_o['method'], '__name__', _o['method']), [str(a)[:150] for a in _o['args']], {k: str(v)[:150] for k, v in _o['kwargs'].items()})
            self.ops = self.ops[:_cut]
        ops = self.ops
        for o in ops:
            for d in o["deps"]:
                od = ops[d]
                if od["dma"]:
                    continue
                if od["stream"] == o["stream"] and not o["dma"]:
                    if od["stream"] == "pe" or not SAME_ENGINE_SYNC:
                        continue
                od["sig"] = True
        lastc = {}
        for i, o in enumerate(ops):
            if not o["dma"]:
                lastc[o["stream"]] = i
        for s, i in lastc.items():
            ops[i]["sig"] = True
        for i, o in enumerate(ops):
            s = o["stream"]
            self.dbg = (_cut > 0 and i >= _cut - 12)
            if self.dbg:
                print('EMIT', i, s, getattr(o['method'], '__name__', ''), sorted(o['deps']), 'sig', o['sig'])
            for d in sorted(o["deps"]):
                od = ops[d]
                if od["dma"]:
                    self._wait(s, od["dsem"], od["dval"])
                else:
                    if od["stream"] == s and not o["dma"]:
                        if s == "pe" or not SAME_ENGINE_SYNC:
                            continue
                    self._wait(s, od["ssem"], od["sval"])
            if o["dma"]:
                k = self.dma_count[s]
                self.dma_count[s] = k + 1
                sem = self.dma_sems[s][k % NDMA_SEM]
                use = k // NDMA_SEM
                if use > 0:
                    self._wait(s, sem, 16 * use)
                inst = o["method"](*o["args"], **o["kwargs"])
                inst.then_inc(sem, 16)
                o["dsem"] = sem
                o["dval"] = 16 * (use + 1)
            else:
                inst = o["method"](*o["args"], **o["kwargs"])
                if o["sig"]:
                    k = self.sig_count[s]
                    self.sig_count[s] = k + 1
                    sem = self._sem(s, k // EPOCH)
                    inst.then_inc(sem, 1)
                    o["ssem"] = sem
                    o["sval"] = k % EPOCH + 1
            o["args"] = None
            o["kwargs"] = None
        self.n_emitted += len(ops)
        for s in list(self.eng.keys()):
            for s2, i in lastc.items():
                if s2 != s:
                    self._wait(s, ops[i]["ssem"], ops[i]["sval"])
            for q in self.dma_sems:
                n = self.dma_count[q]
                for j in range(min(n, NDMA_SEM)):
                    uses = (n - 1 - j) // NDMA_SEM + 1
                    self._wait(s, self.dma_sems[q][j], 16 * uses)
        self.ops = []
        self.last_writer = {}
        self.readers = {}


class Ring:
    def __init__(self, items):
        self.items = items
        self.i = 0

    def next(self):
        it = self.items[self.i % len(self.items)]
        self.i += 1
        return it


SEG = {}
_o = 0
for _n, _w in (("z", DI), ("xbc", CD), ("dt", 32), ("q", D), ("k", D), ("v", D), ("gs", D), ("ga", D)):
    SEG[_n] = (_o, _w)
    _o += _w
WIN_BLOCKS = []
for _n in ("xbc", "dt", "z", "q", "k", "v", "gs", "ga"):
    o0, w0 = SEG[_n]
    nb = (w0 + 511) // 512
    for b in range(nb):
        WIN_BLOCKS.append((_n, b, o0 + b * 512, min(512, w0 - b * 512)))
NWB = len(WIN_BLOCKS)


def build_program(Tp, NS, Ts, past):
    nc = bass.Bass("TRN2", target_bir_lowering=False)
    dram_in = lambda n, sh: nc.dram_tensor(n, sh, F32, kind="ExternalInput").ap()
    dram_out = lambda n, sh: nc.dram_tensor(n, sh, F32, kind="ExternalOutput").ap()
    xp = dram_in("xp", [Tp, D])
    xs = dram_in("xs", [NS, Ts, D])
    ck = dram_in("ck", [NS, past, D])
    cv = dram_in("cv", [NS, past, D])
    sssm = dram_in("sssm", [NS, 32, 64, 128])
    sconv = dram_in("sconv", [NS, 3, CD])
    sffn = dram_in("sffn", [NS, 2, DFF])
    norm_mix_w = dram_in("norm_mix_w", [1, D])
    w_in = dram_in("w_in", [D, INW])
    ssm_conv_w = dram_in("ssm_conv_w", [4, CD])
    ssm_conv_b = dram_in("ssm_conv_b", [1, CD])
    ssm_dt_bias = dram_in("ssm_dt_bias", [1, 32])
    ssm_a_log = dram_in("ssm_a_log", [1, 32])
    ssm_d = dram_in("ssm_d", [1, 32])
    ssm_norm_w = dram_in("ssm_norm_w", [1, DI])
    q_norm_w = dram_in("q_norm_w", [1, 64])
    k_norm_w = dram_in("k_norm_w", [1, 64])
    lam_in = dram_in("lam_in", [4, 64])
    subln_w = dram_in("subln_w", [128, 1])
    w_bs = dram_in("w_bs", [DI, D])
    w_ba = dram_in("w_ba", [D, D])
    w_out = dram_in("w_out", [D, D])
    norm_ffn_w = dram_in("norm_ffn_w", [1, D])
    w_up = dram_in("w_up", [D, 2 * DFF])
    ffn_conv_w = dram_in("ffn_conv_w", [3, DFF])
    ffn_conv_b = dram_in("ffn_conv_b", [1, DFF])
    w_down = dram_in("w_down", [DFF, D])

    y_p = dram_out("y_p", [Tp, D]); k_p = dram_out("k_p", [Tp, D]); v_p = dram_out("v_p", [Tp, D])
    ssm_p = dram_out("ssm_p", [32 * 64, 128]); conv_p = dram_out("conv_p", [3, CD]); ffn_p = dram_out("ffn_p", [2, DFF])
    y_s = dram_out("y_s", [NS, Ts, D]); k_s = dram_out("k_s", [NS, Ts, D]); v_s = dram_out("v_s", [NS, Ts, D])
    ssm_s = dram_out("ssm_s", [NS, 32 * 64, 128]); conv_s = dram_out("conv_s", [NS, 3, CD]); ffn_s = dram_out("ffn_s", [NS, 2, DFF])

    win_b = nc.dram_tensor("win_b", [NWB, 128, 8, 512], BF16).ap()
    wbs_b = nc.dram_tensor("wbs_b", [8, 128, 16, 128], BF16).ap()
    wba_b = nc.dram_tensor("wba_b", [128, 8, D], BF16).ap()
    wout_b = nc.dram_tensor("wout_b", [128, 8, D], BF16).ap()
    wup_b = nc.dram_tensor("wup_b", [11, 128, 8, 512], BF16).ap()
    wdn_b = nc.dram_tensor("wdn_b", [128, 22, D], BF16).ap()

    seqs = [dict(T=Tp, P=128, x=xp, past=0, y=y_p, k=k_p, v=v_p, ssm=ssm_p, conv=conv_p, ffn=ffn_p, idx=None)]
    for i in range(NS):
        seqs.append(dict(T=Ts, P=64, x=xs[i], past=past, y=y_s[i], k=k_s[i], v=v_s[i], ssm=ssm_s[i], conv=conv_s[i],
                         ffn=ffn_s[i], idx=i))
    for si, sq in enumerate(seqs):
        Tk = sq["past"] + sq["T"]
        sq["Tk"] = Tk
        sq["QT"] = nc.dram_tensor(f"QT{si}", [D, sq["T"]], BF16).ap()
        sq["KT"] = nc.dram_tensor(f"KT{si}", [D, Tk], BF16).ap()
        sq["V"] = nc.dram_tensor(f"V{si}", [Tk, D], BF16).ap()
        import os as _os
        _dbg = _os.environ.get("KDBG", "0") == "1"
        sq["OT"] = nc.dram_tensor(f"OT{si}", [D, sq["T"]], BF16, **({"kind": "ExternalOutput"} if _dbg else {})).ap()
        sq["MS"] = nc.dram_tensor(f"MS{si}", [D, sq["T"]], F32, **({"kind": "ExternalOutput"} if _dbg else {})).ap()
        sq["GA"] = nc.dram_tensor(f"GA{si}", [D, sq["T"]], F32, **({"kind": "ExternalOutput"} if _dbg else {})).ap()

    with ExitStack() as gst:
        S = Sched(nc, gst)
        gsb = lambda n, sh, dt=F32: gst.enter_context(nc.sbuf_tensor(n, sh, dt))

        ident_bf = gsb("ident_bf", [128, 128], BF16)
        ident_f = gsb("ident_f", [128, 128])
        ones_bf = gsb("ones_bf", [128, 128], BF16)
        ones_f = gsb("ones_f", [128, 128])
        U = gsb("U", [128, 128]); SU = gsb("SU", [128, 128]); BDm = gsb("BDm", [128, 128])
        IA = gsb("IA", [128, 128]); IB = gsb("IB", [128, 128])
        tmpc = gsb("tmpc", [128, 128])
        S.pool(nc.gpsimd.memset, tmpc[:], 0.0, w=["tmpc"])
        S.pool(nc.gpsimd.affine_select, tmpc[:], tmpc[:], [[-1, 128]], ALU.not_equal, 1.0, base=0, channel_multiplier=1,
               r=["tmpc"], w=["tmpc"])
        S.dve(nc.vector.tensor_copy, ident_bf[:], tmpc[:], r=["tmpc"], w=["ident_bf"])
        S.dve(nc.vector.tensor_copy, ident_f[:], tmpc[:], r=["tmpc"], w=["ident_f"])
        S.pool(nc.gpsimd.memset, ones_f[:], 1.0, w=["ones_f"])
        S.pool(nc.gpsimd.memset, ones_bf[:], 1.0, w=["ones_bf"])
        S.pool(nc.gpsimd.memset, U[:], 1.0, w=["U"])
        S.pool(nc.gpsimd.affine_select, U[:], U[:], [[1, 128]], ALU.is_ge, 0.0, base=0, channel_multiplier=-1, r=["U"], w=["U"])
        S.pool(nc.gpsimd.memset, U[0:64, 64:128], 0.0, r=["U"], w=["U"])
        S.pool(nc.gpsimd.memset, SU[:], 1.0, w=["SU"])
        S.pool(nc.gpsimd.affine_select, SU[:], SU[:], [[-1, 128]], ALU.is_gt, 0.0, base=0, channel_multiplier=1, r=["SU"], w=["SU"])
        S.pool(nc.gpsimd.memset, SU[64:128, 0:64], 0.0, r=["SU"], w=["SU"])
        SU_bf = gsb("SU_bf", [128, 128], BF16)
        S.dve(nc.vector.tensor_copy, SU_bf[:], SU[:], r=["SU"], w=["SU_bf"])
        S.pool(nc.gpsimd.memset, BDm[:], 0.0, w=["BDm"])
        S.pool(nc.gpsimd.memset, BDm[0:64, 0:64], 1.0, r=["BDm"], w=["BDm"])
        S.pool(nc.gpsimd.memset, BDm[64:128, 64:128], 1.0, r=["BDm"], w=["BDm"])
        S.pool(nc.gpsimd.memset, IA[:], 0.0, w=["IA"])
        S.pool(nc.gpsimd.memset, IA[0:64, :], 1.0, r=["IA"], w=["IA"])
        S.pool(nc.gpsimd.memset, IB[:], 0.0, w=["IB"])
        S.pool(nc.gpsimd.memset, IB[64:128, :], 1.0, r=["IB"], w=["IB"])

        def bc_load(name, src_row, n, stk=None):
            t = (stk or gst).enter_context(nc.sbuf_tensor(name, [128, n], F32))
            S.dma("sp", t[:], src_row.partition_broadcast(128), w=[name])
            return t

        qnw_bc = bc_load("qnw_bc", q_norm_w[0], 64)
        knw_bc = bc_load("knw_bc", k_norm_w[0], 64)
        dtb_bc = bc_load("dtb_bc", ssm_dt_bias[0], 32)
        alog_bc = bc_load("alog_bc", ssm_a_log[0], 32)
        dsk_bc = bc_load("dsk_bc", ssm_d[0], 32)
        A_bc = gsb("A_bc", [128, 32])
        S.act(nc.scalar.activation, A_bc[:], alog_bc[:], AF.Exp, r=["alog_bc"], w=["A_bc"])
        S.dve(nc.vector.tensor_scalar, A_bc[:], A_bc[:], -1.0, None, ALU.mult, r=["A_bc"], w=["A_bc"])
        scw = gsb("scw", [128, 24, 4]); scb = gsb("scb", [128, 24])
        fcw = gsb("fcw", [128, 22, 3]); fcb = gsb("fcb", [128, 22])
        v_scw = ssm_conv_w.rearrange("j (c p) -> p c j", p=128)
        for c in range(24):
            S.dma("sp", scw[:, c, :], v_scw[:, c, :], w=["scw"], allow_slow_non_contiguous=True)
        S.dma("sp", scb[:], ssm_conv_b[0].rearrange("(c p) -> p c", p=128), w=["scb"], allow_slow_non_contiguous=True)
        v_fcw = ffn_conv_w.rearrange("j (c p) -> p c j", p=128)
        for c in range(22):
            S.dma("sp", fcw[:, c, :], v_fcw[:, c, :], w=["fcw"], allow_slow_non_contiguous=True)
        S.dma("sp", fcb[:], ffn_conv_b[0].rearrange("(c p) -> p c", p=128), w=["fcb"], allow_slow_non_contiguous=True)
        lam_bc = gsb("lam_bc", [128, 4, 64])
        for i in range(4):
            S.dma("sp", lam_bc[:, i, :], lam_in[i].partition_broadcast(128), w=["lam_bc"])
        lamt = gsb("lamt", [128, 2, 64]); lams = gsb("lams", [128, 4])
        S.dve(nc.vector.tensor_tensor, lamt[:, 0, :], lam_bc[:, 0, :], lam_bc[:, 1, :], ALU.mult, r=["lam_bc"], w=["lamt"])
        S.dve(nc.vector.tensor_tensor, lamt[:, 1, :], lam_bc[:, 2, :], lam_bc[:, 3, :], ALU.mult, r=["lam_bc", "lamt"], w=["lamt"])
        S.dve(nc.vector.tensor_reduce, lams[:, 0:2], lamt[:], AX.X, ALU.add, r=["lamt"], w=["lams"])
        S.act(nc.scalar.activation, lams[:, 0:2], lams[:, 0:2], AF.Exp, r=["lams"], w=["lams"])
        lambda_init = 0.8 - 0.6 * math.exp(-0.3 * 0)
        S.dve(nc.vector.tensor_tensor, lams[:, 2:3], lams[:, 1:2], lams[:, 0:1], ALU.subtract, r=["lams"], w=["lams"])
        S.dve(nc.vector.tensor_scalar, lams[:, 2:3], lams[:, 2:3], -lambda_init, None, ALU.add, r=["lams"], w=["lams"])
        neglam = lams[:, 2:3]
        sublw = gsb("sublw", [128, 1])
        S.dma("sp", sublw[:], subln_w[:, :], w=["sublw"])
        S.dve(nc.vector.tensor_scalar, sublw[:], sublw[:], 1.0 - lambda_init, None, ALU.mult, r=["sublw"], w=["sublw"])

        v_win = w_in.rearrange("(k p) n -> p k n", p=128)
        for bi, (_n, _b, c0, wd) in enumerate(WIN_BLOCKS):
            S.dma("pool", win_b[bi, :, :, 0:wd], v_win[:, :, c0:c0 + wd], w=["win_b%d" % bi])
        v_wbs = w_bs.rearrange("(k p) n -> p k n", p=128)
        for m in range(8):
            S.dma("pool", wbs_b[m], v_wbs[:, :, m * 128:(m + 1) * 128], w=["wbs_b%d" % m])
        for h2 in range(2):
            S.dma("pool", wba_b[:, :, h2 * 512:(h2 + 1) * 512], w_ba.rearrange("(k p) n -> p k n", p=128)[:, :, h2 * 512:(h2 + 1) * 512], w=["wba_b"])
            S.dma("pool", wout_b[:, :, h2 * 512:(h2 + 1) * 512], w_out.rearrange("(k p) n -> p k n", p=128)[:, :, h2 * 512:(h2 + 1) * 512], w=["wout_b"])
        v_wup = w_up.rearrange("(k p) n -> p k n", p=128)
        for j in range(11):
            S.dma("pool", wup_b[j, :, :, 0:256], v_wup[:, :, 256 * j:256 * j + 256], w=["wup_b"])
            S.dma("pool", wup_b[j, :, :, 256:512], v_wup[:, :, DFF + 256 * j:DFF + 256 * j + 256], w=["wup_b"])
        v_wdn = w_down.rearrange("(k p) n -> p k n", p=128)
        for h2 in range(2):
            for kk in range(2):
                S.dma("pool", wdn_b[:, kk * 11:(kk + 1) * 11, h2 * 512:(h2 + 1) * 512],
                      v_wdn[:, kk * 11:(kk + 1) * 11, h2 * 512:(h2 + 1) * 512], w=["wdn_b"])
        for sq in seqs:
            if sq["past"] > 0:
                i = sq["idx"]
                npc = sq["past"] // 512
                for j in range(npc):
                    S.dma("pool", sq["V"][j * 512:(j + 1) * 512, :], cv[i, j * 512:(j + 1) * 512, :], w=["Vscr%d" % i])

        invf = gsb("invf", [128, 8])
        inv_np = np.power(np.float32(ROPE_THETA), -(np.arange(8, dtype=np.float32) * np.float32(2.0) / np.float32(16))).astype(np.float32)
        for i in range(8):
            S.pool(nc.gpsimd.memset, invf[:, i:i + 1], float(inv_np[i]) / (2.0 * math.pi), r=["invf"], w=["invf"])

        def rope_tables(st, name, P, ntile, pos0, tmp_i, tmp_f, tmp_u):
            cs = st.enter_context(nc.sbuf_tensor(name + "_cs", [128, 2, ntile, 8], F32))
            pi_ = st.enter_context(nc.sbuf_tensor(name + "_pi", [128, ntile], I32))
            pf = st.enter_context(nc.sbuf_tensor(name + "_pf", [128, ntile], F32))
            u = tmp_u[:, 0:2 * ntile * 8].rearrange("p (a b c) -> p a b c", a=2, b=ntile)
            ui = tmp_i[:, 0:2 * ntile * 8].rearrange("p (a b c) -> p a b c", a=2, b=ntile)
            uf = tmp_f[:, 0:2 * ntile * 8].rearrange("p (a b c) -> p a b c", a=2, b=ntile)
            rn = name + "_rope"
            S.pool(nc.gpsimd.iota, pi_[:], [[P, ntile]], base=pos0, channel_multiplier=1, w=[rn])
            S.dve(nc.vector.tensor_copy, pf[:], pi_[:], r=[rn], w=[rn])
            S.dve(nc.vector.tensor_tensor, u[:, 1], pf[:, :, None].broadcast_to([128, ntile, 8]),
                  invf[:, None, :].broadcast_to([128, ntile, 8]), ALU.mult, r=[rn, "invf", "sgs"], w=[rn, "sgs"])
            S.dve(nc.vector.tensor_scalar, u[:, 0], u[:, 1], 0.25, None, ALU.add, r=[rn, "sgs"], w=[rn, "sgs"])
            S.dve(nc.vector.tensor_copy, ui, u, r=[rn, "sgs"], w=[rn, "sttmp"])
            S.dve(nc.vector.tensor_copy, uf, ui, r=[rn, "sttmp", "sgs"], w=[rn, "sttmp", "sgs"])
            S.dve(nc.vector.tensor_tensor, u, u, uf, ALU.subtract, r=[rn, "sttmp", "sgs"], w=[rn, "sgs"])
            S.dve(nc.vector.tensor_scalar, u, u, 0.49999, -0.49999, ALU.min, ALU.max, r=[rn, "sgs"], w=[rn, "sgs"])
            S.act(nc.scalar.activation, cs[:], u, AF.Sin, scale=2.0 * math.pi, r=[rn, "sgs"], w=[rn])
            return cs, rn

        import os as _os
        _stop = int(_os.environ.get('KSTOP', '9'))
        if _stop == 0:
            S.flush()
            return nc
        with ExitStack() as st:
            sb = lambda n, sh, dt=F32: st.enter_context(nc.sbuf_tensor(n, sh, dt))
            psb = lambda n, sh, dt=F32: st.enter_context(nc.psum_tensor(n, sh, dt))
            nmw_bc = bc_load("nmw_bc", norm_mix_w[0], D, st)
            snw_bc = bc_load("snw_bc", ssm_norm_w[0], DI, st)
            GEN = Ring([(psb(f"psg{i}", [128, 512]), f"psg{i}") for i in range(2)])
            pstr = psb("pstr", [128, 1024], BF16)
            pscb = psb("pscb", [128, 512])
            PSB = Ring([(pscb, "pscb")])
            psdt = psb("psdt", [128, 512])
            psseg = psb("psseg", [128, 1024])
            psyi = psb("psyi", [128, 512])
            WR = Ring([(sb(f"wr{i}", [128, 8, 512], BF16), f"wr{i}") for i in range(2)])
            WBS = Ring([(sb(f"wbs{i}", [128, 16, 128], BF16), f"wbs{i}") for i in range(2)])
            XT = Ring([(sb(f"xt{i}", [128, D]), f"xt{i}") for i in range(2)])
            xnb = sb("xnb", [128, D], BF16)
            XNT = Ring([(sb(f"xnT{i}", [128, 8, 256], BF16), f"xnT{i}") for i in range(2)])
            PRE = Ring([(sb(f"pre{i}", [128, 3 + 256]), f"pre{i}") for i in range(4)])
            CTMP = Ring([(sb(f"cvt{i}", [128, 256]), f"cvt{i}") for i in range(2)])
            halo = sb("halo", [128, 24, 3])
            xbcT = sb("xbcT", [128, 24, 256], BF16)
            sz = sb("sz", [128, 2, DI])
            sgs = sb("sgs", [128, 8, 256])
            small = sb("small", [128, 16, 32])
            ssq = sb("ssq", [128, 16])
            junk = sb("junk", [128, D], BF16)
            XS = Ring([(sb(f"xs{i}", [128, 512], BF16), f"xs{i}") for i in range(3)])
            XDT = Ring([(sb(f"xdt{i}", [128, 512], BF16), f"xdt{i}") for i in range(3)])
            XDE = Ring([(sb(f"xde{i}", [128, 512], BF16), f"xde{i}") for i in range(3)])
            BTOK = Ring([(sb(f"btok{i}", [128, 4, 128], BF16), f"btok{i}") for i in range(2)])
            RG = Ring([((sb(f"rgh{i}", [128, 8, 128], BF16), sb(f"rgl{i}", [128, 8, 128], BF16)), f"rg{i}") for i in range(1)])
            small2 = sb("small2", [128, 4, 32]); smallb = sb("smallb", [128, 2, 32], BF16)
            DEC = Ring([(sb(f"dec{i}", [128, 8, 128]), f"dec{i}") for i in range(1)])
            MT = Ring([(sb(f"mt{i}", [128, 8, 128], BF16), f"mt{i}") for i in range(3)])
            CBM = Ring([(sb(f"cbm{i}", [128, 4, 128]), f"cbm{i}") for i in range(2)])
            CT = Ring([((sb(f"cta{i}", [128, 4, 128], BF16), sb(f"ctb{i}", [128, 4, 128], BF16)), (f"cta{i}", f"ctb{i}")) for i in range(2)])
            Sst = sb("Sst", [128, DI])
            s0bf = sb("s0bf", [128, DI], BF16); s1bf = sb("s1bf", [128, DI], BF16)
            YB = Ring([(sb(f"yb{i}", [128, 512]), f"yb{i}") for i in range(2)])
            dsk = sb("dsk", [128, 512])
            stmp = sb("stmp", [128, 512])
            YNB = Ring([(sb(f"ynb{i}", [128, 512], BF16), f"ynb{i}") for i in range(2)])
            ynT = sb("ynT", [128, 16, 256], BF16)
            STG = Ring([(sb(f"stg{i}", [128, 512]), f"stg{i}") for i in range(2)])
            STGB = Ring([(sb(f"stgb{i}", [128, 512], BF16), f"stgb{i}") for i in range(4)])
            QF = Ring([(sb(f"qf{i}", [128, 8, 64]), f"qf{i}") for i in range(2)])
            QSQ = Ring([(sb(f"qsq{i}", [128, 8, 64]), f"qsq{i}") for i in range(2)])
            SSK = Ring([0, 1, 2])
            ssqk = sb("ssqk", [128, 3, 8])
            rt = sb("rt", [128, 4, 8, 8])
            sttmp = sb("sttmp", [128, 16, 128])
            _stf = sttmp[:].rearrange("p a b -> p (a b)")
            rtmp_i = _stf[:, 0:1024].bitcast(I32); rtmp_f = _stf[:, 1024:2048]
            rtmp_u = sgs[:].rearrange("p a b -> p (a b)")
            SSTN = ["Sst0", "Sst1", "Sst2", "Sst3"]
            HALON = ["halo%d" % c for c in range(24)]
            XBCN = ["xbcT%d" % c for c in range(24)]
            for (_ca, _cb), (_can, _cbn) in CT.items:
                S.pool(nc.gpsimd.memset, _ca[:], 0.0, w=[_can])
                S.pool(nc.gpsimd.memset, _cb[:], 0.0, w=[_cbn])

            def rstd_from_ss(ap, n, res):
                S.act(nc.scalar.activation, ap, ap, AF.Ln, scale=1.0 / n, bias=EPS, r=res, w=res)
                S.act(nc.scalar.activation, ap, ap, AF.Exp, scale=-0.5, r=res, w=res)

            for si, sq in enumerate(seqs):
                T, P = sq["T"], sq["P"]
                NT = 2 if P == 128 else 1
                GW = P * NT
                NG = T // GW
                assert NG * GW == T
                nch = P // 64
                ntile = T // P
                pastn = sq["past"]
                cs, rn = rope_tables(st, f"rp{si}", P, ntile, pastn, rtmp_i, rtmp_f, rtmp_u)
                if pastn == 0:
                    S.pool(nc.gpsimd.memset, halo[:], 0.0, w=HALON)
                    S.pool(nc.gpsimd.memset, Sst[:], 0.0, w=SSTN)
                    S.pool(nc.gpsimd.memset, s0bf[:], 0.0, w=["s0bf0", "s0bf1", "s0bf2", "s0bf3"])
                else:
                    i = sq["idx"]
                    vsc = sconv[i].rearrange("j (c p) -> p c j", p=128)
                    for c in range(24):
                        S.dma("sp", halo[:, c, :], vsc[:, c, :], w=["halo%d" % c], allow_slow_non_contiguous=True)
                    S.dma("sp", sttmp[:], sssm[i].rearrange("(c h) p n -> (h p) c n", h=2), w=["sttmp"])
                    for c in range(16):
                        ps, pn = GEN.next()
                        S.pe(nc.tensor.transpose, ps[:, 0:128], sttmp[:, c, :], ident_f[:], r=["sttmp", "ident_f"], w=[pn])
                        S.act(nc.scalar.copy, Sst[:, c * 128:(c + 1) * 128], ps[:, 0:128], r=[pn], w=["Sst%d" % (c // 4)])
                    S.dve(nc.vector.tensor_copy, s0bf[:], Sst[:], r=SSTN, w=["s0bf0", "s0bf1", "s0bf2", "s0bf3"])
                    for j in range(pastn // 128):
                        xt, xn_ = XT.next()
                        S.dma("sp", xt[:], ck[i, j * 128:(j + 1) * 128, :], w=[xn_])
                        S.dve(nc.vector.tensor_copy, xnb[:], xt[:], r=[xn_], w=["xnb"])
                        for k in range(8):
                            S.pe(nc.tensor.transpose, pstr[:, k * 128:(k + 1) * 128], xnb[:, k * 128:(k + 1) * 128], ident_bf[:],
                                 r=["xnb", "ident_bf"], w=["pstr"])
                        for hh in range(2):
                            stb, stbn = STGB.next()
                            S.act(nc.scalar.copy, stb[:], pstr[:, hh * 512:(hh + 1) * 512], r=["pstr"], w=[stbn])
                            S.dma("sp", sq["KT"].rearrange("(c p) t -> p c t", p=128)[:, hh * 4:(hh + 1) * 4, j * 128:(j + 1) * 128],
                                  stb[:].rearrange("p (c t) -> p c t", c=4), r=[stbn], w=["KTscr%d" % si])

                for g in range(NG):
                    t0 = g * GW
                    xnT, xnTn = XNT.next()
                    for ti in range(NT):
                        r0 = t0 + ti * P
                        xt, xn_ = XT.next()
                        S.dma("sp", xt[:P], sq["x"][r0:r0 + P, :], w=[xn_])
                        S.dve(nc.vector.memset, ssq[:P, 0:1], 0.0, w=["ssq"])
                        S.act(nc.scalar.activation, junk[:P], xt[:P], AF.Square, accum_out=ssq[:P, 0:1], r=[xn_, "ssq"], w=["junk", "ssq"])
                        rstd_from_ss(ssq[:P, 0:1], D, ["ssq"])
                        S.dve(nc.vector.scalar_tensor_tensor, xnb[:P], xt[:P], ssq[:P, 0:1], nmw_bc[:P], ALU.mult, ALU.mult,
                              r=[xn_, "ssq", "nmw_bc"], w=["xnb"])
                        for k in range(8):
                            S.pe(nc.tensor.transpose, pstr[:, k * P:(k + 1) * P], xnb[:P, k * 128:(k + 1) * 128], ident_bf[:P, :P],
                                 r=["xnb", "ident_bf"], w=["pstr"])
                        S.act(nc.scalar.copy, xnT[:, :, ti * P:(ti + 1) * P], pstr[:, 0:8 * P].rearrange("p (k t) -> p k t", k=8),
                              r=["pstr"], w=[xnTn])
                    def qk_s0(it):
                        qf_, qfn_, qs_, qsn_, ss_ = it["qf"], it["qfn"], it["qsq"], it["qsn"], it["ss"]
                        ssa = ssqk[:P, ss_, :]
                        ssn = "ssqk%d" % ss_
                        S.act(nc.scalar.copy, qf_[:P], it["ps"][:P, :].rearrange("p (h d) -> p h d", h=8), r=[it["pn"]], w=[qfn_])
                        S.pool(nc.gpsimd.tensor_tensor, qs_[:P], qf_[:P], qf_[:P], ALU.mult, r=[qfn_], w=[qsn_])
                        S.dve(nc.vector.tensor_reduce, ssa, qs_[:P], AX.X, ALU.add, r=[qsn_], w=[ssn])
                        rstd_from_ss(ssa, 64, [ssn])

                    def qk_s1(it):
                        seg_, b_, r0_ = it["seg"], it["b"], it["r0"]
                        qf_, qfn_, qs_, qsn_, ss_ = it["qf"], it["qfn"], it["qsq"], it["qsn"], it["ss"]
                        ssn = "ssqk%d" % ss_
                        wbc = qnw_bc if seg_ == "q" else knw_bc
                        S.dve(nc.vector.tensor_tensor, qs_[:P], qf_[:P], ssqk[:P, ss_, :, None].broadcast_to([P, 8, 64]), ALU.mult,
                              r=[qfn_, ssn, qsn_], w=[qsn_])
                        S.dve(nc.vector.tensor_tensor, qf_[:P], qs_[:P], wbc[:P, None, :].broadcast_to([P, 8, 64]), ALU.mult,
                              r=[qsn_, qfn_, "qnw_bc", "knw_bc"], w=[qfn_])
                        tidx = (r0_ // P)
                        cosb = cs[:P, 0, tidx, None, :].broadcast_to([P, 8, 8])
                        sinb = cs[:P, 1, tidx, None, :].broadcast_to([P, 8, 8])
                        S.dve(nc.vector.tensor_tensor, rt[:P, 0], qf_[:P, :, 0:8], cosb, ALU.mult, r=[qfn_, rn], w=["rt0"])
                        S.dve(nc.vector.tensor_tensor, rt[:P, 1], qf_[:P, :, 8:16], sinb, ALU.mult, r=[qfn_, rn], w=["rt1"])
                        S.dve(nc.vector.tensor_tensor, rt[:P, 2], qf_[:P, :, 8:16], cosb, ALU.mult, r=[qfn_, rn], w=["rt2"])
                        S.dve(nc.vector.tensor_tensor, rt[:P, 3], qf_[:P, :, 0:8], sinb, ALU.mult, r=[qfn_, rn], w=["rt3"])
                        S.dve(nc.vector.tensor_tensor, qf_[:P, :, 0:8], rt[:P, 0], rt[:P, 1], ALU.subtract, r=["rt0", "rt1", qfn_], w=[qfn_])
                        S.dve(nc.vector.tensor_tensor, qf_[:P, :, 8:16], rt[:P, 2], rt[:P, 3], ALU.add, r=["rt2", "rt3", qfn_], w=[qfn_])
                        stb, stbn = STGB.next()
                        if seg_ == "k":
                            stg, stgn = STG.next()
                            S.pool(nc.gpsimd.tensor_copy, stg[:P], qf_[:P].rearrange("p h d -> p (h d)"), r=[qfn_], w=[stgn])
                            S.dma("sp", sq["k"][r0_:r0_ + P, b_ * 512:(b_ + 1) * 512], stg[:P], r=[stgn])
                            S.dve(nc.vector.tensor_copy, stb[:P], qf_[:P].rearrange("p h d -> p (h d)"), r=[qfn_], w=[stbn])
                        else:
                            S.dve(nc.vector.tensor_scalar, stb[:P], qf_[:P].rearrange("p h d -> p (h d)"), 0.125, None, ALU.mult,
                                  r=[qfn_], w=[stbn])
                        it["stb"] = (stb, stbn)

                    def qk_s2(it):
                        seg_, b_, r0_ = it["seg"], it["b"], it["r0"]
                        stb, stbn = it["stb"]
                        for j in range(4):
                            S.pe(nc.tensor.transpose, pstr[:, j * P:(j + 1) * P], stb[:P, j * 128:(j + 1) * 128], ident_bf[:P, :P],
                                 r=[stbn, "ident_bf"], w=["pstr"])
                        stb2, stbn2 = STGB.next()
                        S.act(nc.scalar.copy, stb2[:, 0:4 * P], pstr[:, 0:4 * P], r=["pstr"], w=[stbn2])
                        dst = sq["QT"] if seg_ == "q" else sq["KT"]
                        coff = r0_ if seg_ == "q" else pastn + r0_
                        S.dma("sp", dst.rearrange("(c p) t -> p c t", p=128)[:, b_ * 4:(b_ + 1) * 4, coff:coff + P],
                              stb2[:, 0:4 * P].rearrange("p (c t) -> p c t", c=4), r=[stbn2],
                              w=[("QTscr%d" if seg_ == "q" else "KTscr%d") % si])

                    qk_items = []

                    def qk_advance(item):
                        if item is not None:
                            qk_items.append(item)
                        n_ = len(qk_items)
                        if item is not None:
                            if n_ >= 3:
                                qk_s2(qk_items[n_ - 3])
                            if n_ >= 2:
                                qk_s1(qk_items[n_ - 2])
                            qk_s0(item)
                        else:
                            if n_ >= 2:
                                qk_s2(qk_items[n_ - 2])
                            if n_ >= 1:
                                qk_s1(qk_items[n_ - 1])
                                qk_s2(qk_items[n_ - 1])

                    def proc_block(bi, PSR):
                        seg, b, c0, wd = WIN_BLOCKS[bi]
                        wt, wn = WR.next()
                        S.dma("sp", wt[:], win_b[bi], r=["win_b%d" % bi], w=[wn])
                        if seg == "xbc":
                            for cp in range(2):
                                cs_ = [b * 4 + cp * 2, b * 4 + cp * 2 + 1]
                                pss_ = []
                                for c in cs_:
                                    cc = c - b * 4
                                    ps, pn = PSR.next()
                                    for k in range(8):
                                        S.pe(nc.tensor.matmul, ps[:, 0:GW], wt[:, k, cc * 128:(cc + 1) * 128], xnT[:, k, 0:GW],
                                             start=(k == 0), stop=(k == 7), r=[wn, xnTn], w=[pn])
                                    pss_.append((ps, pn))
                                pres = []
                                for c, (ps, pn) in zip(cs_, pss_):
                                    pre, pren = PRE.next()
                                    ct_, ctn_ = CTMP.next()
                                    S.pool(nc.gpsimd.tensor_copy, pre[:, 0:3], halo[:, c, :], r=["halo%d" % c], w=[pren])
                                    S.act(nc.scalar.copy, pre[:, 3:3 + GW], ps[:, 0:GW], r=[pn], w=[pren])
                                    S.pool(nc.gpsimd.tensor_copy, halo[:, c, :], pre[:, GW:GW + 3], r=[pren], w=["halo%d" % c])
                                    pres.append((pre, pren, ct_, ctn_))
                                for j in range(4):
                                    for c, (pre, pren, ct_, ctn_) in zip(cs_, pres):
                                        if j == 0:
                                            S.dve(nc.vector.tensor_scalar, ct_[:, 0:GW], pre[:, 0:GW], scw[:, c, 0:1], scb[:, c:c + 1], ALU.mult, ALU.add,
                                                  r=[pren, "scw", "scb"], w=[ctn_])
                                        else:
                                            S.dve(nc.vector.scalar_tensor_tensor, ct_[:, 0:GW], pre[:, j:j + GW], scw[:, c, j:j + 1], ct_[:, 0:GW],
                                                  ALU.mult, ALU.add, r=[pren, "scw", ctn_], w=[ctn_])
                                for c, (pre, pren, ct_, ctn_) in zip(cs_, pres):
                                    S.act(nc.scalar.activation, xbcT[:, c, 0:GW], ct_[:, 0:GW], AF.Silu, r=[ctn_], w=["xbcT%d" % c])
                        elif seg == "dt":
                            for ti in range(NT):
                                for k in range(8):
                                    S.pe(nc.tensor.matmul, psdt[:P, ti * 32:(ti + 1) * 32], xnT[:, k, ti * P:(ti + 1) * P], wt[:, k, 0:32],
                                         start=(k == 0), stop=(k == 7), r=[xnTn, wn], w=["psdt"])
                            for ti in range(NT):
                                S.dve(nc.vector.tensor_tensor, small[:P, ti, :], psdt[:P, ti * 32:(ti + 1) * 32], dtb_bc[:P], ALU.add,
                                      r=["psdt", "dtb_bc"], w=["sm%d" % ti])
                                S.act(nc.scalar.activation, small[:P, ti, :], small[:P, ti, :], AF.Exp, r=["sm%d" % ti], w=["sm%d" % ti])
                                S.act(nc.scalar.activation, small[:P, ti, :], small[:P, ti, :], AF.Ln, bias=1.0, r=["sm%d" % ti], w=["sm%d" % ti])
                        elif seg == "z":
                            for ti in range(NT):
                                ps, pn = PSR.next()
                                for k in range(8):
                                    S.pe(nc.tensor.matmul, ps[:P, :], xnT[:, k, ti * P:(ti + 1) * P], wt[:, k, :],
                                         start=(k == 0), stop=(k == 7), r=[xnTn, wn], w=[pn])
                                S.act(nc.scalar.activation, sz[:P, ti, b * 512:(b + 1) * 512], ps[:P, :], AF.Silu, r=[pn], w=["sz"])
                        elif seg in ("q", "k", "v"):
                            for ti in range(NT):
                                r0 = t0 + ti * P
                                ps, pn = PSR.next()
                                for k in range(8):
                                    S.pe(nc.tensor.matmul, ps[:P, :], xnT[:, k, ti * P:(ti + 1) * P], wt[:, k, :],
                                         start=(k == 0), stop=(k == 7), r=[xnTn, wn], w=[pn])
                                if seg == "v":
                                    if b == 0 and ti == 0:
                                        qk_advance(None)
                                    stg, stgn = STG.next()
                                    S.act(nc.scalar.copy, stg[:P], ps[:P, :], r=[pn], w=[stgn])
                                    S.dma("sp", sq["v"][r0:r0 + P, b * 512:(b + 1) * 512], stg[:P], r=[stgn])
                                    stb, stbn = STGB.next()
                                    S.dve(nc.vector.tensor_copy, stb[:P], stg[:P], r=[stgn], w=[stbn])
                                    S.dma("sp", sq["V"][pastn + r0:pastn + r0 + P, b * 512:(b + 1) * 512], stb[:P], r=[stbn], w=["Vscr%s" % si])
                                    continue
                                qfb, qfn = QF.next()
                                qsb, qsn = QSQ.next()
                                sslot = SSK.next()
                                item = dict(seg=seg, b=b, ti=ti, r0=r0, ps=ps, pn=pn, qf=qfb, qfn=qfn, qsq=qsb, qsn=qsn, ss=sslot)
                                qk_advance(item)
                        else:
                            for cc in range(4):
                                m = b * 4 + cc
                                ps, pn = PSR.next()
                                for k in range(8):
                                    S.pe(nc.tensor.matmul, ps[:, 0:GW], wt[:, k, cc * 128:(cc + 1) * 128], xnT[:, k, 0:GW],
                                         start=(k == 0), stop=(k == 7), r=[wn, xnTn], w=[pn])
                                if seg == "gs":
                                    S.act(nc.scalar.activation, sgs[:, m, 0:GW], ps[:, 0:GW], AF.Sigmoid, r=[pn], w=["sgs"])
                                else:
                                    stg, stgn = STG.next()
                                    S.act(nc.scalar.activation, stg[:, 0:GW], ps[:, 0:GW], AF.Sigmoid, r=[pn], w=[stgn])
                                    S.dma("sp", sq["GA"][m * 128:(m + 1) * 128, t0:t0 + GW], stg[:, 0:GW], r=[stgn], w=["GAscr%d" % si])

                    BLK_A = [i_ for i_, w_ in enumerate(WIN_BLOCKS) if w_[0] in ("xbc", "dt", "z", "gs")]
                    BLK_B = [i_ for i_, w_ in enumerate(WIN_BLOCKS) if w_[0] in ("q", "k", "v", "ga")]
                    for bi_ in BLK_A:
                        proc_block(bi_, GEN)
                    wbq = []
                    for m in range(2):
                        wb, wbn = WBS.next()
                        S.dma("sp", wb[:], wbs_b[m], r=["wbs_b%d" % m], w=[wbn])
                        wbq.append((wb, wbn))
                    def prologue(ti):
                        tc0 = ti * P
                        dt = small[:P, ti, :]
                        dtA = small[:P, 2 + ti, :]
                        expA = small[:P, 4 + ti, :]
                        acs = small[:P, 6 + ti, :]
                        toend = small[:P, 8 + ti, :]
                        dte = small[:P, 10 + ti, :]
                        dA = small[:, 12 + ti, :]
                        dB = small[:, 14 + ti, :]
                        n = lambda k: "sm%d" % k
                        S.dve(nc.vector.tensor_tensor, dtA, dt, A_bc[:P], ALU.mult, r=[n(ti), "A_bc"], w=[n(2 + ti)])
                        S.dve(nc.vector.tensor_copy, smallb[:P, ti, :], dtA, r=[n(2 + ti)], w=["smb%d" % ti])
                        S.dve(nc.vector.tensor_copy, small2[:P, ti, :], smallb[:P, ti, :], r=["smb%d" % ti], w=["smh%d" % ti])
                        S.dve(nc.vector.tensor_tensor, small2[:P, 2 + ti, :], dtA, small2[:P, ti, :], ALU.subtract, r=[n(2 + ti), "smh%d" % ti], w=["sml%d" % ti])
                        S.pe(nc.tensor.matmul, psdt[:P, 64:96], U[:P, :P], dtA, start=True, stop=True, r=["U", n(2 + ti)], w=["psdt"])
                        S.pe(nc.tensor.matmul, psdt[:P, 96:128], BDm[:P, :P], dtA, start=True, stop=True, r=["BDm", n(2 + ti)], w=["psdt"])
                        S.pe(nc.tensor.matmul, psdt[:, 128:160], IA[:P, :], dtA, start=True, stop=True, r=["IA", n(2 + ti)], w=["psdt"])
                        if nch == 2:
                            S.pe(nc.tensor.matmul, psdt[:, 160:192], IB[:P, :], dtA, start=True, stop=True, r=["IB", n(2 + ti)], w=["psdt"])
                        S.act(nc.scalar.activation, expA, psdt[:P, 64:96], AF.Exp, r=["psdt"], w=[n(4 + ti)])
                        S.act(nc.scalar.copy, acs, psdt[:P, 64:96], r=["psdt"], w=[n(6 + ti)])
                        S.dve(nc.vector.tensor_tensor, toend, psdt[:P, 96:128], acs, ALU.subtract, r=["psdt", n(6 + ti)], w=[n(8 + ti)])
                        S.act(nc.scalar.activation, toend, toend, AF.Exp, r=[n(8 + ti)], w=[n(8 + ti)])
                        S.dve(nc.vector.tensor_tensor, dte, dt, toend, ALU.mult, r=[n(ti), n(8 + ti)], w=[n(10 + ti)])
                        S.act(nc.scalar.activation, dA, psdt[:, 128:160], AF.Exp, r=["psdt"], w=[n(12 + ti)])
                        if nch == 2:
                            S.act(nc.scalar.activation, dB, psdt[:, 160:192], AF.Exp, r=["psdt"], w=[n(14 + ti)])
                        btok, btokn = BTOK.next()
                        cbm, cbmn = CBM.next()
                        (cta, ctb), (ctan, ctbn) = CT.next()
                        for gg in range(4):
                            S.pe(nc.tensor.transpose, pstr[:P, gg * 128:(gg + 1) * 128], xbcT[:, 16 + gg, tc0:tc0 + P], ident_bf[:],
                                 r=["xbcT%d" % (16 + gg), "ident_bf"], w=["pstr"])
                        S.act(nc.scalar.copy, btok[:P], pstr[:P, 0:512].rearrange("p (g n) -> p g n", g=4), r=["pstr"], w=[btokn])
                        for gg in range(4):
                            S.pe(nc.tensor.matmul, pscb[:P, gg * P:(gg + 1) * P], xbcT[:, 16 + gg, tc0:tc0 + P], xbcT[:, 20 + gg, tc0:tc0 + P],
                                 start=True, stop=True, r=["xbcT%d" % (16 + gg), "xbcT%d" % (20 + gg)], w=["pscb"])
                        S.dve(nc.vector.tensor_tensor, cbm[:P, :, :P], pscb[:P, 0:4 * P].rearrange("p (g l) -> p g l", g=4),
                              U[:P, None, :P].broadcast_to([P, 4, P]), ALU.mult, r=["pscb", "U"], w=[cbmn])
                        if nch == 2:
                            S.pool(nc.gpsimd.tensor_copy, cta[:, :, 0:64], xbcT[:, 20:24, tc0:tc0 + 64], r=XBCN[20:24], w=[ctan])
                            S.pool(nc.gpsimd.tensor_copy, ctb[:, :, 64:128], xbcT[:, 20:24, tc0 + 64:tc0 + 128], r=XBCN[20:24], w=[ctbn])
                        return dict(ti=ti, tc0=tc0, btok=btok, btokn=btokn, cbm=cbm, cbmn=cbmn, cta=cta, ctb=ctb, ctan=ctan, ctbn=ctbn)

                    def st0(it):
                        ti, gg = it["ti"], it["gg"]
                        if gg == 0:
                            PRO[ti] = prologue(ti)
                        pro = PRO[ti]
                        it["pro"] = pro
                        tc0 = pro["tc0"]
                        hs = slice(8 * gg, 8 * gg + 8)
                        n = lambda k: "sm%d" % k
                        for j in range(4):
                            S.pe(nc.tensor.transpose, pstr[:P, 512 + j * 128:512 + (j + 1) * 128], xbcT[:, 4 * gg + j, tc0:tc0 + P], ident_bf[:],
                                 r=["xbcT%d" % (4 * gg + j), "ident_bf"], w=["pstr"])
                        xs_, xsn = XS.next()
                        S.act(nc.scalar.copy, xs_[:P], pstr[:P, 512:1024], r=["pstr"], w=[xsn])
                        xdt, xdtn = XDT.next()
                        xde, xden = XDE.next()
                        S.dve(nc.vector.tensor_tensor, xdt[:P].rearrange("p (h d) -> p h d", h=8), xs_[:P].rearrange("p (h d) -> p h d", h=8),
                              small[:P, ti, hs, None].broadcast_to([P, 8, 64]), ALU.mult, r=[xsn, n(ti)], w=[xdtn])
                        S.pool(nc.gpsimd.tensor_tensor, xde[:P].rearrange("p (h d) -> p h d", h=8), xs_[:P].rearrange("p (h d) -> p h d", h=8),
                               small[:P, 10 + ti, hs, None].broadcast_to([P, 8, 64]), ALU.mult, r=[xsn, n(10 + ti)], w=[xden])
                        (rgh, rgl), rgn = RG.next()
                        S.dve(nc.vector.tensor_tensor, rgh[:P, :, :P], U[:P, None, :P].broadcast_to([P, 8, P]),
                              small2[:P, ti, hs, None].broadcast_to([P, 8, P]), ALU.mult, r=["U", "smh%d" % ti], w=[rgn + "h"])
                        S.dve(nc.vector.tensor_tensor, rgl[:P, :, :P], U[:P, None, :P].broadcast_to([P, 8, P]),
                              small2[:P, 2 + ti, hs, None].broadcast_to([P, 8, P]), ALU.mult, r=["U", "sml%d" % ti], w=[rgn + "l"])
                        it.update(xs=xs_, xsn=xsn, xdt=xdt, xdtn=xdtn, xde=xde, xden=xden, rgh=rgh, rgl=rgl, rgn=rgn)

                    def st0b(it):
                        ti, gg, pro = it["ti"], it["gg"], it["pro"]
                        rgh, rgl, rgn = it["rgh"], it["rgl"], it["rgn"]
                        nsub = (8 * P) // 512
                        hps = 512 // P
                        for j in range(nsub):
                            S.pe(nc.tensor.matmul, psseg[:P, j * 512:(j + 1) * 512], SU_bf[:P, :P],
                                 rgh[:P, j * hps:(j + 1) * hps, :P], start=True, stop=False, r=["SU_bf", rgn + "h"], w=["psseg"])
                            S.pe(nc.tensor.matmul, psseg[:P, j * 512:(j + 1) * 512], SU_bf[:P, :P],
                                 rgl[:P, j * hps:(j + 1) * hps, :P], start=False, stop=True, r=["SU_bf", rgn + "l"], w=["psseg"])
                        dec, decn = DEC.next()
                        S.act(nc.scalar.activation, dec[:P, :, :P], psseg[:P, 0:8 * P].rearrange("p (h l) -> p h l", h=8), AF.Exp,
                              r=["psseg"], w=[decn])
                        mt, mtn = MT.next()
                        S.dve(nc.vector.tensor_tensor, mt[:P, :, :P], dec[:P, :, :P], pro["cbm"][:P, gg, None, :P].broadcast_to([P, 8, P]), ALU.mult,
                              r=[decn, pro["cbmn"]], w=[mtn])
                        it.update(mt=mt, mtn=mtn)

                    def st1(it):
                        ti, gg, pro = it["ti"], it["gg"], it["pro"]
                        tc0 = pro["tc0"]
                        hs = slice(8 * gg, 8 * gg + 8)
                        fs = slice(512 * gg, 512 * gg + 512)
                        n = lambda k: "sm%d" % k
                        mt, mtn, xdt, xdtn, xde, xden = it["mt"], it["mtn"], it["xdt"], it["xdtn"], it["xde"], it["xden"]
                        btok, btokn = pro["btok"], pro["btokn"]
                        sn, s0n, s1n = "Sst%d" % gg, "s0bf%d" % gg, "s1bf%d" % gg
                        for h in range(8):
                            S.pe(nc.tensor.matmul, psyi[:P, h * 64:(h + 1) * 64], mt[:P, h, :P], xdt[:P, h * 64:(h + 1) * 64],
                                 start=True, stop=True, r=[mtn, xdtn], w=["psyi"])
                        pyo, pyon = GEN.next()
                        if nch == 2:
                            S.pe(nc.tensor.matmul, pyo[:P, :], pro["cta"][:, gg, :], s0bf[:, fs], start=True, stop=False,
                                 r=[pro["ctan"], s0n], w=[pyon])
                        else:
                            S.pe(nc.tensor.matmul, pyo[:P, :], xbcT[:, 20 + gg, tc0:tc0 + P], s0bf[:, fs], start=True, stop=True,
                                 r=["xbcT%d" % (20 + gg), s0n], w=[pyon])
                        pds, pdsn = GEN.next()
                        S.pe(nc.tensor.matmul, pds[:, :], btok[0:64, gg, :], xde[0:64, :], start=True, stop=True, r=[btokn, xden], w=[pdsn])
                        S.pool(nc.gpsimd.tensor_tensor, stmp[:].rearrange("p (h d) -> p h d", h=8), Sst[:, fs].rearrange("p (h d) -> p h d", h=8),
                               small[:, 12 + ti, hs, None].broadcast_to([128, 8, 64]), ALU.mult, r=[sn, n(12 + ti), "stmp"], w=["stmp"])
                        S.dve(nc.vector.tensor_tensor, Sst[:, fs], stmp[:], pds[:, :], ALU.add, r=["stmp", pdsn, sn], w=[sn])
                        if nch == 2:
                            S.act(nc.scalar.copy, s1bf[:, fs], Sst[:, fs], r=[sn], w=[s1n])
                        it.update(pyo=pyo, pyon=pyon, pds=pds, pdsn=pdsn)

                    def st1b(it):
                        ti, gg, pro = it["ti"], it["gg"], it["pro"]
                        hs = slice(8 * gg, 8 * gg + 8)
                        fs = slice(512 * gg, 512 * gg + 512)
                        n = lambda k: "sm%d" % k
                        xde, xden = it["xde"], it["xden"]
                        btok, btokn = pro["btok"], pro["btokn"]
                        sn, s0n, s1n = "Sst%d" % gg, "s0bf%d" % gg, "s1bf%d" % gg
                        pyo, pyon, pds, pdsn = it["pyo"], it["pyon"], it["pds"], it["pdsn"]
                        if nch == 2:
                            S.pe(nc.tensor.matmul, pyo[:P, :], pro["ctb"][:, gg, :], s1bf[:, fs], start=False, stop=True,
                                 r=[pro["ctbn"], s1n], w=[pyon])
                            S.pe(nc.tensor.matmul, pds[:, :], btok[64:128, gg, :], xde[64:128, :], start=True, stop=True,
                                 r=[btokn, xden], w=[pdsn])
                            S.pool(nc.gpsimd.tensor_tensor, stmp[:].rearrange("p (h d) -> p h d", h=8),
                                   Sst[:, fs].rearrange("p (h d) -> p h d", h=8),
                                   small[:, 14 + ti, hs, None].broadcast_to([128, 8, 64]), ALU.mult, r=[sn, n(14 + ti), "stmp"], w=["stmp"])
                            S.dve(nc.vector.tensor_tensor, Sst[:, fs], stmp[:], pds[:, :], ALU.add, r=["stmp", pdsn, sn], w=[sn])
                        S.act(nc.scalar.copy, s0bf[:, fs], Sst[:, fs], r=[sn], w=[s0n])
                        it.update(pyo=pyo, pyon=pyon)

                    def st2(it):
                        ti, gg = it["ti"], it["gg"]
                        hs = slice(8 * gg, 8 * gg + 8)
                        fs = slice(512 * gg, 512 * gg + 512)
                        n = lambda k: "sm%d" % k
                        pyo, pyon, xs_, xsn = it["pyo"], it["pyon"], it["xs"], it["xsn"]
                        yb, ybn = YB.next()
                        S.dve(nc.vector.tensor_tensor, yb[:P].rearrange("p (h d) -> p h d", h=8), pyo[:P, :].rearrange("p (h d) -> p h d", h=8),
                              small[:P, 4 + ti, hs, None].broadcast_to([P, 8, 64]), ALU.mult, r=[pyon, n(4 + ti)], w=[ybn])
                        S.dve(nc.vector.tensor_tensor, yb[:P], yb[:P], psyi[:P, :], ALU.add, r=[ybn, "psyi"], w=[ybn])
                        S.pool(nc.gpsimd.tensor_tensor, dsk[:P].rearrange("p (h d) -> p h d", h=8), xs_[:P].rearrange("p (h d) -> p h d", h=8),
                               dsk_bc[:P, hs, None].broadcast_to([P, 8, 64]), ALU.mult, r=[xsn, "dsk_bc"], w=["dsk"])
                        S.dve(nc.vector.tensor_tensor, yb[:P], yb[:P], dsk[:P], ALU.add, r=[ybn, "dsk"], w=[ybn])
                        S.dve(nc.vector.tensor_tensor, yb[:P], yb[:P], sz[:P, ti, fs], ALU.mult, r=[ybn, "sz"], w=[ybn])
                        S.dve(nc.vector.memset, ssq[:P, 1:2], 0.0, w=["ssq"])
                        S.act(nc.scalar.activation, junk[:P, 0:512], yb[:P], AF.Square, accum_out=ssq[:P, 1:2], r=[ybn, "ssq"], w=["junk", "ssq"])
                        rstd_from_ss(ssq[:P, 1:2], 512, ["ssq"])
                        ynb, ynbn = YNB.next()
                        S.dve(nc.vector.scalar_tensor_tensor, ynb[:P], yb[:P], ssq[:P, 1:2], snw_bc[:P, fs], ALU.mult, ALU.mult,
                              r=[ybn, "ssq", "snw_bc"], w=[ynbn])
                        it.update(ynb=ynb, ynbn=ynbn)

                    def st3(it):
                        gg, tc0 = it["gg"], it["pro"]["tc0"]
                        ynb, ynbn = it["ynb"], it["ynbn"]
                        for j in range(4):
                            S.pe(nc.tensor.transpose, pstr[:, j * P:(j + 1) * P], ynb[:P, j * 128:(j + 1) * 128], ident_bf[:P, :P],
                                 r=[ynbn, "ident_bf"], w=["pstr"])
                        S.act(nc.scalar.copy, ynT[:, 4 * gg:4 * gg + 4, tc0:tc0 + P], pstr[:, 0:4 * P].rearrange("p (c t) -> p c t", c=4),
                              r=["pstr"], w=["ynT"])

                    PRO = {}
                    items = [dict(ti=ti, gg=gg) for ti in range(NT) for gg in range(4)]
                    def run(fn, kk):
                        if 0 <= kk < len(items):
                            fn(items[kk])
                    for step in range(len(items) + 3):
                        run(st0, step)
                        run(st3, step - 3)
                        run(st2, step - 2)
                        run(st1, step - 1)
                        run(st0b, step)
                        run(st1b, step - 1)
                        if step < len(BLK_B):
                            proc_block(BLK_B[step], PSB)
                    for bi_ in BLK_B[len(items) + 3:]:
                        proc_block(bi_, PSB)
                    for m in range(8):
                        wb, wbn = wbq.pop(0)
                        ps, pn = GEN.next()
                        for k in range(16):
                            S.pe(nc.tensor.matmul, ps[:, 0:GW], wb[:, k, :], ynT[:, k, 0:GW], start=(k == 0), stop=(k == 15),
                                 r=[wbn, "ynT"], w=[pn])
                        if m + 2 < 8:
                            wb2, wbn2 = WBS.next()
                            S.dma("sp", wb2[:], wbs_b[m + 2], r=["wbs_b%d" % (m + 2)], w=[wbn2])
                            wbq.append((wb2, wbn2))
                        stg, stgn = STG.next()
                        S.dve(nc.vector.tensor_tensor, stg[:, 0:GW], ps[:, 0:GW], sgs[:, m, 0:GW], ALU.mult, r=[pn, "sgs"], w=[stgn])
                        S.dma("sp", sq["MS"][m * 128:(m + 1) * 128, t0:t0 + GW], stg[:, 0:GW], r=[stgn], w=["MSscr%d" % si])
                vco = sq["conv"].rearrange("j (c p) -> p c j", p=128)
                for c in range(24):
                    S.dma("sp", vco[:, c, :], halo[:, c, :], r=["halo%d" % c], allow_slow_non_contiguous=True)
                for c in range(16):
                    ps, pn = GEN.next()
                    S.pe(nc.tensor.transpose, ps[:, 0:128], Sst[:, c * 128:(c + 1) * 128], ident_f[:],
                         r=["Sst%d" % (c // 4), "ident_f"], w=[pn])
                    S.act(nc.scalar.copy, sttmp[:, c, :], ps[:, 0:128], r=[pn], w=["sttmp"])
                S.dma("sp", sq["ssm"].rearrange("(c q) n -> q c n", q=128), sttmp[:], r=["sttmp"])
            S.flush()

        if _stop == 1:
            return nc
        with ExitStack() as st:
            sb = lambda n, sh, dt=F32: st.enter_context(nc.sbuf_tensor(n, sh, dt))
            psb = lambda n, sh, dt=F32: st.enter_context(nc.psum_tensor(n, sh, dt))
            LA = 1
            PSS = Ring([(psb(f"pss{i}", [128, 2, 512]), f"pss{i}") for i in range(2)])
            psl0 = psb("psl0", [128, 512])
            pso = [psb(f"pso{j}", [128, 512]) for j in range(2)]
            TKmax = max(s_["Tk"] for s_ in seqs)
            TQmax = max(s_["T"] for s_ in seqs)
            NKTmax = (TKmax + 127) // 128
            KTB = Ring([(sb(f"ktb{i}", [128, TKmax], BF16), f"ktb{i}") for i in range(2)])
            VB = Ring([(sb(f"vb{i}", [128, NKTmax, 128], BF16), f"vb{i}") for i in range(2)])
            QTB = Ring([(sb(f"qtb{i}", [128, TQmax], BF16), f"qtb{i}") for i in range(2)])
            PT = Ring([(sb(f"pt{i}", [128, 2, 512], BF16), f"pt{i}") for i in range(LA + 3)])
            ACC = Ring([(sb(f"acc{i}", [128, 2, 512]), f"acc{i}") for i in range(2)])
            rl = sb("rl", [128, 2, 512])
            ob = sb("ob", [128, 512]); ob2 = sb("ob2", [128, 512]); osq = sb("osq", [128, 512]); rs2 = sb("rs2", [128, 512])
            OST = Ring([(sb(f"ost{i}", [128, 512], BF16), f"ost{i}") for i in range(2)])
            iters = []
            for si, sq in enumerate(seqs):
                T, Tk, pastn = sq["T"], sq["Tk"], sq["past"]
                causal = (pastn == 0)
                nkt = (Tk + 127) // 128
                QG = min(512, T)
                for h in range(8):
                    for qg in range(T // QG):
                        q0 = qg * QG
                        kts = list(range(0, (q0 + QG) // 128)) if causal else list(range(nkt))
                        grp = dict(si=si, sq=sq, h=h, q0=q0, QG=QG, first=(qg == 0), T=T, Tk=Tk, pastn=pastn)
                        for kt in kts:
                            nk = min(128, Tk - kt * 128)
                            qs = max(q0, kt * 128) if causal else q0
                            iters.append(dict(grp=grp, kt=kt, nk=nk, qs=qs, nq=q0 + QG - qs, lo=qs - q0,
                                              diag=causal and (kt * 128 >= q0), first=(kt == 0), last=(kt == kts[-1])))
            cur = {}

            def load_head(grp):
                sq, si, h, T, Tk, pastn = grp["sq"], grp["si"], grp["h"], grp["T"], grp["Tk"], grp["pastn"]
                ktb, ktn = KTB.next(); vb, vbn = VB.next(); qtb, qtn = QTB.next()
                S.dma("sp", ktb[:, 0:Tk], sq["KT"][h * 128:(h + 1) * 128, :], w=[ktn])
                S.dma("sp", qtb[:, 0:T], sq["QT"][h * 128:(h + 1) * 128, :], w=[qtn])
                nfull = Tk // 128
                S.dma("sp", vb[:, 0:nfull, :], sq["V"][0:nfull * 128, h * 128:(h + 1) * 128].rearrange("(j p) e -> p j e", p=128), w=[vbn])
                if Tk % 128:
                    rem = Tk % 128
                    S.dma("sp", vb[0:rem, nfull, :], sq["V"][nfull * 128:Tk, h * 128:(h + 1) * 128], w=[vbn])
                grp["bufs"] = (ktb, ktn, vb, vbn, qtb, qtn)

            def front(it):
                grp = it["grp"]
                key = (grp["si"], grp["h"])
                if key not in cur:
                    load_head(grp)
                    cur[key] = grp["bufs"]
                grp["bufs"] = cur[key]
                ktb, ktn, vb, vbn, qtb, qtn = grp["bufs"]
                if "acc" not in grp:
                    grp["acc"] = ACC.next()
                acc, accn = grp["acc"]
                kt, nk, qs, nq, lo = it["kt"], it["nk"], it["qs"], it["nq"], it["lo"]
                pss, pssn = PSS.next()
                for j in range(2):
                    S.pe(nc.tensor.matmul, pss[:nk, j, 0:nq], ktb[64 * j:64 * j + 64, kt * 128:kt * 128 + nk],
                         qtb[64 * j:64 * j + 64, qs:qs + nq], start=True, stop=True, r=[ktn, qtn], w=[pssn])
                pt, ptn = PT.next()
                it["pt"] = (pt, ptn)
                S.act(nc.scalar.activation, pt[:nk, :, 0:nq], pss[:nk, :, 0:nq], AF.Exp, r=[pssn], w=[ptn])
                if it["diag"]:
                    S.pool(nc.gpsimd.memset, pt[64:128, :, 0:64], 0.0, r=[ptn], w=[ptn])
                if it["first"]:
                    S.dve(nc.vector.tensor_copy, acc[:nk, 1, lo:lo + nq], pt[:nk, 1, 0:nq], r=[ptn], w=[accn + "b"])
                else:
                    S.dve(nc.vector.tensor_tensor, acc[:nk, 1, lo:lo + nq], acc[:nk, 1, lo:lo + nq], pt[:nk, 1, 0:nq], ALU.add,
                          r=[ptn, accn + "b"], w=[accn + "b"])

            def back(it):
                grp = it["grp"]
                ktb, ktn, vb, vbn, qtb, qtn = grp["bufs"]
                kt, nk, nq, lo = it["kt"], it["nk"], it["nq"], it["lo"]
                pt, ptn = it["pt"]
                for j in range(2):
                    S.pe(nc.tensor.matmul, pso[j][:, lo:lo + nq], vb[:nk, kt, :], pt[:nk, j, 0:nq], start=it["first"], stop=it["last"],
                         r=[vbn, ptn], w=["pso%d" % j])
                S.pe(nc.tensor.matmul, psl0[:, lo:lo + nq], ones_bf[:nk, :], pt[:nk, 0, 0:nq], start=it["first"], stop=it["last"],
                     r=["ones_bf", ptn], w=["psl0"])
                if it["last"]:
                    finalize(grp)

            def finalize(grp):
                sq, si, h, q0, QG = grp["sq"], grp["si"], grp["h"], grp["q0"], grp["QG"]
                acc, accn = grp["acc"]
                psl, psln = PSS.next()
                S.pe(nc.tensor.matmul, psl[:, 1, 0:QG], ones_f[:], acc[:, 1, 0:QG], start=True, stop=True, r=["ones_f", accn + "b"], w=[psln])
                S.dve(nc.vector.reciprocal, rl[:, 0, 0:QG], psl0[:, 0:QG], r=["psl0"], w=["rl"])
                S.dve(nc.vector.reciprocal, rl[:, 1, 0:QG], psl[:, 1, 0:QG], r=[psln, "rl"], w=["rl"])
                S.dve(nc.vector.tensor_tensor, ob[:, 0:QG], pso[0][:, 0:QG], rl[:, 0, 0:QG], ALU.mult, r=["pso0", "rl"], w=["ob"])
                S.dve(nc.vector.tensor_tensor, ob2[:, 0:QG], pso[1][:, 0:QG], rl[:, 1, 0:QG], ALU.mult, r=["pso1", "rl"], w=["ob2"])
                S.dve(nc.vector.scalar_tensor_tensor, osq[:, 0:QG], ob2[:, 0:QG], neglam, ob[:, 0:QG], ALU.mult, ALU.add,
                      r=["ob", "ob2", "lams"], w=["osq"])
                S.pool(nc.gpsimd.tensor_tensor, ob2[:, 0:QG], osq[:, 0:QG], osq[:, 0:QG], ALU.mult, r=["osq"], w=["ob2"])
                psq, psqn = PSS.next()
                S.pe(nc.tensor.matmul, psq[:, 0, 0:QG], ones_f[:], ob2[:, 0:QG], start=True, stop=True, r=["ones_f", "ob2"], w=[psqn])
                S.act(nc.scalar.activation, rs2[:, 0:QG], psq[:, 0, 0:QG], AF.Ln, scale=1.0 / 128, bias=EPS, r=[psqn], w=["rs2"])
                S.act(nc.scalar.activation, rs2[:, 0:QG], rs2[:, 0:QG], AF.Exp, scale=-0.5, r=["rs2"], w=["rs2"])
                ost, ostn = OST.next()
                S.dve(nc.vector.scalar_tensor_tensor, ost[:, 0:QG], osq[:, 0:QG], sublw[:, 0:1], rs2[:, 0:QG], ALU.mult, ALU.mult,
                      r=["osq", "sublw", "rs2"], w=[ostn])
                S.dma("sp", sq["OT"][h * 128:(h + 1) * 128, q0:q0 + QG], ost[:, 0:QG], r=[ostn], w=["OTscr%d" % si])

            pend = []
            for it in iters:
                front(it)
                pend.append(it)
                if len(pend) > LA:
                    back(pend.pop(0))
            while pend:
                back(pend.pop(0))
            S.flush()

        if _stop == 2:
            return nc
        with ExitStack() as st:
            sb = lambda n, sh, dt=F32: st.enter_context(nc.sbuf_tensor(n, sh, dt))
            psb = lambda n, sh, dt=F32: st.enter_context(nc.psum_tensor(n, sh, dt))
            nfw_bc = bc_load("nfw_bc", norm_ffn_w[0], D, st)
            GEN = Ring([(psb(f"p3g{i}", [128, 512]), f"p3g{i}") for i in range(4)])
            pstr = psb("p3tr", [128, 1024], BF16)
            PSHB = Ring([(psb(f"p3hb{i}", [128, 512]), f"p3hb{i}") for i in range(2)])
            wba = sb("wba", [128, 8, D], BF16); wout = sb("wout", [128, 8, D], BF16); wdn = sb("wdn", [128, 22, D], BF16)
            S.dma("sp", wba[:], wba_b, w=["wba"])
            S.dma("sp", wout[:], wout_b, w=["wout"])
            S.dma("sp", wdn[:, 0:11], wdn_b[:, 0:11], w=["wdn"])
            S.dma("sp", wdn[:, 11:22], wdn_b[:, 11:22], w=["wdn"])
            WR = Ring([(sb(f"w3r{i}", [128, 8, 512], BF16), f"w3r{i}") for i in range(2)])
            oT = sb("oT", [128, 8, 256], BF16)
            gat = sb("gat", [128, 8, 256]); mss = sb("mss", [128, 8, 256])
            mixT = sb("mixT", [128, 8, 256], BF16)
            mtmp = sb("mtmp", [128, 256])
            XT = Ring([(sb(f"x3t{i}", [128, D]), f"x3t{i}") for i in range(2)])
            x1 = sb("x1", [128, 2, D])
            xnb = sb("x3nb", [128, D], BF16)
            xn2T = sb("xn2T", [128, 8, 256], BF16)
            ssq = sb("ssq3", [128, 4]); junk = sb("junk3", [128, D], BF16)
            PRE = Ring([(sb(f"pre3{i}", [128, 2 + 256]), f"pre3{i}") for i in range(2)])
            halo2 = sb("halo2", [128, 22, 2])
            ctmp = sb("ctmp3", [128, 256]); sha = sb("sha", [128, 256])
            gT = sb("gT", [128, 22, 256], BF16)
            YST = Ring([(sb(f"yst{i}", [128, D]), f"yst{i}") for i in range(1)])
            for si, sq in enumerate(seqs):
                T, P = sq["T"], sq["P"]
                NT = 2 if P == 128 else 1
                GW = P * NT
                NG = T // GW
                if sq["past"] == 0:
                    S.pool(nc.gpsimd.memset, halo2[:], 0.0, w=["halo2"])
                else:
                    vsf = sffn[sq["idx"]].rearrange("j (c p) -> p c j", p=128)
                    for c in range(22):
                        S.dma("sp", halo2[:, c, :], vsf[:, c, :], w=["halo2"], allow_slow_non_contiguous=True)
                for g in range(NG):
                    t0 = g * GW
                    S.dma("sp", oT[:, :, 0:GW], sq["OT"].rearrange("(c p) t -> p c t", p=128)[:, :, t0:t0 + GW], r=["OTscr%d" % si], w=["oT"])
                    S.dma("sp", gat[:, :, 0:GW], sq["GA"].rearrange("(c p) t -> p c t", p=128)[:, :, t0:t0 + GW], r=["GAscr%d" % si], w=["gat"])
                    S.dma("sp", mss[:, :, 0:GW], sq["MS"].rearrange("(c p) t -> p c t", p=128)[:, :, t0:t0 + GW], r=["MSscr%d" % si], w=["mss"])
                    for m in range(8):
                        ps, pn = GEN.next()
                        for k in range(8):
                            S.pe(nc.tensor.matmul, ps[:, 0:GW], wba[:, k, m * 128:(m + 1) * 128], oT[:, k, 0:GW], start=(k == 0), stop=(k == 7),
                                 r=["wba", "oT"], w=[pn])
                        S.dve(nc.vector.tensor_tensor, mtmp[:, 0:GW], ps[:, 0:GW], gat[:, m, 0:GW], ALU.mult, r=[pn, "gat"], w=["mtmp"])
                        S.dve(nc.vector.tensor_tensor, mixT[:, m, 0:GW], mtmp[:, 0:GW], mss[:, m, 0:GW], ALU.add, r=["mtmp", "mss"], w=["mixT"])
                    for ti in range(NT):
                        r0 = t0 + ti * P
                        xt, xn_ = XT.next()
                        S.dma("sp", xt[:P], sq["x"][r0:r0 + P, :], w=[xn_])
                        for nb in range(2):
                            ps, pn = GEN.next()
                            for k in range(8):
                                S.pe(nc.tensor.matmul, ps[:P, :], mixT[:, k, ti * P:(ti + 1) * P], wout[:, k, nb * 512:(nb + 1) * 512],
                                     start=(k == 0), stop=(k == 7), r=["mixT", "wout"], w=[pn])
                            S.dve(nc.vector.tensor_tensor, x1[:P, ti, nb * 512:(nb + 1) * 512], ps[:P, :], xt[:P, nb * 512:(nb + 1) * 512], ALU.add,
                                  r=[pn, xn_], w=["x1_%d" % ti])
                        S.dve(nc.vector.memset, ssq[:P, 0:1], 0.0, w=["ssq3"])
                        S.act(nc.scalar.activation, junk[:P], x1[:P, ti, :], AF.Square, accum_out=ssq[:P, 0:1], r=["x1_%d" % ti, "ssq3"], w=["junk3", "ssq3"])
                        S.act(nc.scalar.activation, ssq[:P, 0:1], ssq[:P, 0:1], AF.Ln, scale=1.0 / D, bias=EPS, r=["ssq3"], w=["ssq3"])
                        S.act(nc.scalar.activation, ssq[:P, 0:1], ssq[:P, 0:1], AF.Exp, scale=-0.5, r=["ssq3"], w=["ssq3"])
                        S.dve(nc.vector.scalar_tensor_tensor, xnb[:P], x1[:P, ti, :], ssq[:P, 0:1], nfw_bc[:P], ALU.mult, ALU.mult,
                              r=["x1_%d" % ti, "ssq3", "nfw_bc"], w=["x3nb"])
                        for k in range(8):
                            S.pe(nc.tensor.transpose, pstr[:, k * P:(k + 1) * P], xnb[:P, k * 128:(k + 1) * 128], ident_bf[:P, :P],
                                 r=["x3nb", "ident_bf"], w=["p3tr"])
                        S.act(nc.scalar.copy, xn2T[:, :, ti * P:(ti + 1) * P], pstr[:, 0:8 * P].rearrange("p (k t) -> p k t", k=8),
                              r=["p3tr"], w=["xn2T"])
                    for bj in range(11):
                        wt, wn = WR.next()
                        S.dma("sp", wt[:], wup_b[bj], r=["wup_b"], w=[wn])
                        for cc in range(2):
                            c = bj * 2 + cc
                            ps, pn = GEN.next()
                            for k in range(8):
                                S.pe(nc.tensor.matmul, ps[:, 0:GW], wt[:, k, cc * 128:(cc + 1) * 128], xn2T[:, k, 0:GW],
                                     start=(k == 0), stop=(k == 7), r=[wn, "xn2T"], w=[pn])
                            phb, phbn = PSHB.next()
                            for k in range(8):
                                S.pe(nc.tensor.matmul, phb[:, 0:GW], wt[:, k, 256 + cc * 128:256 + (cc + 1) * 128], xn2T[:, k, 0:GW],
                                     start=(k == 0), stop=(k == 7), r=[wn, "xn2T"], w=[phbn])
                            pre, pren = PRE.next()
                            S.pool(nc.gpsimd.tensor_copy, pre[:, 0:2], halo2[:, c, :], r=["halo2"], w=[pren])
                            S.act(nc.scalar.copy, pre[:, 2:2 + GW], ps[:, 0:GW], r=[pn], w=[pren])
                            S.pool(nc.gpsimd.tensor_copy, halo2[:, c, :], pre[:, GW:GW + 2], r=[pren], w=["halo2"])
                            S.dve(nc.vector.tensor_scalar, ctmp[:, 0:GW], pre[:, 0:GW], fcw[:, c, 0:1], fcb[:, c:c + 1], ALU.mult, ALU.add,
                                  r=[pren, "fcw", "fcb"], w=["ctmp3"])
                            for j in range(1, 3):
                                S.dve(nc.vector.scalar_tensor_tensor, ctmp[:, 0:GW], pre[:, j:j + GW], fcw[:, c, j:j + 1], ctmp[:, 0:GW],
                                      ALU.mult, ALU.add, r=[pren, "fcw", "ctmp3"], w=["ctmp3"])
                            S.act(nc.scalar.activation, sha[:, 0:GW], ctmp[:, 0:GW], AF.Silu, r=["ctmp3"], w=["sha"])
                            S.dve(nc.vector.tensor_tensor, gT[:, c, 0:GW], sha[:, 0:GW], phb[:, 0:GW], ALU.mult, r=["sha", phbn], w=["gT"])
                    for ti in range(NT):
                        r0 = t0 + ti * P
                        yst, ystn = YST.next()
                        for nb in range(2):
                            ps, pn = GEN.next()
                            for k in range(22):
                                S.pe(nc.tensor.matmul, ps[:P, :], gT[:, k, ti * P:(ti + 1) * P], wdn[:, k, nb * 512:(nb + 1) * 512],
                                     start=(k == 0), stop=(k == 21), r=["gT", "wdn"], w=[pn])
                            S.dve(nc.vector.tensor_tensor, yst[:P, nb * 512:(nb + 1) * 512], ps[:P, :], x1[:P, ti, nb * 512:(nb + 1) * 512], ALU.add,
                                  r=[pn, "x1_%d" % ti], w=[ystn])
                        S.dma("sp", sq["y"][r0:r0 + P, :], yst[:P], r=[ystn])
                vfo = sq["ffn"].rearrange("j (c p) -> p c j", p=128)
                for c in range(22):
                    S.dma("sp", vfo[:, c, :], halo2[:, c, :], r=["halo2"], allow_slow_non_contiguous=True)
            S.flush()
    return nc


_PROG_CACHE = {}


def kernel(x_prompt, x_sample, cache_k, cache_v, state_ssm, state_ssm_conv, state_ffn_conv,
           norm_mix_w, w_in, ssm_conv_w, ssm_conv_b, ssm_dt_bias, ssm_a_log, ssm_d, ssm_norm_w,
           q_norm_w, k_norm_w, lambda_q1, lambda_k1, lambda_q2, lambda_k2, subln_w,
           w_branch_ssm, w_branch_attn, w_out, norm_ffn_w, w_up, ffn_conv_w, ffn_conv_b, w_down):
    f = lambda a: np.ascontiguousarray(np.asarray(a, dtype=np.float32))
    x_prompt = f(x_prompt); x_sample = f(x_sample)
    B, Tp, _ = x_prompt.shape
    BS, Ts, _ = x_sample.shape
    past = cache_k.shape[2]
    import os as _os
    NC = int(_os.environ.get('KCORES', '8'))
    NS = BS // 8
    key = (Tp, NS, Ts, past)
    nc = build_program(Tp, NS, Ts, past)
    ck = f(cache_k)[0].reshape(BS, past, D)
    cv = f(cache_v)[0].reshape(BS, past, D)
    sssm = f(state_ssm)[0]
    sconv = f(state_ssm_conv)[0]
    sffn = f(state_ffn_conv)[0]
    lam_in = np.stack([f(lambda_q1)[0], f(lambda_k1)[0], f(lambda_q2)[0], f(lambda_k2)[0]], axis=0)
    shared = {
        "norm_mix_w": f(norm_mix_w), "w_in": f(w_in)[0], "ssm_conv_w": f(ssm_conv_w)[0], "ssm_conv_b": f(ssm_conv_b),
        "ssm_dt_bias": f(ssm_dt_bias), "ssm_a_log": f(ssm_a_log), "ssm_d": f(ssm_d), "ssm_norm_w": f(ssm_norm_w),
        "q_norm_w": f(q_norm_w), "k_norm_w": f(k_norm_w), "lam_in": f(lam_in), "subln_w": f(subln_w)[0].reshape(128, 1),
        "w_bs": f(w_branch_ssm)[0], "w_ba": f(w_branch_attn)[0], "w_out": f(w_out)[0], "norm_ffn_w": f(norm_ffn_w),
        "w_up": f(w_up)[0], "ffn_conv_w": f(ffn_conv_w)[0], "ffn_conv_b": f(ffn_conv_b), "w_down": f(w_down)[0],
    }
    in_maps = []
    for c in range(NC):
        m = dict(shared)
        m["xp"] = x_prompt[c]
        sl = slice(c * NS, (c + 1) * NS)
        m["xs"] = x_sample[sl]; m["ck"] = ck[sl]; m["cv"] = cv[sl]
        m["sssm"] = sssm[sl]; m["sconv"] = sconv[sl]; m["sffn"] = sffn[sl]
        in_maps.append(m)
    res = run_bass_kernel_spmd(nc, in_maps, core_ids=list(range(NC)))
    R = res.results
    global _LAST
    _LAST = R
    if NC < 8:
        R = list(R) + [R[0]] * (8 - NC)
        NC = 8
    cat = lambda k: np.stack([R[c][k] for c in range(NC)], axis=0)
    cats = lambda k: np.concatenate([R[c][k] for c in range(NC)], axis=0)
    y_p = cat("y_p")
    y_s = cats("y_s")
    k_p = cat("k_p").reshape(1, B, Tp, 16, 64)
    v_p = cat("v_p").reshape(1, B, Tp, 8, 128)
    ssm_p = cat("ssm_p").reshape(1, B, 32, 64, 128)
    conv_p = cat("conv_p").reshape(1, B, 3, CD)
    ffn_p = cat("ffn_p").reshape(1, B, 2, DFF)
    k_s = cats("k_s").reshape(1, BS, Ts, 16, 64)
    v_s = cats("v_s").reshape(1, BS, Ts, 8, 128)
    ssm_s = cats("ssm_s").reshape(1, BS, 32, 64, 128)
    conv_s = cats("conv_s").reshape(1, BS, 3, CD)
    ffn_s = cats("ffn_s").reshape(1, BS, 2, DFF)
    return (y_p, y_s, k_p, v_p, ssm_p, conv_p, ffn_p, k_s, v_s, ssm_s, conv_s, ffn_s)
```
